# Optimizing a Trainium2 kernel written in Bass

```python
import math
import jax, jax.numpy as jnp
from jax import lax
import numpy as np

D_MODEL = 1024
BATCH = 8
SEQ = 4096
DEPTH = 2

CTX_LEN = 256
GRID_W = 64

A_HEADS = 4
A_DK = 128
A_DV = A_DK
CONV_K = 5
A_CHUNK = 64
B_HEADS = 4
B_DH = 64
WIN_R = 8
WIN_C = 16
C_HEADS = 4
C_DK = 32
C_DV = 64
C_RANK = 16
C_GATE_NORM = 16.0
C_CHUNK = 16
ROPE_THETA = 10000.0

MIX_WIDTH = A_HEADS * A_DV + B_HEADS * B_DH + C_HEADS * C_DV
DN_ALPHA = (2 * DEPTH) ** 0.25
DN_BETA = (8 * DEPTH) ** -0.25
LN_EPS = 1e-6
NEG_INF = -1e30

PROJ_SIZES = (
    3 * A_HEADS * A_DK,
    4 * A_HEADS,
    A_HEADS * A_DV,
    B_HEADS * B_DH,
    B_HEADS * B_DH,
    B_HEADS * B_DH,
    B_HEADS * B_DH,
    C_HEADS * C_DK,
    C_HEADS * C_DK,
    C_HEADS * C_DV,
    2 * C_RANK,
    C_HEADS * C_DV,
)
PROJ_SPLITS = tuple(int(s) for s in np.cumsum(PROJ_SIZES)[:-1])
PROJ_TOTAL = int(sum(PROJ_SIZES))

kernel_name = 'hybrid_gdn_natten_gla_diffusion_block'

F32 = jnp.float32


def layer_norm(x):
    xf = x.astype(F32)
    mu = jnp.mean(xf, -1, keepdims=True)
    var = jnp.mean(jnp.square(xf - mu), -1, keepdims=True)
    return (xf - mu) * lax.rsqrt(var + LN_EPS)


def post_norm(h, g, b, dtype):
    return (layer_norm(h) * g.astype(F32) + b.astype(F32)).astype(dtype)


def ada_modulation(cond, w_mod, b_mod):
    h = (jax.nn.silu(cond) @ w_mod + b_mod)[..., None, :]
    return jnp.split(h, 3, axis=-1)


def l2norm(x):
    return x * lax.rsqrt(jnp.sum(x * x, -1, keepdims=True) + 1e-6)


def head_rmsnorm_gate(o, w, z):
    b_, h, l, dv = o.shape
    o = jnp.transpose(o, (0, 2, 1, 3))
    o = o * lax.rsqrt(jnp.mean(o * o, -1, keepdims=True) + 1e-6) * w.astype(F32)
    return (o.reshape(b_, l, h * dv) * jax.nn.silu(z.astype(F32))).astype(z.dtype)


def short_conv(x, w):
    ch = x.shape[-1]
    return lax.conv_general_dilated(
        x, w[:, None, :].astype(x.dtype), window_strides=(1,),
        padding=[(CONV_K // 2, CONV_K // 2)],
        dimension_numbers=('NWC', 'WIO', 'NWC'), feature_group_count=ch)


def axial_rope(x):
    l = x.shape[1]
    t = jnp.arange(l)
    half = x.shape[-1] // 2
    nf = half // 2
    inv = ROPE_THETA ** (-jnp.arange(nf, dtype=F32) / nf)

    def rotate(xp, pos):
        ang = pos.astype(F32)[:, None] * inv
        cos = jnp.cos(ang)[None, :, None, :]
        sin = jnp.sin(ang)[None, :, None, :]
        x1, x2 = xp[..., :nf], xp[..., nf:]
        return jnp.concatenate([x1 * cos - x2 * sin, x1 * sin + x2 * cos], -1)

    xf = x.astype(F32)
    return jnp.concatenate([rotate(xf[..., :half], t // GRID_W),
                            rotate(xf[..., half:], t % GRID_W)], -1)


def _orient(t, reverse):
    return jnp.flip(t, axis=2) if reverse else t


def bidirectional(core, seq_c, gates_c, seq_l, gates_l, s0):
    out_c, out_l = 0.0, 0.0
    for d, rev in enumerate((False, True)):
        o_c, s_c = core(*[_orient(t, rev) for t in seq_c], *[_orient(g[d], rev) for g in gates_c], s0)
        o_l, _ = core(*[_orient(t, rev) for t in seq_l], *[_orient(g[d], rev) for g in gates_l], s_c)
        out_c = out_c + _orient(o_c, rev)
        out_l = out_l + _orient(o_l, rev)
    return out_c, out_l


def gated_delta_chunked(q, k, v, g, beta, s0):
    b_, h, l, dk = q.shape
    dv = v.shape[-1]
    c = A_CHUNK
    n = l // c
    q, k, v = (t.reshape(b_, h, n, c, -1) for t in (q, k, v))
    g = jnp.cumsum(g.reshape(b_, h, n, c), axis=-1)
    beta = beta.reshape(b_, h, n, c)
    tril = jnp.tril(jnp.ones((c, c), bool))
    strict = jnp.tril(jnp.ones((c, c), bool), -1)
    diff = g[..., :, None] - g[..., None, :]
    decay = jnp.where(tril, jnp.exp(jnp.where(tril, diff, 0.0)), 0.0)
    kb = k * beta[..., None]
    lmat = jnp.where(strict, jnp.einsum('bhnid,bhnjd->bhnij', kb, k) * decay, 0.0)
    amat = jnp.eye(c, dtype=F32) + lmat
    rhs = jnp.concatenate([v * beta[..., None], kb * jnp.exp(g)[..., None]], -1)
    sol = lax.linalg.triangular_solve(amat, rhs, left_side=True, lower=True, unit_diagonal=True)
    u, w = sol[..., :dv], sol[..., dv:]
    qk = jnp.einsum('bhnid,bhnjd->bhnij', q, k) * decay
    q_dec = q * jnp.exp(g)[..., None]
    k_dec = k * jnp.exp(g[..., -1:] - g)[..., None]
    g_last = jnp.exp(g[..., -1])

    def step(s, inp):
        qk_i, qd_i, kd_i, u_i, w_i, gl_i = inp
        v_new = u_i - jnp.einsum('bhck,bhkv->bhcv', w_i, s)
        o = jnp.einsum('bhck,bhkv->bhcv', qd_i, s) + jnp.einsum('bhij,bhjv->bhiv', qk_i, v_new)
        s = s * gl_i[..., None, None] + jnp.einsum('bhck,bhcv->bhkv', kd_i, v_new)
        return s, o

    xs = tuple(jnp.moveaxis(t, 2, 0) for t in (qk, q_dec, k_dec, u, w, g_last))
    s_fin, o = lax.scan(step, s0, xs)
    return jnp.moveaxis(o, 0, 2).reshape(b_, h, l, dv), s_fin


def gdn_heads(qkv, conv_w):
    b_, l, _ = qkv.shape
    qkv = jax.nn.silu(short_conv(qkv, conv_w)).astype(F32)
    qkv = jnp.transpose(qkv.reshape(b_, l, 3, A_HEADS, A_DK), (2, 0, 3, 1, 4))
    q = l2norm(qkv[0]) * (A_DK ** -0.5)
    k = l2norm(qkv[1])
    return q, k, qkv[2]


def gdn_gates(ab, a_log, dt_bias):
    b_, l, _ = ab.shape
    ab = ab.astype(F32).reshape(b_, l, 4, A_HEADS)
    g = -jnp.exp(a_log.astype(F32)) * jax.nn.softplus(ab[:, :, :2] + dt_bias.astype(F32))
    beta = jax.nn.sigmoid(ab[:, :, 2:])
    return jnp.transpose(g, (2, 0, 3, 1)), jnp.transpose(beta, (2, 0, 3, 1))


def gdn_branch(qkv_c, ab_c, qkv_l, ab_l, conv_w, a_log, dt_bias):
    qc, kc, vc = gdn_heads(qkv_c, conv_w)
    ql, kl, vl = gdn_heads(qkv_l, conv_w)
    gc, bc = gdn_gates(ab_c, a_log, dt_bias)
    gl, bl = gdn_gates(ab_l, a_log, dt_bias)
    s0 = jnp.zeros(qc.shape[:2] + (A_DK, A_DV), F32)
    return bidirectional(gated_delta_chunked, (qc, kc, vc), (gc, bc), (ql, kl, vl), (gl, bl), s0)


def na_branch(q_c, k_c, v_c, q_l, k_l, v_l, rpb, ctx_out):
    b_, s, _ = q_l.shape
    lc = q_c.shape[1]
    rows = s // GRID_W
    wr = min(WIN_R, rows)
    scale = B_DH ** -0.5
    qc = q_c.reshape(b_, lc, B_HEADS, B_DH) * scale
    kc = k_c.reshape(b_, lc, B_HEADS, B_DH)
    vc = v_c.reshape(b_, lc, B_HEADS, B_DH)
    o_c = None
    if ctx_out:
        p_cc = jax.nn.softmax(jnp.einsum('bqhd,bkhd->bhqk', qc, kc).astype(F32), axis=-1)
        o_c = jnp.einsum('bhqk,bkhd->bqhd', p_cc.astype(vc.dtype), vc).reshape(b_, lc, B_HEADS * B_DH)
    qg = q_l.reshape(b_, rows, GRID_W, B_HEADS, B_DH) * scale
    kg = k_l.reshape(b_, rows, GRID_W, B_HEADS, B_DH)
    vg = v_l.reshape(b_, rows, GRID_W, B_HEADS, B_DH)
    r = jnp.arange(rows)
    r0 = jnp.clip(r - wr // 2, 0, rows - wr)
    row_idx = r0[:, None] + jnp.arange(wr)
    k_rows = kg[:, row_idx]
    v_rows = vg[:, row_idx]
    s_win = jnp.einsum('brqhd,brikhd->bhrqik', qg, k_rows).astype(F32)
    cq = jnp.arange(GRID_W)
    c0 = jnp.clip(cq - WIN_C // 2, 0, GRID_W - WIN_C)
    col_ok = (cq[None, :] >= c0[:, None]) & (cq[None, :] < c0[:, None] + WIN_C)
    di = row_idx - r[:, None] + (WIN_R - 1)
    dj = jnp.clip(cq[None, :] - cq[:, None] + (WIN_C - 1), 0, 2 * WIN_C - 2)
    bias = rpb.astype(F32)[:, di[:, None, :, None], dj[None, :, None, :]]
    s_win = jnp.where(col_ok[:, None, :], s_win + bias, NEG_INF)
    s_lc = jnp.einsum('brqhd,bkhd->bhrqk', qg, kc).astype(F32)
    n_win = wr * GRID_W
    p = jax.nn.softmax(jnp.concatenate([s_win.reshape(b_, B_HEADS, rows, GRID_W, n_win), s_lc], -1), axis=-1)
    p = p.astype(v_l.dtype)
    p_win = p[..., :n_win].reshape(b_, B_HEADS, rows, GRID_W, wr, GRID_W)
    o_l = (jnp.einsum('bhrqik,brikhd->brqhd', p_win, v_rows)
           + jnp.einsum('bhrqk,bkhd->brqhd', p[..., n_win:], vc))
    return o_c, o_l.reshape(b_, s, B_HEADS * B_DH)


def gla_chunked(q, k, v, gk, s0):
    b_, h, l, dk = q.shape
    dv = v.shape[-1]
    c = C_CHUNK
    n = l // c
    q, k, v, gk = (t.reshape(b_, h, n, c, -1) for t in (q, k, v, gk))
    bcum = jnp.cumsum(gk, axis=3)
    tril = jnp.tril(jnp.ones((c, c), bool))[:, :, None]
    diff = bcum[..., :, None, :] - bcum[..., None, :, :]
    decay = jnp.where(tril, jnp.exp(jnp.where(tril, diff, 0.0)), 0.0)
    amat = jnp.einsum('bhnid,bhnjd,bhnijd->bhnij', q, k, decay)
    o_intra = jnp.einsum('bhnij,bhnjv->bhniv', amat, v)
    q_dec = q * jnp.exp(bcum)
    k_dec = k * jnp.exp(bcum[..., -1:, :] - bcum)
    g_last = jnp.exp(bcum[..., -1, :])

    def step(s, inp):
        qd_i, kd_i, v_i, gl_i = inp
        o = jnp.einsum('bhck,bhkv->bhcv', qd_i, s)
        s = s * gl_i[..., None] + jnp.einsum('bhck,bhcv->bhkv', kd_i, v_i)
        return s, o

    xs = tuple(jnp.moveaxis(t, 2, 0) for t in (q_dec, k_dec, v, g_last))
    s_fin, o_inter = lax.scan(step, s0, xs)
    o = o_intra + jnp.moveaxis(o_inter, 0, 2)
    return o.reshape(b_, h, l, dv), s_fin


def gla_gates(r, w2, b2):
    b_, l, _ = r.shape
    r = r.astype(F32).reshape(b_, l, 2, C_RANK)
    gk = jax.nn.log_sigmoid(jnp.einsum('bldr,drk->dblk', r, w2.astype(F32))
                            + b2.astype(F32)[:, None, None, :]) / C_GATE_NORM
    return jnp.transpose(gk.reshape(2, b_, l, C_HEADS, C_DK), (0, 1, 3, 2, 4))


def gla_heads(t, d, rope):
    t = t.reshape(t.shape[0], t.shape[1], C_HEADS, d)
    t = axial_rope(t) if rope else t.astype(F32)
    return jnp.transpose(t, (0, 2, 1, 3))


def gla_branch(q_c, k_c, v_c, r_c, q_l, k_l, v_l, r_l, w2, b2):
    scale = C_DK ** -0.5
    qc = gla_heads(q_c, C_DK, False) * scale
    kc = gla_heads(k_c, C_DK, False)
    vc = gla_heads(v_c, C_DV, False)
    ql = gla_heads(q_l, C_DK, True) * scale
    kl = gla_heads(k_l, C_DK, True)
    vl = gla_heads(v_l, C_DV, False)
    s0 = jnp.zeros(qc.shape[:2] + (C_DK, C_DV), F32)
    return bidirectional(gla_chunked, (qc, kc, vc), (gla_gates(r_c, w2, b2),),
                         (ql, kl, vl), (gla_gates(r_l, w2, b2),), s0)


def setup_inputs(seed: int = 0) -> dict:
    key = jax.random.key(seed)
    ks = jax.random.split(key, 18)

    def nrm(k, shape, s):
        return jax.random.normal(k, shape, F32) * s

    x = nrm(ks[0], (BATCH, SEQ, D_MODEL), 1.0)
    c = nrm(ks[1], (BATCH, D_MODEL), 1.0)
    ctx = nrm(ks[2], (BATCH, CTX_LEN, D_MODEL), 1.0)
    c_ctx = nrm(ks[3], (D_MODEL,), 1.0)
    w_mod = nrm(ks[4], (DEPTH, D_MODEL, 3 * D_MODEL), 0.25 * D_MODEL ** -0.5)
    b_mod = nrm(ks[5], (DEPTH, 3 * D_MODEL), 0.01)
    w_in = nrm(ks[6], (DEPTH, D_MODEL, PROJ_TOTAL), D_MODEL ** -0.5)
    conv_w = nrm(ks[7], (DEPTH, CONV_K, 3 * A_HEADS * A_DK), CONV_K ** -0.5)
    a_log = jnp.log(jax.random.uniform(ks[8], (DEPTH, 2, A_HEADS), F32, 1.0, 16.0))
    dt = jnp.exp(jax.random.uniform(ks[9], (DEPTH, 2, A_HEADS), F32, math.log(1e-3), math.log(1e-1)))
    dt_bias = dt + jnp.log(-jnp.expm1(-dt))
    gdn_norm = 1.0 + nrm(ks[10], (DEPTH, A_DV), 0.02)
    rpb = nrm(ks[11], (DEPTH, B_HEADS, 2 * WIN_R - 1, 2 * WIN_C - 1), 0.1)
    gla_w2 = nrm(ks[12], (DEPTH, 2, C_RANK, C_HEADS * C_DK), C_RANK ** -0.5)
    gla_b2 = nrm(ks[13], (DEPTH, 2, C_HEADS * C_DK), 0.1)
    gla_norm = 1.0 + nrm(ks[14], (DEPTH, C_DV), 0.02)
    w_out = nrm(ks[15], (DEPTH, MIX_WIDTH, D_MODEL), DN_BETA * MIX_WIDTH ** -0.5)
    ln_g = 1.0 + nrm(ks[16], (DEPTH, D_MODEL), 0.02)
    ln_b = nrm(ks[17], (DEPTH, D_MODEL), 0.01)
    return {'x': x, 'c': c, 'ctx': ctx, 'c_ctx': c_ctx, 'w_mod': w_mod, 'b_mod': b_mod,
            'w_in': w_in, 'conv_w': conv_w, 'a_log': a_log, 'dt_bias': dt_bias,
            'gdn_norm': gdn_norm, 'rpb': rpb, 'gla_w2': gla_w2, 'gla_b2': gla_b2,
            'gla_norm': gla_norm, 'w_out': w_out, 'ln_g': ln_g, 'ln_b': ln_b}


def reference(x, c, ctx, c_ctx, w_mod, b_mod, w_in, conv_w, a_log, dt_bias, gdn_norm, rpb,
              gla_w2, gla_b2, gla_norm, w_out, ln_g, ln_b):
    xl, xc = x, ctx
    for i in range(DEPTH):
        ctx_out = i < DEPTH - 1
        sh_l, sc_l, gt_l = ada_modulation(c, w_mod[i], b_mod[i])
        sh_c, sc_c, gt_c = ada_modulation(c_ctx, w_mod[i], b_mod[i])
        ml = (layer_norm(xl) * (1.0 + sc_l) + sh_l).astype(xl.dtype)
        mc = (layer_norm(xc) * (1.0 + sc_c) + sh_c).astype(xc.dtype)
        (a_qkv_l, a_ab_l, a_z_l, b_q_l, b_k_l, b_v_l, b_z_l,
         c_q_l, c_k_l, c_v_l, c_r_l, c_z_l) = jnp.split(ml @ w_in[i], PROJ_SPLITS, axis=-1)
        (a_qkv_c, a_ab_c, a_z_c, b_q_c, b_k_c, b_v_c, b_z_c,
         c_q_c, c_k_c, c_v_c, c_r_c, c_z_c) = jnp.split(mc @ w_in[i], PROJ_SPLITS, axis=-1)
        oa_c, oa_l = gdn_branch(a_qkv_c, a_ab_c, a_qkv_l, a_ab_l, conv_w[i], a_log[i], dt_bias[i])
        ob_c, ob_l = na_branch(b_q_c, b_k_c, b_v_c, b_q_l, b_k_l, b_v_l, rpb[i], ctx_out)
        oc_c, oc_l = gla_branch(c_q_c, c_k_c, c_v_c, c_r_c, c_q_l, c_k_l, c_v_l, c_r_l,
                                gla_w2[i], gla_b2[i])
        y_l = jnp.concatenate([head_rmsnorm_gate(oa_l, gdn_norm[i], a_z_l),
                               ob_l * jax.nn.silu(b_z_l),
                               head_rmsnorm_gate(oc_l, gla_norm[i], c_z_l)], -1) @ w_out[i]
        if ctx_out:
            y_c = jnp.concatenate([head_rmsnorm_gate(oa_c, gdn_norm[i], a_z_c),
                                   ob_c * jax.nn.silu(b_z_c),
                                   head_rmsnorm_gate(oc_c, gla_norm[i], c_z_c)], -1) @ w_out[i]
            xc = post_norm(DN_ALPHA * xc + gt_c * y_c, ln_g[i], ln_b[i], xc.dtype)
        xl = post_norm(DN_ALPHA * xl + gt_l * y_l, ln_g[i], ln_b[i], xl.dtype)
    return xl
```

```python
import numpy as np
from contextlib import ExitStack
import concourse.bass as bass
import concourse.mybir as mybir
from concourse.bass_utils import run_bass_kernel_spmd

F32 = mybir.dt.float32
AF = mybir.ActivationFunctionType
ALU = mybir.AluOpType
AX = mybir.AxisListType

D = 1024
SEQ = 4096
CTX = 256
T = SEQ + CTX
NT = T // 128
DEPTH = 2
NTM = 1552
NFM = 2336
DN_ALPHA = (2 * DEPTH) ** 0.25
LN_EPS = 1e-6
SEM_ROT = 30000
DBG = {}
MERGE_C = True
CONV_ON_DVE = False
SWDGE_TO = "act"
FUSE_FIN = True
F32R = mybir.dt.float32r
FAST = dict(p2=False, p4=False, b=False, a0=False)


def fr(ap, on=True):
    return ap.bitcast(F32R) if on else ap

TM_AZ, TM_BV, TM_CQ, TM_CK, TM_CV, TM_CZ, TM_AB = 0, 512, 768, 896, 1024, 1280, 1536
FM_AQKV, FM_BQ, FM_BK, FM_BZ, FM_CR = 0, 1536, 1792, 2048, 2304


class Buf:
    __slots__ = ("name", "w", "r")

    def __init__(self, name):
        self.name = name
        self.w = None
        self.r = {}


class Prog:
    ENGS = ("pe", "act", "dve", "pool", "sp")

    def __init__(self, nc):
        self.nc = nc
        self.streams = {e: [] for e in self.ENGS}
        self.sems = []
        self.cur = {}
        self.cnt = {}
        self.waited = {e: {} for e in self.ENGS}
        self.dma_sem = {}
        self.dma_cnt = {}
        self.eng_of_sem = {}
        self.rec = None
        for e in self.ENGS:
            self._new_eng_sem(e)

    def record(self):
        assert self.rec is None
        self.rec = []

    def stop(self):
        r, self.rec = self.rec, None
        return r

    def play(self, lists, lead=None):
        idx = [0] * len(lists)
        lead = lead or [0] * len(lists)
        for i, l in enumerate(lists):
            for _ in range(min(lead[i], len(l))):
                l[idx[i]]()
                idx[i] += 1
        total = sum(len(l) - idx[i] for i, l in enumerate(lists))
        for _ in range(total):
            best, bf = None, None
            for i, l in enumerate(lists):
                if idx[i] < len(l):
                    fr = (idx[i] - lead[i]) / max(1, len(l) - lead[i])
                    if bf is None or fr < bf:
                        best, bf = i, fr
            lists[best][idx[best]]()
            idx[best] += 1

    def _new_sem(self, name):
        s = self.nc.alloc_semaphore(name)
        self.sems.append(s)
        return len(self.sems) - 1

    def _new_eng_sem(self, e):
        i = self._new_sem(f"s_{e}_{len(self.sems)}")
        self.cur[e] = i
        self.cnt[e] = 0
        self.eng_of_sem[i] = e

    def _deps(self, engine, reads, writes):
        deps = {}

        def add(tok, raw):
            if tok is None:
                return
            s, v = tok
            if (not raw) and self.eng_of_sem.get(s) == engine:
                return
            if deps.get(s, 0) < v:
                deps[s] = v

        for b in reads:
            add(b.w, True)
        for b in writes:
            add(b.w, False)
            for s, v in b.r.items():
                add((s, v), False)
        w = self.waited[engine]
        out = []
        for s, v in deps.items():
            if w.get(s, 0) < v:
                w[s] = v
                out.append((s, v))
        return out

    def _commit(self, tok, reads, writes):
        s, v = tok
        for b in reads:
            if b.r.get(s, 0) < v:
                b.r[s] = v
        for b in writes:
            b.w = tok
            b.r = {}

    def op(self, engine, fn, reads=(), writes=()):
        if self.rec is not None:
            self.rec.append(lambda: self.op(engine, fn, reads, writes))
            return
        waits = self._deps(engine, reads, writes)
        if self.cnt[engine] >= SEM_ROT:
            self._new_eng_sem(engine)
        self.cnt[engine] += 1
        tok = (self.cur[engine], self.cnt[engine])
        self._commit(tok, reads, writes)
        self.streams[engine].append((waits, fn, tok, 1))

    def dma(self, queue, out_ap, in_ap, reads=(), writes=(), key=None, **kw):
        if queue == "pool":
            queue = SWDGE_TO
        if self.rec is not None:
            self.rec.append(lambda: self.dma(queue, out_ap, in_ap, reads, writes, key, **kw))
            return
        waits = self._deps(queue, reads, writes)
        if key not in self.dma_sem or self.dma_cnt[key] >= SEM_ROT * 16:
            self.dma_sem[key] = self._new_sem(f"d_{len(self.sems)}")
            self.dma_cnt[key] = 0
        self.dma_cnt[key] += 16
        tok = (self.dma_sem[key], self.dma_cnt[key])
        self._commit(tok, reads, writes)
        self.streams[queue].append((waits, lambda e: e.dma_start(out=out_ap, in_=in_ap, **kw), tok, 16))

    def barrier(self):
        toks = [(self.cur[e], self.cnt[e]) for e in self.ENGS if self.cnt[e] > 0]
        toks += [(self.dma_sem[k], self.dma_cnt[k]) for k in self.dma_sem]
        for e in self.ENGS:
            w = self.waited[e]
            waits = []
            for s, v in toks:
                if self.eng_of_sem.get(s) == e:
                    continue
                if w.get(s, 0) < v:
                    w[s] = v
                    waits.append((s, v))
            if waits:
                self.streams[e].append((waits, None, None, 0))

    def final_wait(self, engine, bufs):
        waits = self._deps(engine, bufs, ())
        self.streams[engine].append((waits, None, None, 0))

    def emit(self):
        nc = self.nc
        with nc.Block() as block:
            def run(name):
                def body(eng):
                    for waits, fn, tok, inc in self.streams[name]:
                        for s, v in waits:
                            eng.wait_ge(self.sems[s], v)
                        if fn is not None:
                            ins = fn(eng)
                            ins.then_inc(self.sems[tok[0]], inc)
                return body
            block.tensor(run("pe"))
            block.scalar(run("act"))
            block.vector(run("dve"))
            block.gpsimd(run("pool"))
            block.sync(run("sp"))


class Arena:
    def __init__(self, sb, ncols):
        self.sb = sb
        self.ncols = ncols
        self.off = 0

    def alloc(self, n, name="t"):
        n = (n + 7) // 8 * 8
        assert self.off + n <= self.ncols, f"SBUF arena overflow at {name}: {self.off}+{n}>{self.ncols}"
        ap = self.sb[:, self.off:self.off + n]
        self.off += n
        return ap

    def mark(self):
        return self.off

    def release(self, m):
        self.off = m


def build_program(layers=(0, 1), phases=("P1", "P2", "A", "B", "P4"), dbg_in=(), dbg_out=()):
    nc = bass.Bass("TRN2", target_bir_lowering=False)
    P = Prog(nc)
    dbg = False

    def din(name, shape):
        return nc.dram_tensor(name, list(shape), F32, kind="ExternalInput").ap()

    def dscr(name, shape, out=False):
        kind = "ExternalOutput" if (out or name in dbg_out) else ("ExternalInput" if name in dbg_in else "Internal")
        return nc.dram_tensor(name, list(shape), F32, kind=kind).ap()

    x_all = din("x_all", (T, D))
    cvT = din("cvT", (128, 16))
    w_mod = din("w_mod", (DEPTH, D, 3 * D))
    b_mod2 = din("b_mod2", (DEPTH, 2, 3 * D))
    w_tm = din("w_tm", (DEPTH, D, NTM))
    w_fm = din("w_fm", (DEPTH, D, NFM))
    ident_d = din("ident", (128, 128))
    sel_d = din("sel", (2, 256))

    y_out = dscr("y", (SEQ, D), out=True)
    TM = dscr("TM", (T, NTM))
    FM = dscr("FM", (NFM, T))
    XC = dscr("XC", (T, D))
    MIXA = dscr("MIXA", (T, 512))
    MIXB = dscr("MIXB", (256, T))
    MIXC = dscr("MIXC", (T, 256))

    ones_d = din("ones", (128, 128))
    tril_d = din("tril", (128, 128))
    triu_d = din("triu", (128, 128))
    hm_d = din("hm", (128, 8))
    bd_d = din("bd", (128, 256))
    ropec_d = din("ropec", (SEQ, 128))
    ropes_d = din("ropes", (SEQ, 128))
    w2p_d = din("w2p", (DEPTH, 32, 256))
    b2r_d = din("b2r", (DEPTH, 1, 256))
    glan_d = din("glan", (DEPTH, 128, 256))
    gdnn_d = din("gdnn", (DEPTH, 128, 512))
    lng_d = din("lng", (DEPTH, 128, D))
    lnb_d = din("lnb", (DEPTH, 128, D))
    w_out = din("w_out", (DEPTH, D, D))
    nab_d = din("nab", (DEPTH, 128, 3840))
    convw_d = din("convw", (DEPTH, 128, 60))
    offs_d = din("offs", (128, 14 * 128))
    negm_d = din("negm", (128, 1024))
    offd_d = din("offd", (128, 128))
    alog_d = din("alog8", (DEPTH, 128, 8))
    dtb_d = din("dtb8", (DEPTH, 128, 8))
    QKV = dscr("QKV", (T, 1536))
    OAF = dscr("OAF", (T, 512))
    OAB = dscr("OAB", (T, 512))
    OCF = dscr("OCF", (T, 256))
    OCB = dscr("OCB", (T, 256))

    NCOLS = 53000
    sb_h = nc.alloc_sbuf_tensor("sb", [128, NCOLS], F32)
    A = Arena(sb_h, NCOLS)
    banks = [nc.alloc_psum_tensor(f"ps{i}", [128, 512], F32) for i in range(8)]
    PSB = [Buf(f"psum{i}") for i in range(8)]

    ident = A.alloc(128, "ident")
    sel = A.alloc(256, "sel")
    B_const = Buf("const")
    P.dma("sp", ident, ident_d, writes=[B_const], key="const")
    P.dma("sp", sel[0:2, :], sel_d, writes=[B_const], key="const")
    ones = A.alloc(128, "ones")
    tril = A.alloc(128, "tril")
    triu = A.alloc(128, "triu")
    hm = A.alloc(8, "hm")
    bd = A.alloc(256, "bd")
    P.dma("sp", ones, ones_d, writes=[B_const], key="const")
    P.dma("sp", tril, tril_d, writes=[B_const], key="const")
    P.dma("sp", triu, triu_d, writes=[B_const], key="const")
    P.dma("sp", hm, hm_d, writes=[B_const], key="const")
    P.dma("sp", bd, bd_d, writes=[B_const], key="const")
    modb = [A.alloc(3 * D, f"modb{r}") for r in range(2)]
    B_modb = Buf("modb")
    persist_mark = A.mark()

    out_bufs = []


    def mk(n, name, slots=1):
        return [(A.alloc(n, f"{name}{i}"), Buf(f"{name}{i}")) for i in range(slots)]

    def OP(eng, fn, r, w):
        P.op(eng, fn, reads=r, writes=w)

    def phase_A(layer):
        P.barrier()
        A.release(persist_mark)
        (cw, B_cw), = mk(64, "cw")
        P.dma("sp", cw[:, 0:60], convw_d[layer], writes=[B_cw], key="a_cw")
        dgs = mk(5 * 128, "dg", 2)
        xps = mk(4360, "xp", 2)
        yss = mk(512, "ys", 2)
        stgs = mk(512, "stg", 2)
        for xp, B_xp in xps:
            OP("pool", lambda e, xp=xp: e.memset(xp, 0.0), [], [B_xp])
        blocks = [(0, 256, 0)] + [(256 + 512 * b, 512, 260 + 512 * b) for b in range(8)]
        bi = 0
        pend = []
        for ct in range(12):
            sl = ct % 2
            xp, B_xp = xps[sl]
            dg, B_dg = dgs[sl]
            P.dma("sp", xp[:, 2:258], FM[ct * 128:(ct + 1) * 128, 0:256], writes=[B_xp], key=f"a_xp{sl}")
            P.dma("pool", xp[:, 262:4358], FM[ct * 128:(ct + 1) * 128, 256:T], writes=[B_xp], key=f"a_xp{sl}")
            for j in range(0 if CONV_ON_DVE else 5):
                OP("dve", lambda e, dg=dg, j=j, ct=ct: e.tensor_scalar(out=dg[:, j * 128:(j + 1) * 128], in0=ident,
                                                                        scalar1=cw[:, ct * 5 + j:ct * 5 + j + 1], scalar2=None, op0=ALU.mult),
                   [B_const, B_cw], [B_dg])
            for (t0, ntok, cb) in blocks:
                bs = bi % 2
                bi += 1
                ys, B_ys = yss[bs]
                stg, B_stg = stgs[bs]
                if CONV_ON_DVE:
                    OP("dve", lambda e, ys=ys, xp=xp, cb=cb, ntok=ntok, ct=ct: e.tensor_scalar(
                        out=ys[:, 0:ntok], in0=xp[:, cb:cb + ntok], scalar1=cw[:, ct * 5:ct * 5 + 1], scalar2=None, op0=ALU.mult), [B_xp, B_cw], [B_ys])
                    for j in range(1, 5):
                        OP("dve", lambda e, ys=ys, xp=xp, cb=cb, ntok=ntok, ct=ct, j=j: e.scalar_tensor_tensor(
                            out=ys[:, 0:ntok], in0=xp[:, cb + j:cb + j + ntok], scalar=cw[:, ct * 5 + j:ct * 5 + j + 1], in1=ys[:, 0:ntok],
                            op0=ALU.mult, op1=ALU.add), [B_xp, B_cw, B_ys], [B_ys])
                    OP("act", lambda e, ys=ys, ntok=ntok: e.activation(out=ys[:, 0:ntok], in_=ys[:, 0:ntok], func=AF.Silu), [B_ys], [B_ys])
                else:
                    def mmc(e, dg=dg, xp=xp, cb=cb, ntok=ntok, bs=bs):
                        ins = None
                        for j in range(5):
                            ins = e.matmul(banks[bs][:, 0:ntok], lhsT=dg[:, j * 128:(j + 1) * 128], rhs=xp[:, cb + j:cb + j + ntok],
                                           start=(j == 0), stop=(j == 4))
                        return ins
                    OP("pe", mmc, [B_dg, B_xp], [PSB[bs]])
                    OP("act", lambda e, ys=ys, bs=bs, ntok=ntok: e.activation(out=ys[:, 0:ntok], in_=banks[bs][:, 0:ntok], func=AF.Silu), [PSB[bs]], [B_ys])

                def tail(ys=ys, B_ys=B_ys, stg=stg, B_stg=B_stg, bs=bs, ntok=ntok, t0=t0, ct=ct):
                    def trc(e):
                        ins = None
                        for b in range(ntok // 128):
                            ins = e.transpose(banks[2 + bs][:, b * 128:(b + 1) * 128], ys[:, b * 128:(b + 1) * 128], ident)
                        return ins
                    OP("pe", trc, [B_ys, B_const], [PSB[2 + bs]])
                    OP("act", lambda e: e.copy(out=stg[:, 0:ntok], in_=banks[2 + bs][:, 0:ntok]), [PSB[2 + bs]], [B_stg])
                    P.dma("pool", QKV[t0:t0 + ntok, ct * 128:(ct + 1) * 128].rearrange("(b p) c -> p b c", p=128),
                          stg[:, 0:ntok].rearrange("p (b c) -> p b c", c=128), reads=[B_stg], writes=[Buf("qkvd")], key=f"a_stg{bs}")
                if pend:
                    pend.pop()()
                pend.append(tail)
        while pend:
            pend.pop()()

        P.barrier()
        A.release(persist_mark)
        (offs, B_offs), = mk(14 * 128, "offs")
        (negm, B_negm), = mk(2 * 512, "negm")
        (offd, B_offd), = mk(128, "offd")
        (al8, B_al8), = mk(8, "al8")
        (dt8, B_dt8), = mk(8, "dt8")
        P.dma("sp", offs, offs_d, writes=[B_offs], key="a_offs")
        P.dma("sp", negm, negm_d, writes=[B_negm], key="a_negm")
        P.dma("sp", offd, offd_d, writes=[B_offd], key="a_offd")
        P.dma("sp", al8, alog_d[layer], writes=[B_al8], key="a_al8")
        P.dma("sp", dt8, dtb_d[layer], writes=[B_dt8], key="a_dt8")
        OP("act", lambda e: e.activation(out=al8, in_=al8, func=AF.Exp), [B_al8], [B_al8])
        OP("dve", lambda e: e.tensor_scalar(out=al8, in0=al8, scalar1=-1.0, scalar2=None, op0=ALU.mult), [B_al8], [B_al8])
        TL = {}
        for nm, n in (("qkv", 1536), ("ab", 16), ("sq", 1024), ("st", 64), ("qkn", 1024), ("kqT", 1024), ("R", 512), ("DT", 512),
                      ("DTs", 512), ("N", 512), ("NT", 512), ("D", 512), ("Dt", 512), ("T1", 512), ("T2", 512),
                      ("qkT", 512), ("KG", 512), ("nW", 512), ("kdec", 512), ("vnew", 512), ("ot", 512), ("t2", 512)):
            TL[nm] = mk(n, "a" + nm, 4 if nm in ("qkv", "ab") else 2)
        TL["m1"], TL["m2"] = TL["T1"], TL["T2"]
        Ss = [mk(512, f"aS{d_}", 2) for d_ in range(2)]
        v3 = lambda ap: ap.rearrange("p (h c) -> p h c", h=4)
        hs_ = lambda ap: (lambda h: ap[:, h * 128:(h + 1) * 128])

        def a_load(tile, d, ql):
            r0 = tile * 128
            qkv, B_qkv = TL["qkv"][ql]
            ab, B_ab = TL["ab"][ql]
            P.dma("sp", qkv, QKV[r0:r0 + 128, :], writes=[B_qkv], key=f"a_qkv{ql}")
            P.dma("sp", ab, TM[r0:r0 + 128, TM_AB:TM_AB + 16], writes=[B_ab], key=f"a_ab{ql}")

        def a_chunk(tile, d, ql, S, B_S, Sn, B_Sn):
            r0 = tile * 128
            sl = d
            g = lambda nm: TL[nm][sl]
            qkv, B_qkv = TL["qkv"][ql]; ab, B_ab = TL["ab"][ql]; sq, B_sq = g("sq"); st, B_st = g("st")
            qkn, B_qkn = g("qkn"); kqT, B_kqT = g("kqT"); R_, B_R = g("R"); DT, B_DT = g("DT"); DTs, B_DTs = g("DTs")
            N_, B_N = g("N"); NT_, B_NT = g("NT"); Dm, B_Dm = g("D"); Dt, B_Dt = g("Dt"); T1, B_T1 = g("T1"); T2, B_T2 = g("T2")
            m1, B_m1 = g("m1"); m2, B_m2 = g("m2"); qkT, B_qkT = g("qkT"); KG, B_KG = g("KG"); nW, B_nW = g("nW")
            kdec, B_kdec = g("kdec"); vnew, B_vnew = g("vnew"); ot, B_ot = g("ot"); t2_, B_t2 = g("t2")
            bk = (lambda lb: 3 * sl + (0, 1, 2, 0, 1, 2, 0, 1, 2)[lb]) if MERGE_C else (lambda lb: 4 * sl + (0, 1, 2, 3, 0, 1, 2, 3, 0)[lb])
            BK = lambda lb: banks[bk(lb)]
            PB = lambda lb: PSB[bk(lb)]

            def mm4(lb, lf, rf, reads, start=True, stop=True):
                def f(e):
                    ins = None
                    for h in range(4):
                        ins = e.matmul(BK(lb)[:, h * 128:(h + 1) * 128], lhsT=lf(h), rhs=rf(h), start=start, stop=stop)
                    return ins
                OP("pe", f, reads, [PB(lb)])

            TR = tril if d == 0 else triu
            offN = (lambda li: offs[:, li * 128:(li + 1) * 128]) if d == 0 else (lambda li: offs[:, (7 + li) * 128:(8 + li) * 128])
            offT = (lambda li: offs[:, (7 + li) * 128:(8 + li) * 128]) if d == 0 else (lambda li: offs[:, li * 128:(li + 1) * 128])
            ngm = negm[:, d * 512:(d + 1) * 512]
            b4 = lambda ap: ap.unsqueeze(1).to_broadcast([128, 4, 128])
            c4 = lambda ap: ap.unsqueeze(2).to_broadcast([128, 4, 128])
            OP("act", lambda e: e.activation(out=sq, in_=qkv[:, 0:1024], func=AF.Square), [B_qkv], [B_sq])
            OP("dve", lambda e: e.reduce_sum(out=st[:, 0:8], in_=sq.rearrange("p (g c) -> p g c", g=8), axis=AX.X), [B_sq], [B_st])
            OP("act", lambda e: e.activation(out=st[:, 0:8], in_=st[:, 0:8], func=AF.Sqrt, bias=1e-6, scale=1.0), [B_st], [B_st])
            OP("dve", lambda e: e.reciprocal(out=st[:, 0:8], in_=st[:, 0:8]), [B_st], [B_st])
            OP("dve", lambda e: e.tensor_scalar(out=st[:, 0:4], in0=st[:, 0:4], scalar1=128 ** -0.5, scalar2=None, op0=ALU.mult), [B_st], [B_st])
            OP("dve", lambda e: e.tensor_tensor(out=qkn.rearrange("p (g c) -> p g c", g=8), in0=qkv[:, 0:1024].rearrange("p (g c) -> p g c", g=8),
                                                in1=st[:, 0:8].unsqueeze(2).to_broadcast([128, 8, 128]), op=ALU.mult), [B_qkv, B_st], [B_qkn])
            qn, kn, v_ = qkn[:, 0:512], qkn[:, 512:1024], qkv[:, 1024:1536]
            OP("dve", lambda e: e.tensor_tensor(out=st[:, 8:12], in0=ab[:, d * 4:d * 4 + 4], in1=dt8[:, d * 4:d * 4 + 4], op=ALU.add), [B_ab, B_dt8], [B_st])
            OP("act", lambda e: e.activation(out=st[:, 8:12], in_=st[:, 8:12], func=AF.Exp), [B_st], [B_st])
            OP("act", lambda e: e.activation(out=st[:, 8:12], in_=st[:, 8:12], func=AF.Ln, bias=1.0, scale=1.0), [B_st], [B_st])
            OP("dve", lambda e: e.tensor_tensor(out=st[:, 8:12], in0=st[:, 8:12], in1=al8[:, d * 4:d * 4 + 4], op=ALU.mult), [B_st, B_al8], [B_st])
            OP("act", lambda e: e.activation(out=st[:, 12:16], in_=ab[:, 8 + d * 4:12 + d * 4], func=AF.Exp, scale=-1.0), [B_ab], [B_st])
            OP("dve", lambda e: e.tensor_scalar_add(out=st[:, 12:16], in0=st[:, 12:16], scalar1=1.0), [B_st], [B_st])
            OP("dve", lambda e: e.reciprocal(out=st[:, 12:16], in_=st[:, 12:16]), [B_st], [B_st])
            g_, beta = st[:, 8:12], st[:, 12:16]
            def mmg(e):
                e.matmul(BK(0)[:, 0:4], lhsT=TR, rhs=g_, start=True, stop=True)
                return e.matmul(BK(0)[:, 4:8], lhsT=ones, rhs=g_, start=True, stop=True)
            OP("pe", mmg, [B_st, B_const], [PB(0)])
            OP("act", lambda e: e.copy(out=st[:, 16:24], in_=BK(0)[:, 0:8]), [PB(0)], [B_st])
            OP("dve", lambda e: e.tensor_scalar(out=st[:, 24:28], in0=st[:, 16:20], scalar1=-1.0, scalar2=None, op0=ALU.mult), [B_st], [B_st])
            OP("act", lambda e: e.activation(out=st[:, 28:32], in_=st[:, 16:20], func=AF.Exp), [B_st], [B_st])
            OP("dve", lambda e: e.tensor_tensor(out=st[:, 32:36], in0=st[:, 20:24], in1=st[:, 16:20], op=ALU.subtract), [B_st], [B_st])
            OP("act", lambda e: e.activation(out=st[:, 32:36], in_=st[:, 32:36], func=AF.Exp), [B_st], [B_st])
            OP("act", lambda e: e.activation(out=st[:, 36:40], in_=st[:, 20:24], func=AF.Exp), [B_st], [B_st])
            negGc, expG, kds, glast = st[:, 24:28], st[:, 28:32], st[:, 32:36], st[:, 36:40]
            OP("dve", lambda e: e.tensor_tensor(out=v3(R_), in0=b4(TR), in1=c4(g_), op=ALU.mult), [B_const, B_st], [B_R])
            def mmb(e):
                e.matmul(BK(1)[:, :], lhsT=ones, rhs=R_, start=True, stop=False)
                return e.matmul(BK(1)[:, :], lhsT=ident, rhs=ngm, start=False, stop=True)
            OP("pe", mmb, [B_R, B_const, B_negm], [PB(1)])
            for h in range(4):
                OP("act", lambda e, h=h: e.activation(out=DT[:, h * 128:(h + 1) * 128], in_=BK(1)[:, h * 128:(h + 1) * 128], func=AF.Exp,
                                                      bias=negGc[:, h:h + 1], scale=1.0), [PB(1), B_st], [B_DT])
            def trk(e):
                ins = None
                for h in range(4):
                    ins = e.transpose(BK(2)[:, h * 128:(h + 1) * 128], kn[:, h * 128:(h + 1) * 128], ident)
                for h in range(4):
                    ins = e.transpose(BK(3)[:, h * 128:(h + 1) * 128], qn[:, h * 128:(h + 1) * 128], ident)
                return ins
            OP("pe", trk, [B_qkn, B_const], [PB(2), PB(3)])
            OP("act", lambda e: e.copy(out=kqT[:, 0:512], in_=BK(2)[:, :]), [PB(2)], [B_kqT])
            OP("act", lambda e: e.copy(out=kqT[:, 512:1024], in_=BK(3)[:, :]), [PB(3)], [B_kqT])
            kT, qT = kqT[:, 0:512], kqT[:, 512:1024]
            mm4(4, hs_(kT), hs_(kT), [B_kqT])
            mm4(5, hs_(kT), hs_(qT), [B_kqT])
            for h in range(4):
                OP("dve", lambda e, h=h: e.scalar_tensor_tensor(out=N_[:, h * 128:(h + 1) * 128], in0=BK(4)[:, h * 128:(h + 1) * 128],
                                                                scalar=beta[:, h:h + 1], in1=DT[:, h * 128:(h + 1) * 128], op0=ALU.mult, op1=ALU.mult),
                   [PB(4), B_st, B_DT], [B_N])
            OP("dve", lambda e: e.tensor_tensor(out=qkT, in0=BK(5)[:, :], in1=DT, op=ALU.mult), [PB(5), B_DT], [B_qkT])
            def trn(e):
                ins = None
                for h in range(4):
                    ins = e.transpose(BK(6)[:, h * 128:(h + 1) * 128], N_[:, h * 128:(h + 1) * 128], ident)
                return ins
            OP("pe", trn, [B_N, B_const], [PB(6)])
            OP("act", lambda e: e.copy(out=NT_, in_=BK(6)[:, :]), [PB(6)], [B_NT])
            OP("dve", lambda e: e.tensor_tensor(out=v3(m1), in0=v3(N_), in1=b4(offN(0)), op=ALU.mult), [B_N, B_offs], [B_m1])
            OP("dve", lambda e: e.tensor_tensor(out=v3(m2), in0=v3(NT_), in1=b4(offT(0)), op=ALU.mult), [B_NT, B_offs], [B_m2])
            OP("dve", lambda e: e.tensor_tensor(out=v3(Dm), in0=b4(ident), in1=v3(m1), op=ALU.subtract), [B_const, B_m1], [B_Dm])
            OP("dve", lambda e: e.tensor_tensor(out=v3(Dt), in0=b4(ident), in1=v3(m2), op=ALU.subtract), [B_const, B_m2], [B_Dt])
            for li in range(1, 7):
                lastl = li == 6
                mm4(2, hs_(NT_), hs_(Dm), [B_NT, B_Dm])
                OP("dve", lambda e, li=li: e.tensor_tensor(out=v3(T1), in0=v3(BK(2)[:, :]), in1=b4(offN(li)), op=ALU.mult), [PB(2), B_offs], [B_T1])
                mm4(4, hs_(Dt), hs_(T1), [B_Dt, B_T1])
                OP("dve", lambda e: e.tensor_tensor(out=Dm, in0=Dm, in1=BK(4)[:, :], op=ALU.subtract), [B_Dm, PB(4)], [B_Dm])
                if not lastl:
                    def trd(e):
                        ins = None
                        for h in range(4):
                            ins = e.transpose(BK(3)[:, h * 128:(h + 1) * 128], Dm[:, h * 128:(h + 1) * 128], ident)
                        return ins
                    OP("pe", trd, [B_Dm, B_const], [PB(3)])
                    OP("act", lambda e: e.copy(out=Dt, in_=BK(3)[:, :]), [PB(3)], [B_Dt])
            X = Dm
            OP("dve", lambda e: e.tensor_tensor(out=v3(KG), in0=v3(kn), in1=c4(expG), op=ALU.mult), [B_qkn, B_st], [B_KG])
            OP("dve", lambda e: e.tensor_tensor(out=v3(kdec), in0=v3(kn), in1=c4(kds), op=ALU.mult), [B_qkn, B_st], [B_kdec])
            mm4(6, hs_(KG), hs_(X), [B_KG, B_Dm])
            OP("act", lambda e: e.mul(out=nW, in_=BK(6)[:, :], mul=-1.0), [PB(6)], [B_nW])
            def mmv(e):
                ins = None
                for h in range(4):
                    e.matmul(BK(7)[:, h * 128:(h + 1) * 128], lhsT=X[:, h * 128:(h + 1) * 128], rhs=v_[:, h * 128:(h + 1) * 128], start=True, stop=False)
                    ins = e.matmul(BK(7)[:, h * 128:(h + 1) * 128], lhsT=nW[:, h * 128:(h + 1) * 128], rhs=S[:, h * 128:(h + 1) * 128], start=False, stop=True)
                return ins
            OP("pe", mmv, [B_Dm, B_qkv, B_nW, B_S], [PB(7)])
            OP("dve", lambda e: e.tensor_tensor(out=v3(vnew), in0=v3(BK(7)[:, :]), in1=c4(beta), op=ALU.mult), [PB(7), B_st], [B_vnew])
            mm4(1, hs_(qT), hs_(S), [B_kqT, B_S])
            mm4(6, hs_(qkT), hs_(vnew), [B_qkT, B_vnew])
            mm4(8, hs_(kdec), hs_(vnew), [B_kdec, B_vnew])
            OP("dve", lambda e: e.tensor_tensor(out=v3(t2_), in0=v3(S), in1=c4(glast), op=ALU.mult), [B_S, B_st], [B_t2])
            OP("dve", lambda e: e.tensor_tensor(out=Sn, in0=t2_, in1=BK(8)[:, :], op=ALU.add), [B_t2, PB(8)], [B_Sn])
            OP("dve", lambda e: e.tensor_tensor(out=v3(ot), in0=v3(BK(1)[:, :]), in1=c4(expG), op=ALU.mult), [PB(1), B_st], [B_ot])
            OP("dve", lambda e: e.tensor_tensor(out=ot, in0=ot, in1=BK(6)[:, :], op=ALU.add), [B_ot, PB(6)], [B_ot])
            P.dma("pool", (OAF if d == 0 else OAB)[r0:r0 + 128, :], ot, reads=[B_ot], writes=[Buf("oad")], key=f"a_ot{sl}")

        lists = []
        for d in range(2):
            order = list(range(NT)) if d == 0 else [1, 0] + list(range(NT - 1, 1, -1))
            if DBG.get('a_tiles'):
                order = [t_ for t_ in order if t_ in DBG['a_tiles']]
            P.record()
            OP("pool", lambda e, d=d: e.memset(Ss[d][0][0], 0.0), [], [Ss[d][0][1]])
            a_load(order[0], d, d * 2)
            cur = 0
            for it, tile in enumerate(order):
                if it + 1 < len(order):
                    a_load(order[it + 1], d, d * 2 + (it + 1) % 2)
                a_chunk(tile, d, d * 2 + it % 2, Ss[d][cur][0], Ss[d][cur][1], Ss[d][1 - cur][0], Ss[d][1 - cur][1])
                cur = 1 - cur
            lists.append(P.stop())
        cfin = None
        if MERGE_C:
            clists, cfin = c_build(layer)
            if DBG.get("seq_c"):
                P.play(lists)
                lists = clists
            else:
                lists = lists + clists
        per_chunk = len(lists[0]) // max(1, NT)
        P.play(lists, lead=[0, DBG.get("a_lead", per_chunk // 2)] + [0] * (len(lists) - 2))

        if FUSE_FIN:
            return
        P.barrier()
        A.release(persist_mark)
        (gnn2, B_gnn2), = mk(512, "gnn2")
        P.dma("sp", gnn2, gdnn_d[layer], writes=[B_gnn2], key="a2_gnn")
        fs_ = mk(512, "a2f", 2)
        bs_ = mk(512, "a2b", 2)
        zs_ = mk(512, "a2z", 2)
        sqs_ = mk(512, "a2sq", 2)
        sts_ = mk(16, "a2st", 2)
        tiles2 = list(range(NT))
        if DBG.get('a_tiles'):
            tiles2 = [t_ for t_ in tiles2 if t_ in DBG['a_tiles']]

        def a2_load(tile, sl):
            r0 = tile * 128
            P.dma("sp", fs_[sl][0], OAF[r0:r0 + 128, :], writes=[fs_[sl][1]], key=f"a2f{sl}")
            P.dma("sp", bs_[sl][0], OAB[r0:r0 + 128, :], writes=[bs_[sl][1]], key=f"a2b{sl}")
            P.dma("sp", zs_[sl][0], TM[r0:r0 + 128, TM_AZ:TM_AZ + 512], writes=[zs_[sl][1]], key=f"a2z{sl}")

        def a2_comp(tile, sl):
            r0 = tile * 128
            f_, B_f = fs_[sl]
            b_, B_b = bs_[sl]
            z_, B_z = zs_[sl]
            sq, B_sq = sqs_[sl]
            st, B_st = sts_[sl]
            OP("dve", lambda e: e.tensor_tensor(out=f_, in0=f_, in1=b_, op=ALU.add), [B_f, B_b], [B_f])
            OP("pool", lambda e: e.tensor_tensor(out=sq, in0=f_, in1=f_, op=ALU.mult), [B_f], [B_sq])
            OP("dve", lambda e: e.reduce_sum(out=st[:, 0:4], in_=v3(sq), axis=AX.X), [B_sq], [B_st])
            OP("act", lambda e: e.activation(out=st[:, 0:4], in_=st[:, 0:4], func=AF.Sqrt, bias=1e-6, scale=1.0 / 128), [B_st], [B_st])
            OP("dve", lambda e: e.reciprocal(out=st[:, 0:4], in_=st[:, 0:4]), [B_st], [B_st])
            OP("dve", lambda e: e.tensor_tensor(out=v3(f_), in0=v3(f_), in1=st[:, 0:4].unsqueeze(2).to_broadcast([128, 4, 128]), op=ALU.mult),
               [B_f, B_st], [B_f])
            OP("act", lambda e: e.activation(out=z_, in_=z_, func=AF.Silu), [B_z], [B_z])
            OP("pool", lambda e: e.tensor_tensor(out=z_, in0=z_, in1=gnn2, op=ALU.mult), [B_z, B_gnn2], [B_z])
            OP("dve", lambda e: e.tensor_tensor(out=f_, in0=f_, in1=z_, op=ALU.mult), [B_f, B_z], [B_f])
            P.dma("pool", MIXA[r0:r0 + 128, :], f_, reads=[B_f], writes=[Buf("mixa")], key=f"a2o{sl}")

        a2_load(tiles2[0], 0)
        for it, tile in enumerate(tiles2):
            if it + 1 < len(tiles2):
                a2_load(tiles2[it + 1], (it + 1) % 2)
            a2_comp(tile, it % 2)
        if cfin is not None:
            cfin()

    def phase_B(layer):
        P.barrier()
        A.release(persist_mark)
        (kT, B_kT), = mk(2 * T, "kT")
        (qT, B_qT), = mk(2 * T, "qT")
        (V1, B_V1), = mk(NT * 260, "V1")
        (G, B_G), = mk(4 * 15 * 64, "G")
        for hp in range(2):
            P.dma("sp", kT[:, hp * T:(hp + 1) * T], FM[FM_BK + hp * 128:FM_BK + (hp + 1) * 128, :], writes=[B_kT], key="b_kT")
            P.dma("pool", qT[:, hp * T:(hp + 1) * T], FM[FM_BQ + hp * 128:FM_BQ + (hp + 1) * 128, :], writes=[B_qT], key="b_qT")
        P.dma("sp", G, nab_d[layer], writes=[B_G], key="b_G")
        OP("pool", lambda e: e.memset(V1, 1.0), [], [B_V1])
        for t_ in range(NT):
            P.dma("sp", V1[:, t_ * 260:(t_ + 1) * 260].rearrange("p (h c) -> p h c", h=4)[:, :, 0:64],
                  TM[t_ * 128:(t_ + 1) * 128, TM_BV:TM_BV + 256].rearrange("p (h c) -> p h c", h=4), writes=[B_V1], key="b_V1")
        G4 = G.rearrange("p (h e c) -> p h e c", h=4, e=15)
        tmps = mk(512, "btmp", 4)
        pTs = mk(512, "bpT", 4)
        nums = mk(512, "bnum", 4)
        rrs = mk(512, "brr", 4)
        zts = mk(512, "bz", 4)
        qtiles = [(256 + 512 * a, 512, a) for a in range(8)]
        if layer < DEPTH - 1:
            qtiles = [(0, 256, None)] + qtiles
        cnts = [dict(st=0, w=0, o=0) for _ in range(2)]
        def do_qtile(h, c0, nq, a):
            if True:
                hp, base = h // 2, (h % 2) * 64
                sm = h % 2
                cnt = cnts[sm]
                ob = 4 * sm + 2
                fs = 2 * sm + cnt["o"] % 2
                cnt["o"] += 1
                qop = qT[base:base + 64, hp * T + c0:hp * T + c0 + nq]
                first = True
                for kt in range(2):
                    sb_ = 4 * sm + cnt["st"] % 2
                    cnt["st"] += 1
                    ws = 2 * sm + cnt["w"] % 2
                    cnt["w"] += 1
                    pT, B_pT = pTs[ws]
                    OP("pe", lambda e, sb_=sb_, kt=kt, qop=qop, nq=nq: e.matmul(
                        banks[sb_][:, 0:nq], lhsT=kT[base:base + 64, hp * T + kt * 128:hp * T + (kt + 1) * 128], rhs=qop, start=True, stop=True),
                        [B_kT, B_qT], [PSB[sb_]])
                    OP("act", lambda e, sb_=sb_, pT=pT, nq=nq: e.activation(out=pT[:, 0:nq], in_=banks[sb_][:, 0:nq], func=AF.Exp, scale=0.125),
                       [PSB[sb_]], [B_pT])
                    last_mm = (a is None and kt == 1)
                    def pv(e, ob=ob, kt=kt, pT=pT, nq=nq, first=first, last_mm=last_mm):
                        e.matmul(banks[ob][0:64, 0:nq], lhsT=V1[:, kt * 260 + h * 65:kt * 260 + h * 65 + 64], rhs=pT[:, 0:nq], start=first, stop=last_mm)
                        return e.matmul(banks[ob + 1][0:64, 0:nq], lhsT=ones[:, 0:64], rhs=pT[:, 0:nq], start=first, stop=last_mm)
                    OP("pe", pv, [B_V1, B_pT, B_const], [PSB[ob], PSB[ob + 1]])
                    first = False
                if a is not None:
                    rows = {}
                    for kr in range(64):
                        js = [j for j in range(8) if min(max(8 * a + j - 4, 0), 56) <= kr <= min(max(8 * a + j - 4, 0), 56) + 7]
                        if js:
                            rows[kr] = (js[0], js[-1])
                    kts = sorted(set(kr // 2 for kr in rows))
                    for mi, m in enumerate(kts):
                        halves = [(hf, rows[2 * m + hf]) for hf in range(2) if (2 * m + hf) in rows]
                        jl = min(v[0] for _, v in halves)
                        jh = max(v[1] for _, v in halves)
                        sb_ = 4 * sm + cnt["st"] % 2
                        cnt["st"] += 1
                        ws = 2 * sm + cnt["w"] % 2
                        cnt["w"] += 1
                        tmp, B_tmp = tmps[ws]
                        pT, B_pT = pTs[ws]
                        tk = 2 + m
                        OP("pe", lambda e, sb_=sb_, tk=tk, jl=jl, jh=jh: e.matmul(
                            banks[sb_][:, jl * 64:(jh + 1) * 64], lhsT=kT[base:base + 64, hp * T + tk * 128:hp * T + (tk + 1) * 128],
                            rhs=qT[base:base + 64, hp * T + c0 + jl * 64:hp * T + c0 + (jh + 1) * 64], start=True, stop=True),
                            [B_kT, B_qT], [PSB[sb_]])
                        ucs = slice(jl * 64, (jh + 1) * 64)
                        uneven = any((j0, j1) != (jl, jh) for _, (j0, j1) in halves) or len(halves) < 2
                        if uneven:
                            OP("pool", lambda e, pT=pT, ucs=ucs: e.memset(pT[:, ucs], 0.0), [], [B_pT])
                        for hi, (hf, (j0, j1)) in enumerate(halves):
                            kr = 2 * m + hf
                            e_lo = 7 - kr + 8 * a + j0
                            nj = j1 - j0 + 1
                            assert 0 <= e_lo and e_lo + nj <= 15
                            pr = slice(hf * 64, hf * 64 + 64)
                            cs = slice(j0 * 64, (j1 + 1) * 64)
                            OP("dve", lambda e, tmp=tmp, sb_=sb_, pr=pr, cs=cs, e_lo=e_lo, nj=nj: e.scalar_tensor_tensor(
                                out=tmp[pr, cs].rearrange("p (j c) -> p j c", c=64), in0=banks[sb_][pr, cs].rearrange("p (j c) -> p j c", c=64),
                                scalar=0.125, in1=G4[pr, h, e_lo:e_lo + nj, :], op0=ALU.mult, op1=ALU.add), [PSB[sb_], B_G], [B_tmp])
                            OP("act", lambda e, tmp=tmp, pT=pT, pr=pr, cs=cs: e.activation(out=pT[pr, cs], in_=tmp[pr, cs], func=AF.Exp),
                               [B_tmp], [B_pT])
                        last_mm = (mi == len(kts) - 1)
                        def pv(e, ob=ob, tk=tk, pT=pT, ucs=ucs, last_mm=last_mm):
                            e.matmul(banks[ob][0:64, ucs], lhsT=V1[:, tk * 260 + h * 65:tk * 260 + h * 65 + 64], rhs=pT[:, ucs], start=False, stop=last_mm)
                            return e.matmul(banks[ob + 1][0:64, ucs], lhsT=ones[:, 0:64], rhs=pT[:, ucs], start=False, stop=last_mm)
                        OP("pe", pv, [B_V1, B_pT, B_const], [PSB[ob], PSB[ob + 1]])
                num, B_num = nums[fs]
                rr, B_rr = rrs[fs]
                zt, B_zt = zts[fs]
                P.dma("sp", zt[0:64, 0:nq], FM[FM_BZ + h * 64:FM_BZ + (h + 1) * 64, c0:c0 + nq], writes=[B_zt], key=f"b_z{fs}")
                OP("act", lambda e, zt=zt, nq=nq: e.activation(out=zt[0:64, 0:nq], in_=zt[0:64, 0:nq], func=AF.Silu), [B_zt], [B_zt])
                OP("act", lambda e, num=num, ob=ob, nq=nq: e.copy(out=num[0:64, 0:nq], in_=banks[ob][0:64, 0:nq]), [PSB[ob]], [B_num])
                OP("dve", lambda e, rr=rr, ob=ob, nq=nq: e.reciprocal(out=rr[0:64, 0:nq], in_=banks[ob + 1][0:64, 0:nq]), [PSB[ob + 1]], [B_rr])
                OP("dve", lambda e, num=num, rr=rr, nq=nq: e.tensor_tensor(out=num[0:64, 0:nq], in0=num[0:64, 0:nq], in1=rr[0:64, 0:nq], op=ALU.mult),
                   [B_num, B_rr], [B_num])
                OP("pool", lambda e, num=num, zt=zt, nq=nq: e.tensor_tensor(out=num[0:64, 0:nq], in0=num[0:64, 0:nq], in1=zt[0:64, 0:nq], op=ALU.mult),
                   [B_num, B_zt], [B_num])
                P.dma("sp", MIXB[h * 64:(h + 1) * 64, c0:c0 + nq], num[0:64, 0:nq], reads=[B_num], writes=[Buf("mixb")], key=f"b_o{fs}")
        blists = []
        for sm_ in range(2):
            P.record()
            for h in (sm_, sm_ + 2):
                for (c0, nq, a) in qtiles:
                    do_qtile(h, c0, nq, a)
            blists.append(P.stop())
        P.play(blists)

    def c_build(layer):
        (w2p, B_w2p), = mk(256, "w2p")
        (b2r, B_b2r), = mk(256, "b2r")
        P.dma("sp", w2p[0:32, :], w2p_d[layer], writes=[B_w2p], key="c_w2p")
        P.dma("sp", b2r[0:1, :], b2r_d[layer], writes=[B_b2r], key="c_b2r")
        CT = {}
        for nm, n in (("tc", 768), ("rt", 128), ("cos", 128), ("sin", 128)):
            CT[nm] = mk(n, "c" + nm, 4)
        for nm, n in (("xsw", 256), ("sp", 128), ("Ep", 128), ("Em", 128), ("gl", 8), ("qh", 128), ("kh", 128), ("qkT", 256),
                      ("khTm", 512), ("am", 512), ("tmp", 256), ("os", 256)):
            CT[nm] = mk(n, "c" + nm, 2)
        CS = [mk(256, f"cS{d_}", 2) for d_ in range(2)]
        scale_q = 32 ** -0.5

        def c_load(tile, d, ql):
            r0 = tile * 128
            tc, B_tc = CT["tc"][ql]
            rt, B_rt = CT["rt"][ql]
            P.dma("sp", tc, TM[r0:r0 + 128, 768:1536], writes=[B_tc], key=f"c_tc{ql}")
            P.dma("sp", rt[0:32, :], FM[FM_CR:FM_CR + 32, r0:r0 + 128], writes=[B_rt], key=f"c_rt{ql}")
            if tile >= 2:
                l0 = (tile - 2) * 128
                P.dma("sp", CT["cos"][ql][0], ropec_d[l0:l0 + 128, :], writes=[CT["cos"][ql][1]], key=f"c_cs{ql}")
                P.dma("sp", CT["sin"][ql][0], ropes_d[l0:l0 + 128, :], writes=[CT["sin"][ql][1]], key=f"c_sn{ql}")

        def c_chunk(tile, d, ql, S, B_S, Sn, B_Sn):
            r0 = tile * 128
            g = lambda nm: CT[nm][d]
            tc, B_tc = CT["tc"][ql]; rt, B_rt = CT["rt"][ql]; cs_, B_cs = CT["cos"][ql]; sn, B_sn = CT["sin"][ql]
            xsw, B_xsw = g("xsw"); sp_, B_sp = g("sp"); Ep, B_Ep = g("Ep"); Em, B_Em = g("Em"); gl, B_gl = g("gl")
            qh, B_qh = g("qh"); kh, B_kh = g("kh"); qkT, B_qhT = g("qkT"); khTm, B_khTm = g("khTm"); am, B_am = g("am")
            tmp, B_tmp = g("tmp"); os_, B_os = g("os")
            qhT = qkT[:, 0:128]
            TR = tril if d == 0 else triu
            bk = (lambda lb: 6 + d) if MERGE_C else (lambda lb: 4 * d + lb % 4)
            BK = lambda lb: banks[bk(lb)]
            PB = lambda lb: PSB[bk(lb)]
            if tile >= 2:
                x4 = tc[:, 0:256].rearrange("p (g two f) -> p g two f", two=2, f=8)
                xs4 = xsw.rearrange("p (g two f) -> p g two f", two=2, f=8)
                OP("pool", lambda e: e.tensor_copy(out=xs4[:, :, 0, :], in_=x4[:, :, 1, :]), [B_tc], [B_xsw])
                OP("pool", lambda e: e.tensor_copy(out=xs4[:, :, 1, :], in_=x4[:, :, 0, :]), [B_tc], [B_xsw])
                x3 = tc[:, 0:256].rearrange("p (a c) -> p a c", a=2)
                xs3 = xsw.rearrange("p (a c) -> p a c", a=2)
                OP("dve", lambda e: e.tensor_tensor(out=x3, in0=x3, in1=cs_.unsqueeze(1).to_broadcast([128, 2, 128]), op=ALU.mult), [B_tc, B_cs], [B_tc])
                OP("dve", lambda e: e.tensor_tensor(out=xs3, in0=xs3, in1=sn.unsqueeze(1).to_broadcast([128, 2, 128]), op=ALU.mult), [B_xsw, B_sn], [B_xsw])
                OP("dve", lambda e: e.tensor_tensor(out=tc[:, 0:256], in0=tc[:, 0:256], in1=xsw, op=ALU.add), [B_tc, B_xsw], [B_tc])
            def mmz(e):
                e.matmul(BK(0)[:, 0:128], lhsT=rt[0:32, :], rhs=w2p[0:32, d * 128:(d + 1) * 128], start=True, stop=False)
                return e.matmul(BK(0)[:, 0:128], lhsT=ones[0:1, 0:128], rhs=b2r[0:1, d * 128:(d + 1) * 128], start=False, stop=True)
            OP("pe", mmz, [B_rt, B_w2p, B_b2r, B_const], [PB(0)])
            OP("act", lambda e: e.activation(out=sp_, in_=BK(0)[:, 0:128], func=AF.Exp, scale=-1.0), [PB(0)], [B_sp])
            OP("act", lambda e: e.activation(out=sp_, in_=sp_, func=AF.Ln, bias=1.0, scale=1.0), [B_sp], [B_sp])
            def mmc(e):
                e.matmul(BK(1)[:, 0:128], lhsT=TR, rhs=sp_, start=True, stop=True)
                return e.matmul(BK(1)[:, 128:129], lhsT=sp_, rhs=ones[:, 0:1], start=True, stop=True)
            OP("pe", mmc, [B_sp, B_const], [PB(1)])
            OP("act", lambda e: e.activation(out=Ep, in_=BK(1)[:, 0:128], func=AF.Exp, scale=-1.0 / 16), [PB(1)], [B_Ep])
            OP("act", lambda e: e.activation(out=Em, in_=BK(1)[:, 0:128], func=AF.Exp, scale=1.0 / 16), [PB(1)], [B_Em])
            OP("act", lambda e: e.activation(out=gl[:, 0:1], in_=BK(1)[:, 128:129], func=AF.Exp, scale=-1.0 / 16), [PB(1)], [B_gl])
            OP("dve", lambda e: e.scalar_tensor_tensor(out=qh, in0=tc[:, 0:128], scalar=scale_q, in1=Ep, op0=ALU.mult, op1=ALU.mult), [B_tc, B_Ep], [B_qh])
            OP("dve", lambda e: e.tensor_tensor(out=kh, in0=tc[:, 128:256], in1=Em, op=ALU.mult), [B_tc, B_Em], [B_kh])
            def trs(e):
                e.transpose(BK(2)[:, 0:128], qh, ident)
                return e.transpose(BK(2)[:, 128:256], kh, ident)
            OP("pe", trs, [B_qh, B_kh, B_const], [PB(2)])
            OP("act", lambda e: e.copy(out=qkT, in_=BK(2)[:, 0:256]), [PB(2)], [B_qhT])
            for h in range(4):
                OP("dve", lambda e, h=h: e.tensor_scalar(out=khTm[:, h * 128:(h + 1) * 128], in0=qkT[:, 128:256], scalar1=hm[:, h:h + 1],
                                                         scalar2=None, op0=ALU.mult), [B_qhT, B_const], [B_khTm])
            def mma(e):
                ins = None
                for h in range(4):
                    ins = e.matmul(BK(3)[:, h * 128:(h + 1) * 128], lhsT=khTm[:, h * 128:(h + 1) * 128], rhs=qhT, start=True, stop=True)
                return ins
            OP("pe", mma, [B_khTm, B_qhT], [PB(3)])
            OP("dve", lambda e: e.tensor_tensor(out=am.rearrange("p (h j) -> p h j", h=4), in0=BK(3)[:, :].rearrange("p (h j) -> p h j", h=4),
                                                in1=TR.unsqueeze(1).to_broadcast([128, 4, 128]), op=ALU.mult), [PB(3), B_const], [B_am])
            def mmo(e):
                e.matmul(BK(0)[:, 0:256], lhsT=qhT, rhs=S, start=True, stop=False)
                ins = None
                for h in range(4):
                    ins = e.matmul(BK(0)[:, h * 64:(h + 1) * 64], lhsT=am[:, h * 128:(h + 1) * 128],
                                   rhs=tc[:, 256 + h * 64:256 + (h + 1) * 64], start=False, stop=(h == 3))
                return ins
            OP("pe", mmo, [B_qhT, B_S, B_am, B_tc], [PB(0)])
            OP("act", lambda e: e.copy(out=os_, in_=BK(0)[:, 0:256]), [PB(0)], [B_os])
            OP("pe", lambda e: e.matmul(BK(1)[:, 0:256], lhsT=kh, rhs=tc[:, 256:512], start=True, stop=True), [B_kh, B_tc], [PB(1)])
            OP("dve", lambda e: e.tensor_tensor(out=tmp, in0=BK(1)[:, 0:256], in1=bd, op=ALU.mult), [PB(1), B_const], [B_tmp])
            OP("dve", lambda e: e.tensor_tensor(out=tmp, in0=tmp, in1=S, op=ALU.add), [B_tmp, B_S], [B_tmp])
            OP("dve", lambda e: e.tensor_scalar(out=Sn, in0=tmp, scalar1=gl[:, 0:1], scalar2=None, op0=ALU.mult), [B_tmp, B_gl], [B_Sn])
            P.dma("pool", (OCF if d == 0 else OCB)[r0:r0 + 128, :], os_, reads=[B_os], writes=[Buf("ocd")], key=f"c_os{d}")

        lists = []
        for d in range(2):
            order = list(range(NT)) if d == 0 else [1, 0] + list(range(NT - 1, 1, -1))
            if DBG.get('c_tiles'):
                order = [t_ for t_ in order if t_ in DBG['c_tiles']]
            P.record()
            OP("pool", lambda e, d=d: e.memset(CS[d][0][0], 0.0), [], [CS[d][0][1]])
            c_load(order[0], d, d * 2)
            cur = 0
            for it, tile in enumerate(order):
                if it + 1 < len(order):
                    c_load(order[it + 1], d, d * 2 + (it + 1) % 2)
                c_chunk(tile, d, d * 2 + it % 2, CS[d][cur][0], CS[d][cur][1], CS[d][1 - cur][0], CS[d][1 - cur][1])
                cur = 1 - cur
            lists.append(P.stop())

        def c_final():
            (gn, B_gn), = mk(256, "gn")
            P.dma("sp", gn, glan_d[layer], writes=[B_gn], key="c_gn")
            fs_ = mk(256, "c2f", 2)
            bs_ = mk(256, "c2b", 2)
            zs_ = mk(256, "c2z", 2)
            sqs_ = mk(256, "c2sq", 2)
            sts_ = mk(16, "c2st", 2)
            tiles2 = list(range(NT))
            if DBG.get('c_tiles'):
                tiles2 = [t_ for t_ in tiles2 if t_ in DBG['c_tiles']]

            def ld(tile, sl):
                r0 = tile * 128
                P.dma("sp", fs_[sl][0], OCF[r0:r0 + 128, :], writes=[fs_[sl][1]], key=f"c2f{sl}")
                P.dma("sp", bs_[sl][0], OCB[r0:r0 + 128, :], writes=[bs_[sl][1]], key=f"c2b{sl}")
                P.dma("sp", zs_[sl][0], TM[r0:r0 + 128, TM_CZ:TM_CZ + 256], writes=[zs_[sl][1]], key=f"c2z{sl}")

            def comp(tile, sl):
                r0 = tile * 128
                f_, B_f = fs_[sl]
                b_, B_b = bs_[sl]
                z_, B_z = zs_[sl]
                sq, B_sq = sqs_[sl]
                st, B_st = sts_[sl]
                v4 = lambda ap: ap.rearrange("p (h v) -> p h v", h=4)
                OP("dve", lambda e: e.tensor_tensor(out=f_, in0=f_, in1=b_, op=ALU.add), [B_f, B_b], [B_f])
                OP("pool", lambda e: e.tensor_tensor(out=sq, in0=f_, in1=f_, op=ALU.mult), [B_f], [B_sq])
                OP("dve", lambda e: e.reduce_sum(out=st[:, 0:4], in_=v4(sq), axis=AX.X), [B_sq], [B_st])
                OP("act", lambda e: e.activation(out=st[:, 4:8], in_=st[:, 0:4], func=AF.Sqrt, bias=1e-6, scale=1.0 / 64), [B_st], [B_st])
                OP("dve", lambda e: e.reciprocal(out=st[:, 4:8], in_=st[:, 4:8]), [B_st], [B_st])
                OP("dve", lambda e: e.tensor_tensor(out=v4(f_), in0=v4(f_), in1=st[:, 4:8].unsqueeze(2).to_broadcast([128, 4, 64]), op=ALU.mult),
                   [B_f, B_st], [B_f])
                OP("act", lambda e: e.activation(out=z_, in_=z_, func=AF.Silu), [B_z], [B_z])
                OP("pool", lambda e: e.tensor_tensor(out=z_, in0=z_, in1=gn, op=ALU.mult), [B_z, B_gn], [B_z])
                OP("dve", lambda e: e.tensor_tensor(out=f_, in0=f_, in1=z_, op=ALU.mult), [B_f, B_z], [B_f])
                P.dma("pool", MIXC[r0:r0 + 128, :], f_, reads=[B_f], writes=[Buf("mixc")], key=f"c2o{sl}")

            ld(tiles2[0], 0)
            for it, tile in enumerate(tiles2):
                if it + 1 < len(tiles2):
                    ld(tiles2[it + 1], (it + 1) % 2)
                comp(tile, it % 2)

        return lists, c_final

    def phase_C(layer):
        P.barrier()
        A.release(persist_mark)
        lists, fin = c_build(layer)
        P.play(lists)
        P.barrier()
        A.release(persist_mark)
        fin()

    def phase_P4(layer):
        P.barrier()
        A.release(persist_mark)
        (wo, B_wo), = mk(8 * D, "wo")
        (lng, B_lng), = mk(D, "lng")
        (lnb, B_lnb), = mk(D, "lnb")
        wov = w_out[layer].rearrange("(k p) c -> p k c", p=128)
        for k in range(8):
            P.dma("pool", wo[:, k * D:(k + 1) * D], wov[:, k, :], writes=[B_wo], key="p4_wo")
        P.dma("sp", lng, lng_d[layer], writes=[B_lng], key="p4_lng")
        P.dma("sp", lnb, lnb_d[layer], writes=[B_lnb], key="p4_lnb")
        xas = mk(D, "xa", 2)
        mas = mk(512, "ma", 2)
        mcs = mk(256, "mc", 2)
        mixTs = mk(8 * 128, "mixT", 2)
        hs = mk(D, "h", 2)
        sts = mk(16, "st", 2)
        if FUSE_FIN:
            mbs = mk(768, "mb2", 2)
            zzs = mk(768, "zz", 2)
            sqs4 = mk(768, "sq4", 2)
            st8s = mk(16, "st8", 2)
            (gno, B_gno), = mk(768, "gno")
            P.dma("sp", gno[:, 0:512], gdnn_d[layer], writes=[B_gno], key="p4_gno")
            P.dma("sp", gno[:, 512:768], glan_d[layer], writes=[B_gno], key="p4_gno")
        last = layer == DEPTH - 1
        xsrc = x_all if layer == 0 else XC
        tiles = list(range(2, NT)) if last else list(range(NT))
        def p4_load(tile, sl):
            r0 = tile * 128
            xa, B_xa = xas[sl]
            ma, B_ma = mas[sl]
            mc, B_mc = mcs[sl]
            mixT, B_mixT = mixTs[sl]
            P.dma("sp", xa, xsrc[r0:r0 + 128, :], writes=[B_xa], key=f"p4_xa{sl}")
            if FUSE_FIN:
                mb_, B_mb = mbs[sl]
                zz, B_zz = zzs[sl]
                P.dma("sp", ma, OAF[r0:r0 + 128, :], writes=[B_ma], key=f"p4_ma{sl}")
                P.dma("sp", mc, OCF[r0:r0 + 128, :], writes=[B_mc], key=f"p4_mc{sl}")
                P.dma("sp", mb_[:, 0:512], OAB[r0:r0 + 128, :], writes=[B_mb], key=f"p4_mb2{sl}")
                P.dma("sp", mb_[:, 512:768], OCB[r0:r0 + 128, :], writes=[B_mb], key=f"p4_mb2{sl}")
                P.dma("sp", zz[:, 0:512], TM[r0:r0 + 128, TM_AZ:TM_AZ + 512], writes=[B_zz], key=f"p4_zz{sl}")
                P.dma("sp", zz[:, 512:768], TM[r0:r0 + 128, TM_CZ:TM_CZ + 256], writes=[B_zz], key=f"p4_zz{sl}")
            else:
                P.dma("sp", ma, MIXA[r0:r0 + 128, :], writes=[B_ma], key=f"p4_ma{sl}")
                P.dma("sp", mc, MIXC[r0:r0 + 128, :], writes=[B_mc], key=f"p4_mc{sl}")
            for j in range(2):
                P.dma("sp", mixT[:, (4 + j) * 128:(5 + j) * 128], MIXB[j * 128:(j + 1) * 128, r0:r0 + 128], writes=[B_mixT], key=f"p4_mb{sl}")

        p4_load(tiles[0], 0)
        for it, tile in enumerate(tiles):
            sl = it % 2
            if it + 1 < len(tiles):
                p4_load(tiles[it + 1], (it + 1) % 2)
            r = 1 if tile < 2 else 0
            r0 = tile * 128
            xa, B_xa = xas[sl]
            ma, B_ma = mas[sl]
            mc, B_mc = mcs[sl]
            mixT, B_mixT = mixTs[sl]
            h_, B_h = hs[sl]
            st, B_st = sts[sl]
            if FUSE_FIN:
                mb_, B_mb = mbs[sl]
                zz, B_zz = zzs[sl]
                sq4, B_sq4 = sqs4[sl]
                s8, B_s8 = st8s[sl]
                OP("dve", lambda e, ma=ma, mb_=mb_: e.tensor_tensor(out=ma, in0=ma, in1=mb_[:, 0:512], op=ALU.add), [B_ma, B_mb], [B_ma])
                OP("dve", lambda e, mc=mc, mb_=mb_: e.tensor_tensor(out=mc, in0=mc, in1=mb_[:, 512:768], op=ALU.add), [B_mc, B_mb], [B_mc])
                OP("act", lambda e, ma=ma, sq4=sq4: e.activation(out=sq4[:, 0:512], in_=ma, func=AF.Square), [B_ma], [B_sq4])
                OP("act", lambda e, mc=mc, sq4=sq4: e.activation(out=sq4[:, 512:768], in_=mc, func=AF.Square), [B_mc], [B_sq4])
                OP("dve", lambda e, sq4=sq4, s8=s8: e.reduce_sum(out=s8[:, 0:4], in_=sq4[:, 0:512].rearrange("p (h v) -> p h v", h=4), axis=AX.X), [B_sq4], [B_s8])
                OP("dve", lambda e, sq4=sq4, s8=s8: e.reduce_sum(out=s8[:, 4:8], in_=sq4[:, 512:768].rearrange("p (h v) -> p h v", h=4), axis=AX.X), [B_sq4], [B_s8])
                OP("act", lambda e, s8=s8: e.activation(out=s8[:, 0:4], in_=s8[:, 0:4], func=AF.Sqrt, bias=1e-6, scale=1.0 / 128), [B_s8], [B_s8])
                OP("act", lambda e, s8=s8: e.activation(out=s8[:, 4:8], in_=s8[:, 4:8], func=AF.Sqrt, bias=1e-6, scale=1.0 / 64), [B_s8], [B_s8])
                OP("dve", lambda e, s8=s8: e.reciprocal(out=s8[:, 0:8], in_=s8[:, 0:8]), [B_s8], [B_s8])
                OP("dve", lambda e, ma=ma, s8=s8: e.tensor_tensor(out=ma.rearrange("p (h v) -> p h v", h=4), in0=ma.rearrange("p (h v) -> p h v", h=4),
                                                               in1=s8[:, 0:4].unsqueeze(2).to_broadcast([128, 4, 128]), op=ALU.mult), [B_ma, B_s8], [B_ma])
                OP("dve", lambda e, mc=mc, s8=s8: e.tensor_tensor(out=mc.rearrange("p (h v) -> p h v", h=4), in0=mc.rearrange("p (h v) -> p h v", h=4),
                                                               in1=s8[:, 4:8].unsqueeze(2).to_broadcast([128, 4, 64]), op=ALU.mult), [B_mc, B_s8], [B_mc])
                OP("act", lambda e, zz=zz: e.activation(out=zz, in_=zz, func=AF.Silu), [B_zz], [B_zz])
                OP("pool", lambda e, zz=zz: e.tensor_tensor(out=zz, in0=zz, in1=gno, op=ALU.mult), [B_zz, B_gno], [B_zz])
                OP("dve", lambda e, ma=ma, zz=zz: e.tensor_tensor(out=ma, in0=ma, in1=zz[:, 0:512], op=ALU.mult), [B_ma, B_zz], [B_ma])
                OP("dve", lambda e, mc=mc, zz=zz: e.tensor_tensor(out=mc, in0=mc, in1=zz[:, 512:768], op=ALU.mult), [B_mc, B_zz], [B_mc])
            def trs(e, ma=ma, mc=mc):
                ins = None
                for k in range(4):
                    ins = e.transpose(banks[0][:, k * 128:(k + 1) * 128], ma[:, k * 128:(k + 1) * 128], ident)
                for k in range(2):
                    ins = e.transpose(banks[1][:, k * 128:(k + 1) * 128], mc[:, k * 128:(k + 1) * 128], ident)
                return ins
            OP("pe", trs, [B_ma, B_mc, B_const], [PSB[0], PSB[1]])
            OP("act", lambda e, mixT=mixT: e.copy(out=mixT[:, 0:512], in_=banks[0][:, :]), [PSB[0]], [B_mixT])
            OP("act", lambda e, mixT=mixT: e.copy(out=mixT[:, 768:1024], in_=banks[1][:, 0:256]), [PSB[1]], [B_mixT])
            for half in range(2):
                bank = 2 + half
                def mmy(e, mixT=mixT, half=half, bank=bank):
                    ins = None
                    for k in range(8):
                        ins = e.matmul(banks[bank][:, :], lhsT=mixT[:, k * 128:(k + 1) * 128],
                                       rhs=wo[:, k * D + half * 512:k * D + (half + 1) * 512], start=(k == 0), stop=(k == 7))
                    return ins
                OP("pe", mmy, [B_mixT, B_wo], [PSB[bank]])
                OP("dve", lambda e, h_=h_, half=half, bank=bank, r=r: e.tensor_tensor(
                    out=h_[:, half * 512:(half + 1) * 512], in0=banks[bank][:, :],
                    in1=modb[r][:, 2 * D + half * 512:2 * D + (half + 1) * 512], op=ALU.mult), [PSB[bank], B_modb], [B_h])
            OP("dve", lambda e, h_=h_, xa=xa: e.scalar_tensor_tensor(out=h_, in0=xa, scalar=DN_ALPHA, in1=h_, op0=ALU.mult, op1=ALU.add),
               [B_xa, B_h], [B_h])
            OP("dve", lambda e, h_=h_, st=st: e.bn_stats(out=st[:, 0:6], in_=h_[:, 0:512]), [B_h], [B_st])
            OP("dve", lambda e, h_=h_, st=st: e.bn_stats(out=st[:, 6:12], in_=h_[:, 512:1024]), [B_h], [B_st])
            OP("dve", lambda e, st=st: e.bn_aggr(out=st[:, 12:14], in_=st[:, 0:12]), [B_st], [B_st])
            OP("act", lambda e, st=st: e.activation(out=st[:, 14:15], in_=st[:, 13:14], func=AF.Sqrt, bias=LN_EPS, scale=1.0), [B_st], [B_st])
            OP("dve", lambda e, st=st: e.reciprocal(out=st[:, 14:15], in_=st[:, 14:15]), [B_st], [B_st])
            OP("dve", lambda e, h_=h_, st=st: e.tensor_scalar(out=h_, in0=h_, scalar1=st[:, 12:13], scalar2=st[:, 14:15],
                                                              op0=ALU.subtract, op1=ALU.mult), [B_h, B_st], [B_h])
            OP("pool", lambda e, h_=h_: e.tensor_tensor(out=h_, in0=h_, in1=lng, op=ALU.mult), [B_h, B_lng], [B_h])
            OP("pool", lambda e, h_=h_: e.tensor_tensor(out=h_, in0=h_, in1=lnb, op=ALU.add), [B_h, B_lnb], [B_h])
            if last:
                P.dma("pool", y_out[r0 - 256:r0 - 128, :], h_, reads=[B_h], writes=[Buf("yo")], key=f"p4_h{sl}")
            else:
                P.dma("pool", XC[r0:r0 + 128, :], h_, reads=[B_h], writes=[Buf("xc")], key=f"p4_h{sl}")

    def phase_P1(layer):
            P.barrier()
            A.release(persist_mark)
            cT = A.alloc(16, "cT")
            sc = A.alloc(16, "sc")
            bm = A.alloc(3 * D, "bm")
            modrow = A.alloc(3 * D, "modrow")
            wm = [A.alloc(8 * 512, f"wm{i}") for i in range(2)]
            B_c, B_sc, B_bm, B_modrow = Buf("cT"), Buf("sc"), Buf("bm"), Buf("modrow")
            B_wm = [Buf("wm0"), Buf("wm1")]
            P.dma("sp", cT, cvT, writes=[B_c], key="cT")
            P.dma("sp", bm[0:2, :], b_mod2[layer], writes=[B_bm], key="bm")
            P.op("act", lambda e: e.activation(out=sc, in_=cT, func=AF.Silu), reads=[B_c], writes=[B_sc])
            wmv = w_mod[layer].rearrange("(k p) c -> p k c", p=128)
            for ch in range(6):
                slot = ch % 2
                P.dma("sp", wm[slot].rearrange("p (k c) -> p k c", k=8), wmv[:, :, ch * 512:(ch + 1) * 512],
                      writes=[B_wm[slot]], key=f"wm{slot}")
                bank = ch % 2

                def mm(e, slot=slot, bank=bank):
                    ins = None
                    for k in range(8):
                        ins = e.matmul(banks[bank][0:2, :], lhsT=sc[:, 2 * k:2 * k + 2],
                                       rhs=wm[slot][:, k * 512:(k + 1) * 512], start=(k == 0), stop=(k == 7))
                    return ins
                P.op("pe", mm, reads=[B_sc, B_wm[slot]], writes=[PSB[bank]])
                P.op("dve", lambda e, ch=ch, bank=bank: e.tensor_tensor(
                    out=modrow[0:2, ch * 512:(ch + 1) * 512], in0=banks[bank][0:2, :],
                    in1=bm[0:2, ch * 512:(ch + 1) * 512], op=ALU.add),
                    reads=[PSB[bank], B_bm], writes=[B_modrow])
            P.op("dve", lambda e: e.tensor_scalar_add(out=modrow[0:2, D:2 * D], in0=modrow[0:2, D:2 * D], scalar1=1.0),
                 reads=[B_modrow], writes=[B_modrow])
            for r in range(2):
                for ch in range(6):
                    bank = 2 + (ch % 2)
                    P.op("pe", lambda e, r=r, ch=ch, bank=bank: e.matmul(
                        banks[bank][:, :], lhsT=sel[0:2, r * 128:(r + 1) * 128],
                        rhs=modrow[0:2, ch * 512:(ch + 1) * 512], start=True, stop=True),
                        reads=[B_modrow, B_const], writes=[PSB[bank]])
                    P.op("act", lambda e, r=r, ch=ch, bank=bank: e.copy(
                        out=modb[r][:, ch * 512:(ch + 1) * 512], in_=banks[bank][:, :]),
                        reads=[PSB[bank]], writes=[B_modb])
    def phase_P2(layer):
            P.barrier()
            A.release(persist_mark)
            wtm = A.alloc(8 * NTM, "wtm")
            wfm = A.alloc(8 * NFM, "wfm")
            B_wtm, B_wfm = Buf("wtm"), Buf("wfm")
            wtm_v = w_tm[layer].rearrange("(k p) c -> p k c", p=128)
            wfm_v = w_fm[layer].rearrange("(k p) c -> p k c", p=128)
            for k in range(8):
                P.dma("sp", wtm[:, k * NTM:(k + 1) * NTM], wtm_v[:, k, :], writes=[B_wtm], key="wtm")
                P.dma("pool", wfm[:, k * NFM:(k + 1) * NFM], wfm_v[:, k, :], writes=[B_wfm], key="wfm")
                if FAST["p2"]:
                    P.op("dve", lambda e, k=k: e.tensor_copy(out=fr(wtm[:, k * NTM:(k + 1) * NTM]), in_=wtm[:, k * NTM:(k + 1) * NTM]),
                         reads=[B_wtm], writes=[B_wtm])
                    P.op("act", lambda e, k=k: e.copy(out=fr(wfm[:, k * NFM:(k + 1) * NFM]), in_=wfm[:, k * NFM:(k + 1) * NFM]),
                         reads=[B_wfm], writes=[B_wfm])
            NXS = 2
            xt = [A.alloc(D, f"xt{i}") for i in range(NXS)]
            B_xt = [Buf(f"xt{i}") for i in range(NXS)]
            stats = A.alloc(16, "stats")
            B_stats = Buf("stats")
            mlT = [A.alloc(8 * 512, f"mlT{i}") for i in range(2)]
            B_mlT = [Buf("mlT0"), Buf("mlT1")]
            tmst = [A.alloc(NTM, f"tmst{i}") for i in range(2)]
            B_tmst = [Buf("tmst0"), Buf("tmst1")]
            fmst = [A.alloc(512, f"fmst{i}") for i in range(2)]
            B_fmst = [Buf(f"fmst{i}") for i in range(2)]
            xsrc = x_all if layer == 0 else XC
            groups = [list(range(g * 4, min(g * 4 + 4, NT))) for g in range((NT + 3) // 4)]
            fm_i = 0
            for gi, tiles in enumerate(groups):
                ms = gi % 2
                ntok = len(tiles) * 128
                for ti, tile in enumerate(tiles):
                    xs = tile % NXS
                    r = 1 if tile < 2 else 0
                    xa = xt[xs]
                    P.dma("sp", xa, xsrc[tile * 128:(tile + 1) * 128, :], writes=[B_xt[xs]], key=f"xt{xs}")
                    P.op("dve", lambda e, xa=xa: e.bn_stats(out=stats[:, 0:6], in_=xa[:, 0:512]), reads=[B_xt[xs]], writes=[B_stats])
                    P.op("dve", lambda e, xa=xa: e.bn_stats(out=stats[:, 6:12], in_=xa[:, 512:1024]), reads=[B_xt[xs]], writes=[B_stats])
                    P.op("dve", lambda e: e.bn_aggr(out=stats[:, 12:14], in_=stats[:, 0:12]), reads=[B_stats], writes=[B_stats])
                    P.op("act", lambda e: e.activation(out=stats[:, 14:15], in_=stats[:, 13:14], func=AF.Sqrt, bias=LN_EPS, scale=1.0),
                         reads=[B_stats], writes=[B_stats])
                    P.op("dve", lambda e: e.reciprocal(out=stats[:, 14:15], in_=stats[:, 14:15]), reads=[B_stats], writes=[B_stats])
                    P.op("dve", lambda e, xa=xa: e.tensor_scalar(out=xa, in0=xa, scalar1=stats[:, 12:13], scalar2=stats[:, 14:15],
                                                                 op0=ALU.subtract, op1=ALU.mult), reads=[B_xt[xs], B_stats], writes=[B_xt[xs]])
                    P.op("pool", lambda e, xa=xa, r=r: e.tensor_tensor(out=xa, in0=xa, in1=modb[r][:, D:2 * D], op=ALU.mult),
                         reads=[B_xt[xs], B_modb], writes=[B_xt[xs]])
                    P.op("pool", lambda e, xa=xa, r=r: e.tensor_tensor(out=xa, in0=xa, in1=modb[r][:, 0:D], op=ALU.add),
                         reads=[B_xt[xs], B_modb], writes=[B_xt[xs]])
                    for half in range(2):
                        bank = half

                        def tr(e, xa=xa, half=half, bank=bank):
                            ins = None
                            for kk in range(4):
                                k = half * 4 + kk
                                ins = e.transpose(banks[bank][:, kk * 128:(kk + 1) * 128], xa[:, k * 128:(k + 1) * 128], ident)
                            return ins
                        P.op("pe", tr, reads=[B_xt[xs], B_const], writes=[PSB[bank]])
                        dst = mlT[ms].rearrange("p (k t) -> p k t", k=8)[:, half * 4:half * 4 + 4, ti * 128:(ti + 1) * 128]
                        P.op("act", lambda e, dst=dst, bank=bank: e.copy(
                            out=fr(dst, FAST["p2"]), in_=banks[bank][:, :].rearrange("p (k t) -> p k t", k=4)),
                            reads=[PSB[bank]], writes=[B_mlT[ms]])
                for ti, tile in enumerate(tiles):
                    ts_ = tile % 2
                    for ci, (c0, cw) in enumerate(((0, 512), (512, 512), (1024, 512), (1536, 16))):
                        bank = 2 + ci % 2

                        def mm(e, ti=ti, c0=c0, cw=cw, bank=bank, ms=ms):
                            ins = None
                            for k in range(8):
                                ins = e.matmul(banks[bank][:, 0:cw], lhsT=fr(mlT[ms][:, k * 512 + ti * 128:k * 512 + (ti + 1) * 128], FAST["p2"] and cw >= 256),
                                               rhs=fr(wtm[:, k * NTM + c0:k * NTM + c0 + cw], FAST["p2"] and cw >= 256), start=(k == 0), stop=(k == 7))
                            return ins
                        P.op("pe", mm, reads=[B_mlT[ms], B_wtm], writes=[PSB[bank]])
                        P.op("dve", lambda e, c0=c0, cw=cw, bank=bank, ts_=ts_: e.tensor_copy(
                            out=tmst[ts_][:, c0:c0 + cw], in_=banks[bank][:, 0:cw]), reads=[PSB[bank]], writes=[B_tmst[ts_]])
                    P.dma("act", TM[tile * 128:(tile + 1) * 128, :], tmst[ts_], reads=[B_tmst[ts_]], writes=[Buf("tmd")], key=f"tmst{ts_}")
                t0 = tiles[0] * 128
                for ft in range(19):
                    f0 = ft * 128
                    fw = 128 if ft < 18 else 32
                    bank = 4 + ft % 4
                    fs = fm_i % 2
                    fm_i += 1

                    def mm(e, f0=f0, fw=fw, bank=bank, ms=ms, ntok=ntok):
                        ins = None
                        for k in range(8):
                            ins = e.matmul(banks[bank][0:fw, 0:ntok], lhsT=fr(wfm[:, k * NFM + f0:k * NFM + f0 + fw], FAST["p2"]),
                                           rhs=fr(mlT[ms][:, k * 512:k * 512 + ntok], FAST["p2"]), start=(k == 0), stop=(k == 7))
                        return ins
                    P.op("pe", mm, reads=[B_mlT[ms], B_wfm], writes=[PSB[bank]])
                    eng = "act" if ft % 2 == 0 else "dve"
                    if eng == "act":
                        P.op("act", lambda e, fw=fw, bank=bank, fs=fs, ntok=ntok: e.copy(
                            out=fmst[fs][0:fw, 0:ntok], in_=banks[bank][0:fw, 0:ntok]), reads=[PSB[bank]], writes=[B_fmst[fs]])
                    else:
                        P.op("dve", lambda e, fw=fw, bank=bank, fs=fs, ntok=ntok: e.tensor_copy(
                            out=fmst[fs][0:fw, 0:ntok], in_=banks[bank][0:fw, 0:ntok]), reads=[PSB[bank]], writes=[B_fmst[fs]])
                    P.dma("pool", FM[f0:f0 + fw, t0:t0 + ntok], fmst[fs][0:fw, 0:ntok], reads=[B_fmst[fs]],
                          writes=[Buf("fmd")], key=f"fmst{fs}")

    PH = dict(P1=phase_P1, P2=phase_P2, A=phase_A, B=phase_B, C=phase_C, P4=phase_P4)
    for layer in layers:
        for ph in phases:
            PH[ph](layer)


    P.barrier()
    P.emit()
    return nc


_CONST = {}


def _consts():
    if not _CONST:
        _CONST["ident"] = np.eye(128, dtype=np.float32)
        sel = np.zeros((2, 256), np.float32)
        sel[0, 0:128] = 1.0
        sel[1, 128:256] = 1.0
        _CONST["sel"] = sel
        _CONST["ones"] = np.ones((128, 128), np.float32)
        jj, ii = np.meshgrid(np.arange(128), np.arange(128), indexing="ij")
        _CONST["tril"] = (jj <= ii).astype(np.float32)
        _CONST["triu"] = (jj >= ii).astype(np.float32)
        hm = np.zeros((128, 8), np.float32)
        hm[np.arange(128), np.arange(128) // 32] = 1.0
        _CONST["hm"] = hm
        _CONST["bd"] = (np.arange(128)[:, None] // 32 == np.arange(256)[None, :] // 64).astype(np.float32)
        t = np.arange(SEQ)
        inv = (10000.0 ** (-np.arange(8, dtype=np.float32) / 8)).astype(np.float32)
        cosf = np.zeros((SEQ, 4, 2, 2, 8), np.float32)
        sinf = np.zeros((SEQ, 4, 2, 2, 8), np.float32)
        for half, pos in enumerate((t // 64, t % 64)):
            ang = pos.astype(np.float32)[:, None] * inv[None, :]
            c_, s_ = np.cos(ang).astype(np.float32), np.sin(ang).astype(np.float32)
            cosf[:, :, half, 0, :] = c_[:, None, :]
            cosf[:, :, half, 1, :] = c_[:, None, :]
            sinf[:, :, half, 0, :] = -s_[:, None, :]
            sinf[:, :, half, 1, :] = s_[:, None, :]
        _CONST["ropec"] = cosf.reshape(SEQ, 128)
        pj, fi = jj, ii
        offu = []
        for sz_ in (1, 2, 4, 8, 16, 32, 64):
            offu.append(((pj // (2 * sz_) == fi // (2 * sz_)) & (pj % (2 * sz_) < sz_) & (fi % (2 * sz_) >= sz_)).astype(np.float32))
        offl = [m_.T for m_ in offu]
        _CONST["offs"] = np.ascontiguousarray(np.concatenate(offu + offl, 1))
        negf = np.where(pj <= fi, 0.0, -30000.0).astype(np.float32)
        negb = np.where(pj >= fi, 0.0, -30000.0).astype(np.float32)
        _CONST["negm"] = np.ascontiguousarray(np.concatenate([np.tile(negf, (1, 4)), np.tile(negb, (1, 4))], 1))
        _CONST["offd"] = (pj != fi).astype(np.float32)
        _CONST["ropes"] = sinf.reshape(SEQ, 128)
    return _CONST


def make_in_maps(inputs):
    x, c, ctx, c_ctx = (np.asarray(inputs[k], np.float32) for k in ("x", "c", "ctx", "c_ctx"))
    w_in = np.asarray(inputs["w_in"], np.float32)
    cs = _consts()
    sl = lambda a, b: list(range(a, b))
    tm_cols = sl(1552, 2064) + sl(2576, 2832) + sl(3088, 3216) + sl(3216, 3344) + sl(3344, 3600) + sl(3632, 3888) + sl(1536, 1552)
    fm_cols = sl(0, 1536) + sl(2064, 2320) + sl(2320, 2576) + sl(2832, 3088) + sl(3600, 3632)
    assert len(tm_cols) == NTM and len(fm_cols) == NFM
    w_tm = np.ascontiguousarray(w_in[:, :, tm_cols])
    w_fm = np.ascontiguousarray(w_in[:, :, fm_cols])
    b_mod = np.asarray(inputs["b_mod"], np.float32)
    b_mod2 = np.ascontiguousarray(np.broadcast_to(b_mod[:, None, :], (DEPTH, 2, 3 * D)))
    f = lambda k: np.asarray(inputs[k], np.float32)
    shared = dict(w_mod=f("w_mod"), b_mod2=b_mod2, w_tm=w_tm, w_fm=w_fm, w_out=f("w_out"))
    for k in ("ident", "sel", "ones", "tril", "triu", "hm", "bd", "ropec", "ropes"):
        shared[k] = cs[k]
    w2 = f("gla_w2")
    w2p = np.zeros((DEPTH, 32, 256), np.float32)
    w2p[:, 0:16, 0:128] = w2[:, 0]
    w2p[:, 16:32, 128:256] = w2[:, 1]
    shared["w2p"] = w2p
    shared["b2r"] = np.ascontiguousarray(f("gla_b2").reshape(DEPTH, 1, 256))
    shared["glan"] = np.ascontiguousarray(np.broadcast_to(np.tile(f("gla_norm"), (1, 4))[:, None, :], (DEPTH, 128, 256)))
    shared["gdnn"] = np.ascontiguousarray(np.broadcast_to(np.tile(f("gdn_norm"), (1, 4))[:, None, :], (DEPTH, 128, 512)))
    shared["lng"] = np.ascontiguousarray(np.broadcast_to(f("ln_g")[:, None, :], (DEPTH, 128, D)))
    shared["lnb"] = np.ascontiguousarray(np.broadcast_to(f("ln_b")[:, None, :], (DEPTH, 128, D)))
    rpb = f("rpb")
    kc = np.arange(64)[:, None]
    qc = np.arange(64)[None, :]
    c0_ = np.clip(qc - 8, 0, 48)
    ok = (kc >= c0_) & (kc < c0_ + 16)
    dj = np.clip(kc - qc + 15, 0, 30)
    nab = np.full((DEPTH, 64, 4, 15, 64), -30000.0, np.float32)
    for e_ in range(15):
        blk = rpb[:, :, 14 - e_, :][:, :, dj]
        blk = np.where(ok[None, None], blk, np.float32(-30000.0))
        nab[:, :, :, e_, :] = blk.transpose(0, 2, 1, 3)
    nab = nab.reshape(DEPTH, 64, 3840)
    shared["nab"] = np.ascontiguousarray(np.concatenate([nab, nab], 1))
    cwv = f("conv_w")
    shared["convw"] = np.ascontiguousarray(cwv.reshape(DEPTH, 5, 12, 128).transpose(0, 3, 2, 1).reshape(DEPTH, 128, 60))
    shared["alog8"] = np.ascontiguousarray(np.broadcast_to(f("a_log").reshape(DEPTH, 1, 8), (DEPTH, 128, 8)))
    shared["dtb8"] = np.ascontiguousarray(np.broadcast_to(f("dt_bias").reshape(DEPTH, 1, 8), (DEPTH, 128, 8)))
    for k in ("offs", "negm", "offd"):
        shared[k] = cs[k]
    maps = []
    for b in range(8):
        cv = np.stack([c[b], c_ctx], 0)
        cvT = np.ascontiguousarray(cv.reshape(2, 8, 128).transpose(2, 1, 0).reshape(128, 16))
        m = dict(shared)
        m["x_all"] = np.ascontiguousarray(np.concatenate([ctx[b], x[b]], 0))
        m["cvT"] = cvT
        maps.append(m)
    return maps


_NC = {}


def kernel(**inputs):
    if "nc" not in _NC:
        _NC["nc"] = build_program()
    maps = make_in_maps(inputs)
    res = run_bass_kernel_spmd(_NC["nc"], maps, core_ids=list(range(8)))
    return np.stack([np.asarray(r["y"], np.float32) for r in res.results], 0)
```

```python
import numpy as np
from contextlib import ExitStack
import concourse.bass as bass
import concourse.mybir as mybir
from concourse.bass_utils import run_bass_kernel_spmd

F32 = mybir.dt.float32
AF = mybir.ActivationFunctionType
ALU = mybir.AluOpType
AX = mybir.AxisListType

D = 1024
SEQ = 4096
CTX = 256
T = SEQ + CTX
NT = T // 128
DEPTH = 2
NTM = 1552
NFM = 2336
DN_ALPHA = (2 * DEPTH) ** 0.25
LN_EPS = 1e-6
SEM_ROT = 30000
DBG = {}
MERGE_C = True
CONV_ON_DVE = True
CONV_PE_EVERY = 3
SWDGE_TO = "act"
FUSE_FIN = True
F32R = mybir.dt.float32r
FAST = dict(p2=False, p4=False, b=False, a0=False)


def fr(ap, on=True):
    return ap.bitcast(F32R) if on else ap

TM_AZ, TM_BV, TM_CQ, TM_CK, TM_CV, TM_CZ, TM_AB = 0, 512, 768, 896, 1024, 1280, 1536
FM_AQKV, FM_BQ, FM_BK, FM_BZ, FM_CR = 0, 1536, 1792, 2048, 2304


class Buf:
    __slots__ = ("name", "w", "r")

    def __init__(self, name):
        self.name = name
        self.w = None
        self.r = {}


class Prog:
    ENGS = ("pe", "act", "dve", "pool", "sp")

    def __init__(self, nc):
        self.nc = nc
        self.streams = {e: [] for e in self.ENGS}
        self.sems = []
        self.cur = {}
        self.cnt = {}
        self.waited = {e: {} for e in self.ENGS}
        self.dma_sem = {}
        self.dma_cnt = {}
        self.eng_of_sem = {}
        self.rec = None
        for e in self.ENGS:
            self._new_eng_sem(e)

    def record(self):
        assert self.rec is None
        self.rec = []

    def stop(self):
        r, self.rec = self.rec, None
        return r

    def play(self, lists, lead=None):
        idx = [0] * len(lists)
        lead = lead or [0] * len(lists)
        for i, l in enumerate(lists):
            for _ in range(min(lead[i], len(l))):
                l[idx[i]]()
                idx[i] += 1
        total = sum(len(l) - idx[i] for i, l in enumerate(lists))
        for _ in range(total):
            best, bf = None, None
            for i, l in enumerate(lists):
                if idx[i] < len(l):
                    fr = (idx[i] - lead[i]) / max(1, len(l) - lead[i])
                    if bf is None or fr < bf:
                        best, bf = i, fr
            lists[best][idx[best]]()
            idx[best] += 1

    def _new_sem(self, name):
        s = self.nc.alloc_semaphore(name)
        self.sems.append(s)
        return len(self.sems) - 1

    def _new_eng_sem(self, e):
        i = self._new_sem(f"s_{e}_{len(self.sems)}")
        self.cur[e] = i
        self.cnt[e] = 0
        self.eng_of_sem[i] = e

    def _deps(self, engine, reads, writes):
        deps = {}

        def add(tok, raw):
            if tok is None:
                return
            s, v = tok
            if (not raw) and self.eng_of_sem.get(s) == engine:
                return
            if deps.get(s, 0) < v:
                deps[s] = v

        for b in reads:
            add(b.w, True)
        for b in writes:
            add(b.w, False)
            for s, v in b.r.items():
                add((s, v), False)
        w = self.waited[engine]
        out = []
        for s, v in deps.items():
            if w.get(s, 0) < v:
                w[s] = v
                out.append((s, v))
        return out

    def _commit(self, tok, reads, writes):
        s, v = tok
        for b in reads:
            if b.r.get(s, 0) < v:
                b.r[s] = v
        for b in writes:
            b.w = tok
            b.r = {}

    def op(self, engine, fn, reads=(), writes=()):
        if self.rec is not None:
            self.rec.append(lambda: self.op(engine, fn, reads, writes))
            return
        waits = self._deps(engine, reads, writes)
        if self.cnt[engine] >= SEM_ROT:
            self._new_eng_sem(engine)
        self.cnt[engine] += 1
        tok = (self.cur[engine], self.cnt[engine])
        self._commit(tok, reads, writes)
        self.streams[engine].append((waits, fn, tok, 1))

    def dma(self, queue, out_ap, in_ap, reads=(), writes=(), key=None, **kw):
        if queue == "pool":
            queue = SWDGE_TO
        if self.rec is not None:
            self.rec.append(lambda: self.dma(queue, out_ap, in_ap, reads, writes, key, **kw))
            return
        waits = self._deps(queue, reads, writes)
        if key not in self.dma_sem or self.dma_cnt[key] >= SEM_ROT * 16:
            self.dma_sem[key] = self._new_sem(f"d_{len(self.sems)}")
            self.dma_cnt[key] = 0
        self.dma_cnt[key] += 16
        tok = (self.dma_sem[key], self.dma_cnt[key])
        self._commit(tok, reads, writes)
        self.streams[queue].append((waits, lambda e: e.dma_start(out=out_ap, in_=in_ap, **kw), tok, 16))

    def barrier(self):
        toks = [(self.cur[e], self.cnt[e]) for e in self.ENGS if self.cnt[e] > 0]
        toks += [(self.dma_sem[k], self.dma_cnt[k]) for k in self.dma_sem]
        for e in self.ENGS:
            w = self.waited[e]
            waits = []
            for s, v in toks:
                if self.eng_of_sem.get(s) == e:
                    continue
                if w.get(s, 0) < v:
                    w[s] = v
                    waits.append((s, v))
            if waits:
                self.streams[e].append((waits, None, None, 0))

    def final_wait(self, engine, bufs):
        waits = self._deps(engine, bufs, ())
        self.streams[engine].append((waits, None, None, 0))

    def emit(self):
        nc = self.nc
        with nc.Block() as block:
            def run(name):
                def body(eng):
                    for waits, fn, tok, inc in self.streams[name]:
                        for s, v in waits:
                            eng.wait_ge(self.sems[s], v)
                        if fn is not None:
                            ins = fn(eng)
                            ins.then_inc(self.sems[tok[0]], inc)
                return body
            block.tensor(run("pe"))
            block.scalar(run("act"))
            block.vector(run("dve"))
            block.gpsimd(run("pool"))
            block.sync(run("sp"))


class Arena:
    def __init__(self, sb, ncols):
        self.sb = sb
        self.ncols = ncols
        self.off = 0

    def alloc(self, n, name="t"):
        n = (n + 7) // 8 * 8
        assert self.off + n <= self.ncols, f"SBUF arena overflow at {name}: {self.off}+{n}>{self.ncols}"
        ap = self.sb[:, self.off:self.off + n]
        self.off += n
        return ap

    def mark(self):
        return self.off

    def release(self, m):
        self.off = m


def build_program(layers=(0, 1), phases=("P1", "P2", "A", "B", "P4"), dbg_in=(), dbg_out=()):
    nc = bass.Bass("TRN2", target_bir_lowering=False)
    P = Prog(nc)
    dbg = False

    def din(name, shape):
        return nc.dram_tensor(name, list(shape), F32, kind="ExternalInput").ap()

    def dscr(name, shape, out=False):
        kind = "ExternalOutput" if (out or name in dbg_out) else ("ExternalInput" if name in dbg_in else "Internal")
        return nc.dram_tensor(name, list(shape), F32, kind=kind).ap()

    x_all = din("x_all", (T, D))
    cvT = din("cvT", (128, 16))
    w_mod = din("w_mod", (DEPTH, D, 3 * D))
    b_mod2 = din("b_mod2", (DEPTH, 2, 3 * D))
    w_tm = din("w_tm", (DEPTH, D, NTM))
    w_fm = din("w_fm", (DEPTH, D, NFM))
    ident_d = din("ident", (128, 128))
    sel_d = din("sel", (2, 256))

    y_out = dscr("y", (SEQ, D), out=True)
    TM = dscr("TM", (T, NTM))
    FM = dscr("FM", (NFM, T))
    XC = dscr("XC", (T, D))
    MIXA = dscr("MIXA", (T, 512))
    MIXB = dscr("MIXB", (256, T))
    MIXC = dscr("MIXC", (T, 256))

    ones_d = din("ones", (128, 128))
    tril_d = din("tril", (128, 128))
    triu_d = din("triu", (128, 128))
    hm_d = din("hm", (128, 8))
    bd_d = din("bd", (128, 256))
    ropec_d = din("ropec", (SEQ, 128))
    ropes_d = din("ropes", (SEQ, 128))
    w2p_d = din("w2p", (DEPTH, 32, 256))
    b2r_d = din("b2r", (DEPTH, 1, 256))
    glan_d = din("glan", (DEPTH, 128, 256))
    gdnn_d = din("gdnn", (DEPTH, 128, 512))
    lng_d = din("lng", (DEPTH, 128, D))
    lnb_d = din("lnb", (DEPTH, 128, D))
    w_out = din("w_out", (DEPTH, D, D))
    nab_d = din("nab", (DEPTH, 128, 3840))
    convw_d = din("convw", (DEPTH, 128, 60))
    offs_d = din("offs", (128, 14 * 128))
    negm_d = din("negm", (128, 1024))
    offd_d = din("offd", (128, 128))
    alog_d = din("alog8", (DEPTH, 128, 8))
    dtb_d = din("dtb8", (DEPTH, 128, 8))
    QKV = dscr("QKV", (T, 1536))
    OAF = dscr("OAF", (T, 512))
    OAB = dscr("OAB", (T, 512))
    OCF = dscr("OCF", (T, 256))
    OCB = dscr("OCB", (T, 256))

    NCOLS = 53000
    sb_h = nc.alloc_sbuf_tensor("sb", [128, NCOLS], F32)
    A = Arena(sb_h, NCOLS)
    banks = [nc.alloc_psum_tensor(f"ps{i}", [128, 512], F32) for i in range(8)]
    PSB = [Buf(f"psum{i}") for i in range(8)]

    ident = A.alloc(128, "ident")
    sel = A.alloc(256, "sel")
    B_const = Buf("const")
    P.dma("sp", ident, ident_d, writes=[B_const], key="const")
    P.dma("sp", sel[0:2, :], sel_d, writes=[B_const], key="const")
    ones = A.alloc(128, "ones")
    tril = A.alloc(128, "tril")
    triu = A.alloc(128, "triu")
    hm = A.alloc(8, "hm")
    bd = A.alloc(256, "bd")
    P.dma("sp", ones, ones_d, writes=[B_const], key="const")
    P.dma("sp", tril, tril_d, writes=[B_const], key="const")
    P.dma("sp", triu, triu_d, writes=[B_const], key="const")
    P.dma("sp", hm, hm_d, writes=[B_const], key="const")
    P.dma("sp", bd, bd_d, writes=[B_const], key="const")
    modb = [A.alloc(3 * D, f"modb{r}") for r in range(2)]
    B_modb = Buf("modb")
    persist_mark = A.mark()

    out_bufs = []


    def mk(n, name, slots=1):
        return [(A.alloc(n, f"{name}{i}"), Buf(f"{name}{i}")) for i in range(slots)]

    def OP(eng, fn, r, w):
        P.op(eng, fn, reads=r, writes=w)

    def phase_A(layer):
        P.barrier()
        A.release(persist_mark)
        (cw, B_cw), = mk(64, "cw")
        P.dma("sp", cw[:, 0:60], convw_d[layer], writes=[B_cw], key="a_cw")
        dgs = mk(5 * 128, "dg", 2)
        xps = mk(4360, "xp", 2)
        yss = mk(512, "ys", 2)
        stgs = mk(512, "stg", 2)
        for xp, B_xp in xps:
            OP("pool", lambda e, xp=xp: e.memset(xp, 0.0), [], [B_xp])
        blocks = [(0, 256, 0)] + [(256 + 512 * b, 512, 260 + 512 * b) for b in range(8)]
        bi = 0
        pend = []
        for ct in range(12):
            sl = ct % 2
            xp, B_xp = xps[sl]
            dg, B_dg = dgs[sl]
            P.dma("sp", xp[:, 2:258], FM[ct * 128:(ct + 1) * 128, 0:256], writes=[B_xp], key=f"a_xp{sl}")
            P.dma("pool", xp[:, 262:4358], FM[ct * 128:(ct + 1) * 128, 256:T], writes=[B_xp], key=f"a_xp{sl}")
            for j in range(0 if (CONV_ON_DVE and not CONV_PE_EVERY) else 5):
                OP("dve", lambda e, dg=dg, j=j, ct=ct: e.tensor_scalar(out=dg[:, j * 128:(j + 1) * 128], in0=ident,
                                                                        scalar1=cw[:, ct * 5 + j:ct * 5 + j + 1], scalar2=None, op0=ALU.mult),
                   [B_const, B_cw], [B_dg])
            for (t0, ntok, cb) in blocks:
                bs = bi % 2
                bi += 1
                ys, B_ys = yss[bs]
                stg, B_stg = stgs[bs]
                if CONV_ON_DVE and not (CONV_PE_EVERY and bi % CONV_PE_EVERY == 0):
                    OP("dve", lambda e, ys=ys, xp=xp, cb=cb, ntok=ntok, ct=ct: e.tensor_scalar(
                        out=ys[:, 0:ntok], in0=xp[:, cb:cb + ntok], scalar1=cw[:, ct * 5:ct * 5 + 1], scalar2=None, op0=ALU.mult), [B_xp, B_cw], [B_ys])
                    for j in range(1, 5):
                        OP("dve", lambda e, ys=ys, xp=xp, cb=cb, ntok=ntok, ct=ct, j=j: e.scalar_tensor_tensor(
                            out=ys[:, 0:ntok], in0=xp[:, cb + j:cb + j + ntok], scalar=cw[:, ct * 5 + j:ct * 5 + j + 1], in1=ys[:, 0:ntok],
                            op0=ALU.mult, op1=ALU.add), [B_xp, B_cw, B_ys], [B_ys])
                    OP("act", lambda e, ys=ys, ntok=ntok: e.activation(out=ys[:, 0:ntok], in_=ys[:, 0:ntok], func=AF.Silu), [B_ys], [B_ys])
                else:
                    def mmc(e, dg=dg, xp=xp, cb=cb, ntok=ntok, bs=bs):
                        ins = None
                        for j in range(5):
                            ins = e.matmul(banks[bs][:, 0:ntok], lhsT=dg[:, j * 128:(j + 1) * 128], rhs=xp[:, cb + j:cb + j + ntok],
                                           start=(j == 0), stop=(j == 4))
                        return ins
                    OP("pe", mmc, [B_dg, B_xp], [PSB[bs]])
                    OP("act", lambda e, ys=ys, bs=bs, ntok=ntok: e.activation(out=ys[:, 0:ntok], in_=banks[bs][:, 0:ntok], func=AF.Silu), [PSB[bs]], [B_ys])

                def tail(ys=ys, B_ys=B_ys, stg=stg, B_stg=B_stg, bs=bs, ntok=ntok, t0=t0, ct=ct):
                    def trc(e):
                        ins = None
                        for b in range(ntok // 128):
                            ins = e.transpose(banks[2 + bs][:, b * 128:(b + 1) * 128], ys[:, b * 128:(b + 1) * 128], ident)
                        return ins
                    OP("pe", trc, [B_ys, B_const], [PSB[2 + bs]])
                    OP("act", lambda e: e.copy(out=stg[:, 0:ntok], in_=banks[2 + bs][:, 0:ntok]), [PSB[2 + bs]], [B_stg])
                    P.dma("pool", QKV[t0:t0 + ntok, ct * 128:(ct + 1) * 128].rearrange("(b p) c -> p b c", p=128),
                          stg[:, 0:ntok].rearrange("p (b c) -> p b c", c=128), reads=[B_stg], writes=[Buf("qkvd")], key=f"a_stg{bs}")
                if pend:
                    pend.pop()()
                pend.append(tail)
        while pend:
            pend.pop()()

        P.barrier()
        A.release(persist_mark)
        (offs, B_offs), = mk(14 * 128, "offs")
        (negm, B_negm), = mk(2 * 512, "negm")
        (offd, B_offd), = mk(128, "offd")
        (al8, B_al8), = mk(8, "al8")
        (dt8, B_dt8), = mk(8, "dt8")
        P.dma("sp", offs, offs_d, writes=[B_offs], key="a_offs")
        P.dma("sp", negm, negm_d, writes=[B_negm], key="a_negm")
        P.dma("sp", offd, offd_d, writes=[B_offd], key="a_offd")
        P.dma("sp", al8, alog_d[layer], writes=[B_al8], key="a_al8")
        P.dma("sp", dt8, dtb_d[layer], writes=[B_dt8], key="a_dt8")
        OP("act", lambda e: e.activation(out=al8, in_=al8, func=AF.Exp), [B_al8], [B_al8])
        OP("dve", lambda e: e.tensor_scalar(out=al8, in0=al8, scalar1=-1.0, scalar2=None, op0=ALU.mult), [B_al8], [B_al8])
        TL = {}
        for nm, n in (("qkv", 1536), ("ab", 16), ("sq", 1024), ("st", 64), ("qkn", 1024), ("kqT", 1024), ("R", 512), ("DT", 512),
                      ("DTs", 512), ("N", 512), ("NT", 512), ("D", 512), ("Dt", 512), ("T1", 512), ("T2", 512),
                      ("qkT", 512), ("KG", 512), ("nW", 512), ("kdec", 512), ("vnew", 512), ("ot", 512), ("t2", 512)):
            TL[nm] = mk(n, "a" + nm, 4 if nm in ("qkv", "ab") else 2)
        TL["m1"], TL["m2"] = TL["T1"], TL["T2"]
        Ss = [mk(512, f"aS{d_}", 2) for d_ in range(2)]
        v3 = lambda ap: ap.rearrange("p (h c) -> p h c", h=4)
        hs_ = lambda ap: (lambda h: ap[:, h * 128:(h + 1) * 128])

        def a_load(tile, d, ql):
            r0 = tile * 128
            qkv, B_qkv = TL["qkv"][ql]
            ab, B_ab = TL["ab"][ql]
            P.dma("sp", qkv, QKV[r0:r0 + 128, :], writes=[B_qkv], key=f"a_qkv{ql}")
            P.dma("sp", ab, TM[r0:r0 + 128, TM_AB:TM_AB + 16], writes=[B_ab], key=f"a_ab{ql}")

        def a_chunk(tile, d, ql, S, B_S, Sn, B_Sn):
            r0 = tile * 128
            sl = d
            g = lambda nm: TL[nm][sl]
            qkv, B_qkv = TL["qkv"][ql]; ab, B_ab = TL["ab"][ql]; sq, B_sq = g("sq"); st, B_st = g("st")
            qkn, B_qkn = g("qkn"); kqT, B_kqT = g("kqT"); R_, B_R = g("R"); DT, B_DT = g("DT"); DTs, B_DTs = g("DTs")
            N_, B_N = g("N"); NT_, B_NT = g("NT"); Dm, B_Dm = g("D"); Dt, B_Dt = g("Dt"); T1, B_T1 = g("T1"); T2, B_T2 = g("T2")
            m1, B_m1 = g("m1"); m2, B_m2 = g("m2"); qkT, B_qkT = g("qkT"); KG, B_KG = g("KG"); nW, B_nW = g("nW")
            kdec, B_kdec = g("kdec"); vnew, B_vnew = g("vnew"); ot, B_ot = g("ot"); t2_, B_t2 = g("t2")
            bk = (lambda lb: 3 * sl + (0, 1, 2, 0, 1, 2, 0, 1, 2)[lb]) if MERGE_C else (lambda lb: 4 * sl + (0, 1, 2, 3, 0, 1, 2, 3, 0)[lb])
            BK = lambda lb: banks[bk(lb)]
            PB = lambda lb: PSB[bk(lb)]

            def mm4(lb, lf, rf, reads, start=True, stop=True):
                def f(e):
                    ins = None
                    for h in range(4):
                        ins = e.matmul(BK(lb)[:, h * 128:(h + 1) * 128], lhsT=lf(h), rhs=rf(h), start=start, stop=stop)
                    return ins
                OP("pe", f, reads, [PB(lb)])

            TR = tril if d == 0 else triu
            offN = (lambda li: offs[:, li * 128:(li + 1) * 128]) if d == 0 else (lambda li: offs[:, (7 + li) * 128:(8 + li) * 128])
            offT = (lambda li: offs[:, (7 + li) * 128:(8 + li) * 128]) if d == 0 else (lambda li: offs[:, li * 128:(li + 1) * 128])
            ngm = negm[:, d * 512:(d + 1) * 512]
            b4 = lambda ap: ap.unsqueeze(1).to_broadcast([128, 4, 128])
            c4 = lambda ap: ap.unsqueeze(2).to_broadcast([128, 4, 128])
            OP("act", lambda e: e.activation(out=sq, in_=qkv[:, 0:1024], func=AF.Square), [B_qkv], [B_sq])
            OP("dve", lambda e: e.reduce_sum(out=st[:, 0:8], in_=sq.rearrange("p (g c) -> p g c", g=8), axis=AX.X), [B_sq], [B_st])
            OP("act", lambda e: e.activation(out=st[:, 0:8], in_=st[:, 0:8], func=AF.Sqrt, bias=1e-6, scale=1.0), [B_st], [B_st])
            OP("dve", lambda e: e.reciprocal(out=st[:, 0:8], in_=st[:, 0:8]), [B_st], [B_st])
            OP("dve", lambda e: e.tensor_scalar(out=st[:, 0:4], in0=st[:, 0:4], scalar1=128 ** -0.5, scalar2=None, op0=ALU.mult), [B_st], [B_st])
            OP("dve", lambda e: e.tensor_tensor(out=qkn.rearrange("p (g c) -> p g c", g=8), in0=qkv[:, 0:1024].rearrange("p (g c) -> p g c", g=8),
                                                in1=st[:, 0:8].unsqueeze(2).to_broadcast([128, 8, 128]), op=ALU.mult), [B_qkv, B_st], [B_qkn])
            qn, kn, v_ = qkn[:, 0:512], qkn[:, 512:1024], qkv[:, 1024:1536]
            OP("dve", lambda e: e.tensor_tensor(out=st[:, 8:12], in0=ab[:, d * 4:d * 4 + 4], in1=dt8[:, d * 4:d * 4 + 4], op=ALU.add), [B_ab, B_dt8], [B_st])
            OP("act", lambda e: e.activation(out=st[:, 8:12], in_=st[:, 8:12], func=AF.Exp), [B_st], [B_st])
            OP("act", lambda e: e.activation(out=st[:, 8:12], in_=st[:, 8:12], func=AF.Ln, bias=1.0, scale=1.0), [B_st], [B_st])
            OP("dve", lambda e: e.tensor_tensor(out=st[:, 8:12], in0=st[:, 8:12], in1=al8[:, d * 4:d * 4 + 4], op=ALU.mult), [B_st, B_al8], [B_st])
            OP("act", lambda e: e.activation(out=st[:, 12:16], in_=ab[:, 8 + d * 4:12 + d * 4], func=AF.Exp, scale=-1.0), [B_ab], [B_st])
            OP("dve", lambda e: e.tensor_scalar_add(out=st[:, 12:16], in0=st[:, 12:16], scalar1=1.0), [B_st], [B_st])
            OP("dve", lambda e: e.reciprocal(out=st[:, 12:16], in_=st[:, 12:16]), [B_st], [B_st])
            g_, beta = st[:, 8:12], st[:, 12:16]
            def mmg(e):
                e.matmul(BK(0)[:, 0:4], lhsT=TR, rhs=g_, start=True, stop=True)
                return e.matmul(BK(0)[:, 4:8], lhsT=ones, rhs=g_, start=True, stop=True)
            OP("pe", mmg, [B_st, B_const], [PB(0)])
            OP("act", lambda e: e.copy(out=st[:, 16:24], in_=BK(0)[:, 0:8]), [PB(0)], [B_st])
            OP("dve", lambda e: e.tensor_scalar(out=st[:, 24:28], in0=st[:, 16:20], scalar1=-1.0, scalar2=None, op0=ALU.mult), [B_st], [B_st])
            OP("act", lambda e: e.activation(out=st[:, 28:32], in_=st[:, 16:20], func=AF.Exp), [B_st], [B_st])
            OP("dve", lambda e: e.tensor_tensor(out=st[:, 32:36], in0=st[:, 20:24], in1=st[:, 16:20], op=ALU.subtract), [B_st], [B_st])
            OP("act", lambda e: e.activation(out=st[:, 32:36], in_=st[:, 32:36], func=AF.Exp), [B_st], [B_st])
            OP("act", lambda e: e.activation(out=st[:, 36:40], in_=st[:, 20:24], func=AF.Exp), [B_st], [B_st])
            negGc, expG, kds, glast = st[:, 24:28], st[:, 28:32], st[:, 32:36], st[:, 36:40]
            OP("dve", lambda e: e.tensor_tensor(out=v3(R_), in0=b4(TR), in1=c4(g_), op=ALU.mult), [B_const, B_st], [B_R])
            def mmb(e):
                e.matmul(BK(1)[:, :], lhsT=ones, rhs=R_, start=True, stop=False)
                return e.matmul(BK(1)[:, :], lhsT=ident, rhs=ngm, start=False, stop=True)
            OP("pe", mmb, [B_R, B_const, B_negm], [PB(1)])
            for h in range(4):
                OP("act", lambda e, h=h: e.activation(out=DT[:, h * 128:(h + 1) * 128], in_=BK(1)[:, h * 128:(h + 1) * 128], func=AF.Exp,
                                                      bias=negGc[:, h:h + 1], scale=1.0), [PB(1), B_st], [B_DT])
            def trk(e):
                ins = None
                for h in range(4):
                    ins = e.transpose(BK(2)[:, h * 128:(h + 1) * 128], kn[:, h * 128:(h + 1) * 128], ident)
                for h in range(4):
                    ins = e.transpose(BK(3)[:, h * 128:(h + 1) * 128], qn[:, h * 128:(h + 1) * 128], ident)
                return ins
            OP("pe", trk, [B_qkn, B_const], [PB(2), PB(3)])
            OP("act", lambda e: e.copy(out=kqT[:, 0:512], in_=BK(2)[:, :]), [PB(2)], [B_kqT])
            OP("act", lambda e: e.copy(out=kqT[:, 512:1024], in_=BK(3)[:, :]), [PB(3)], [B_kqT])
            kT, qT = kqT[:, 0:512], kqT[:, 512:1024]
            mm4(4, hs_(kT), hs_(kT), [B_kqT])
            mm4(5, hs_(kT), hs_(qT), [B_kqT])
            for h in range(4):
                OP("dve", lambda e, h=h: e.scalar_tensor_tensor(out=N_[:, h * 128:(h + 1) * 128], in0=BK(4)[:, h * 128:(h + 1) * 128],
                                                                scalar=beta[:, h:h + 1], in1=DT[:, h * 128:(h + 1) * 128], op0=ALU.mult, op1=ALU.mult),
                   [PB(4), B_st, B_DT], [B_N])
            OP("dve", lambda e: e.tensor_tensor(out=qkT, in0=BK(5)[:, :], in1=DT, op=ALU.mult), [PB(5), B_DT], [B_qkT])
            def trn(e):
                ins = None
                for h in range(4):
                    ins = e.transpose(BK(6)[:, h * 128:(h + 1) * 128], N_[:, h * 128:(h + 1) * 128], ident)
                return ins
            OP("pe", trn, [B_N, B_const], [PB(6)])
            OP("act", lambda e: e.copy(out=NT_, in_=BK(6)[:, :]), [PB(6)], [B_NT])
            OP("dve", lambda e: e.tensor_tensor(out=v3(m1), in0=v3(N_), in1=b4(offN(0)), op=ALU.mult), [B_N, B_offs], [B_m1])
            OP("dve", lambda e: e.tensor_tensor(out=v3(m2), in0=v3(NT_), in1=b4(offT(0)), op=ALU.mult), [B_NT, B_offs], [B_m2])
            OP("dve", lambda e: e.tensor_tensor(out=v3(Dm), in0=b4(ident), in1=v3(m1), op=ALU.subtract), [B_const, B_m1], [B_Dm])
            OP("dve", lambda e: e.tensor_tensor(out=v3(Dt), in0=b4(ident), in1=v3(m2), op=ALU.subtract), [B_const, B_m2], [B_Dt])
            for li in range(1, 7):
                lastl = li == 6
                mm4(2, hs_(NT_), hs_(Dm), [B_NT, B_Dm])
                OP("dve", lambda e, li=li: e.tensor_tensor(out=v3(T1), in0=v3(BK(2)[:, :]), in1=b4(offN(li)), op=ALU.mult), [PB(2), B_offs], [B_T1])
                mm4(4, hs_(Dt), hs_(T1), [B_Dt, B_T1])
                OP("dve", lambda e: e.tensor_tensor(out=Dm, in0=Dm, in1=BK(4)[:, :], op=ALU.subtract), [B_Dm, PB(4)], [B_Dm])
                if not lastl:
                    def trd(e):
                        ins = None
                        for h in range(4):
                            ins = e.transpose(BK(3)[:, h * 128:(h + 1) * 128], Dm[:, h * 128:(h + 1) * 128], ident)
                        return ins
                    OP("pe", trd, [B_Dm, B_const], [PB(3)])
                    OP("act", lambda e: e.copy(out=Dt, in_=BK(3)[:, :]), [PB(3)], [B_Dt])
            X = Dm
            OP("dve", lambda e: e.tensor_tensor(out=v3(KG), in0=v3(kn), in1=c4(expG), op=ALU.mult), [B_qkn, B_st], [B_KG])
            OP("dve", lambda e: e.tensor_tensor(out=v3(kdec), in0=v3(kn), in1=c4(kds), op=ALU.mult), [B_qkn, B_st], [B_kdec])
            mm4(6, hs_(KG), hs_(X), [B_KG, B_Dm])
            OP("act", lambda e: e.mul(out=nW, in_=BK(6)[:, :], mul=-1.0), [PB(6)], [B_nW])
            def mmv(e):
                ins = None
                for h in range(4):
                    e.matmul(BK(7)[:, h * 128:(h + 1) * 128], lhsT=X[:, h * 128:(h + 1) * 128], rhs=v_[:, h * 128:(h + 1) * 128], start=True, stop=False)
                    ins = e.matmul(BK(7)[:, h * 128:(h + 1) * 128], lhsT=nW[:, h * 128:(h + 1) * 128], rhs=S[:, h * 128:(h + 1) * 128], start=False, stop=True)
                return ins
            OP("pe", mmv, [B_Dm, B_qkv, B_nW, B_S], [PB(7)])
            OP("dve", lambda e: e.tensor_tensor(out=v3(vnew), in0=v3(BK(7)[:, :]), in1=c4(beta), op=ALU.mult), [PB(7), B_st], [B_vnew])
            mm4(1, hs_(qT), hs_(S), [B_kqT, B_S])
            mm4(6, hs_(qkT), hs_(vnew), [B_qkT, B_vnew])
            mm4(8, hs_(kdec), hs_(vnew), [B_kdec, B_vnew])
            OP("dve", lambda e: e.tensor_tensor(out=v3(t2_), in0=v3(S), in1=c4(glast), op=ALU.mult), [B_S, B_st], [B_t2])
            OP("dve", lambda e: e.tensor_tensor(out=Sn, in0=t2_, in1=BK(8)[:, :], op=ALU.add), [B_t2, PB(8)], [B_Sn])
            OP("dve", lambda e: e.tensor_tensor(out=v3(ot), in0=v3(BK(1)[:, :]), in1=c4(expG), op=ALU.mult), [PB(1), B_st], [B_ot])
            OP("dve", lambda e: e.tensor_tensor(out=ot, in0=ot, in1=BK(6)[:, :], op=ALU.add), [B_ot, PB(6)], [B_ot])
            P.dma("pool", (OAF if d == 0 else OAB)[r0:r0 + 128, :], ot, reads=[B_ot], writes=[Buf("oad")], key=f"a_ot{sl}")

        lists = []
        for d in range(2):
            order = list(range(NT)) if d == 0 else [1, 0] + list(range(NT - 1, 1, -1))
            if DBG.get('a_tiles'):
                order = [t_ for t_ in order if t_ in DBG['a_tiles']]
            P.record()
            OP("pool", lambda e, d=d: e.memset(Ss[d][0][0], 0.0), [], [Ss[d][0][1]])
            a_load(order[0], d, d * 2)
            cur = 0
            for it, tile in enumerate(order):
                if it + 1 < len(order):
                    a_load(order[it + 1], d, d * 2 + (it + 1) % 2)
                a_chunk(tile, d, d * 2 + it % 2, Ss[d][cur][0], Ss[d][cur][1], Ss[d][1 - cur][0], Ss[d][1 - cur][1])
                cur = 1 - cur
            lists.append(P.stop())
        cfin = None
        if MERGE_C:
            clists, cfin = c_build(layer)
            if DBG.get("seq_c"):
                P.play(lists)
                lists = clists
            else:
                lists = lists + clists
        per_chunk = len(lists[0]) // max(1, NT)
        P.play(lists, lead=[0, DBG.get("a_lead", per_chunk // 2)] + [0] * (len(lists) - 2))

        if FUSE_FIN:
            return
        P.barrier()
        A.release(persist_mark)
        (gnn2, B_gnn2), = mk(512, "gnn2")
        P.dma("sp", gnn2, gdnn_d[layer], writes=[B_gnn2], key="a2_gnn")
        fs_ = mk(512, "a2f", 2)
        bs_ = mk(512, "a2b", 2)
        zs_ = mk(512, "a2z", 2)
        sqs_ = mk(512, "a2sq", 2)
        sts_ = mk(16, "a2st", 2)
        tiles2 = list(range(NT))
        if DBG.get('a_tiles'):
            tiles2 = [t_ for t_ in tiles2 if t_ in DBG['a_tiles']]

        def a2_load(tile, sl):
            r0 = tile * 128
            P.dma("sp", fs_[sl][0], OAF[r0:r0 + 128, :], writes=[fs_[sl][1]], key=f"a2f{sl}")
            P.dma("sp", bs_[sl][0], OAB[r0:r0 + 128, :], writes=[bs_[sl][1]], key=f"a2b{sl}")
            P.dma("sp", zs_[sl][0], TM[r0:r0 + 128, TM_AZ:TM_AZ + 512], writes=[zs_[sl][1]], key=f"a2z{sl}")

        def a2_comp(tile, sl):
            r0 = tile * 128
            f_, B_f = fs_[sl]
            b_, B_b = bs_[sl]
            z_, B_z = zs_[sl]
            sq, B_sq = sqs_[sl]
            st, B_st = sts_[sl]
            OP("dve", lambda e: e.tensor_tensor(out=f_, in0=f_, in1=b_, op=ALU.add), [B_f, B_b], [B_f])
            OP("pool", lambda e: e.tensor_tensor(out=sq, in0=f_, in1=f_, op=ALU.mult), [B_f], [B_sq])
            OP("dve", lambda e: e.reduce_sum(out=st[:, 0:4], in_=v3(sq), axis=AX.X), [B_sq], [B_st])
            OP("act", lambda e: e.activation(out=st[:, 0:4], in_=st[:, 0:4], func=AF.Sqrt, bias=1e-6, scale=1.0 / 128), [B_st], [B_st])
            OP("dve", lambda e: e.reciprocal(out=st[:, 0:4], in_=st[:, 0:4]), [B_st], [B_st])
            OP("dve", lambda e: e.tensor_tensor(out=v3(f_), in0=v3(f_), in1=st[:, 0:4].unsqueeze(2).to_broadcast([128, 4, 128]), op=ALU.mult),
               [B_f, B_st], [B_f])
            OP("act", lambda e: e.activation(out=z_, in_=z_, func=AF.Silu), [B_z], [B_z])
            OP("pool", lambda e: e.tensor_tensor(out=z_, in0=z_, in1=gnn2, op=ALU.mult), [B_z, B_gnn2], [B_z])
            OP("dve", lambda e: e.tensor_tensor(out=f_, in0=f_, in1=z_, op=ALU.mult), [B_f, B_z], [B_f])
            P.dma("pool", MIXA[r0:r0 + 128, :], f_, reads=[B_f], writes=[Buf("mixa")], key=f"a2o{sl}")

        a2_load(tiles2[0], 0)
        for it, tile in enumerate(tiles2):
            if it + 1 < len(tiles2):
                a2_load(tiles2[it + 1], (it + 1) % 2)
            a2_comp(tile, it % 2)
        if cfin is not None:
            cfin()

    def phase_B(layer):
        P.barrier()
        A.release(persist_mark)
        (kT, B_kT), = mk(2 * T, "kT")
        (qT, B_qT), = mk(2 * T, "qT")
        (V1, B_V1), = mk(NT * 260, "V1")
        (G, B_G), = mk(4 * 15 * 64, "G")
        for hp in range(2):
            P.dma("sp", kT[:, hp * T:(hp + 1) * T], FM[FM_BK + hp * 128:FM_BK + (hp + 1) * 128, :], writes=[B_kT], key="b_kT")
            P.dma("pool", qT[:, hp * T:(hp + 1) * T], FM[FM_BQ + hp * 128:FM_BQ + (hp + 1) * 128, :], writes=[B_qT], key="b_qT")
        P.dma("sp", G, nab_d[layer], writes=[B_G], key="b_G")
        OP("pool", lambda e: e.memset(V1, 1.0), [], [B_V1])
        for t_ in range(NT):
            P.dma("sp", V1[:, t_ * 260:(t_ + 1) * 260].rearrange("p (h c) -> p h c", h=4)[:, :, 0:64],
                  TM[t_ * 128:(t_ + 1) * 128, TM_BV:TM_BV + 256].rearrange("p (h c) -> p h c", h=4), writes=[B_V1], key="b_V1")
        G4 = G.rearrange("p (h e c) -> p h e c", h=4, e=15)
        tmps = mk(512, "btmp", 4)
        pTs = mk(512, "bpT", 4)
        nums = mk(512, "bnum", 4)
        rrs = mk(512, "brr", 4)
        zts = mk(512, "bz", 4)
        qtiles = [(256 + 512 * a, 512, a) for a in range(8)]
        if layer < DEPTH - 1:
            qtiles = [(0, 256, None)] + qtiles
        cnts = [dict(st=0, w=0, o=0) for _ in range(2)]
        def do_qtile(h, c0, nq, a):
            if True:
                hp, base = h // 2, (h % 2) * 64
                sm = h % 2
                cnt = cnts[sm]
                ob = 4 * sm + 2
                fs = 2 * sm + cnt["o"] % 2
                cnt["o"] += 1
                qop = qT[base:base + 64, hp * T + c0:hp * T + c0 + nq]
                first = True
                for kt in range(2):
                    sb_ = 4 * sm + cnt["st"] % 2
                    cnt["st"] += 1
                    ws = 2 * sm + cnt["w"] % 2
                    cnt["w"] += 1
                    pT, B_pT = pTs[ws]
                    OP("pe", lambda e, sb_=sb_, kt=kt, qop=qop, nq=nq: e.matmul(
                        banks[sb_][:, 0:nq], lhsT=kT[base:base + 64, hp * T + kt * 128:hp * T + (kt + 1) * 128], rhs=qop, start=True, stop=True),
                        [B_kT, B_qT], [PSB[sb_]])
                    OP("act", lambda e, sb_=sb_, pT=pT, nq=nq: e.activation(out=pT[:, 0:nq], in_=banks[sb_][:, 0:nq], func=AF.Exp, scale=0.125),
                       [PSB[sb_]], [B_pT])
                    last_mm = (a is None and kt == 1)
                    def pv(e, ob=ob, kt=kt, pT=pT, nq=nq, first=first, last_mm=last_mm):
                        e.matmul(banks[ob][0:64, 0:nq], lhsT=V1[:, kt * 260 + h * 65:kt * 260 + h * 65 + 64], rhs=pT[:, 0:nq], start=first, stop=last_mm)
                        return e.matmul(banks[ob + 1][0:64, 0:nq], lhsT=ones[:, 0:64], rhs=pT[:, 0:nq], start=first, stop=last_mm)
                    OP("pe", pv, [B_V1, B_pT, B_const], [PSB[ob], PSB[ob + 1]])
                    first = False
                if a is not None:
                    rows = {}
                    for kr in range(64):
                        js = [j for j in range(8) if min(max(8 * a + j - 4, 0), 56) <= kr <= min(max(8 * a + j - 4, 0), 56) + 7]
                        if js:
                            rows[kr] = (js[0], js[-1])
                    kts = sorted(set(kr // 2 for kr in rows))
                    for mi, m in enumerate(kts):
                        halves = [(hf, rows[2 * m + hf]) for hf in range(2) if (2 * m + hf) in rows]
                        jl = min(v[0] for _, v in halves)
                        jh = max(v[1] for _, v in halves)
                        sb_ = 4 * sm + cnt["st"] % 2
                        cnt["st"] += 1
                        ws = 2 * sm + cnt["w"] % 2
                        cnt["w"] += 1
                        tmp, B_tmp = tmps[ws]
                        pT, B_pT = pTs[ws]
                        tk = 2 + m
                        OP("pe", lambda e, sb_=sb_, tk=tk, jl=jl, jh=jh: e.matmul(
                            banks[sb_][:, jl * 64:(jh + 1) * 64], lhsT=kT[base:base + 64, hp * T + tk * 128:hp * T + (tk + 1) * 128],
                            rhs=qT[base:base + 64, hp * T + c0 + jl * 64:hp * T + c0 + (jh + 1) * 64], start=True, stop=True),
                            [B_kT, B_qT], [PSB[sb_]])
                        ucs = slice(jl * 64, (jh + 1) * 64)
                        uneven = any((j0, j1) != (jl, jh) for _, (j0, j1) in halves) or len(halves) < 2
                        if uneven:
                            OP("pool", lambda e, pT=pT, ucs=ucs: e.memset(pT[:, ucs], 0.0), [], [B_pT])
                        for hi, (hf, (j0, j1)) in enumerate(halves):
                            kr = 2 * m + hf
                            e_lo = 7 - kr + 8 * a + j0
                            nj = j1 - j0 + 1
                            assert 0 <= e_lo and e_lo + nj <= 15
                            pr = slice(hf * 64, hf * 64 + 64)
                            cs = slice(j0 * 64, (j1 + 1) * 64)
                            OP("dve", lambda e, tmp=tmp, sb_=sb_, pr=pr, cs=cs, e_lo=e_lo, nj=nj: e.scalar_tensor_tensor(
                                out=tmp[pr, cs].rearrange("p (j c) -> p j c", c=64), in0=banks[sb_][pr, cs].rearrange("p (j c) -> p j c", c=64),
                                scalar=0.125, in1=G4[pr, h, e_lo:e_lo + nj, :], op0=ALU.mult, op1=ALU.add), [PSB[sb_], B_G], [B_tmp])
                            OP("act", lambda e, tmp=tmp, pT=pT, pr=pr, cs=cs: e.activation(out=pT[pr, cs], in_=tmp[pr, cs], func=AF.Exp),
                               [B_tmp], [B_pT])
                        last_mm = (mi == len(kts) - 1)
                        def pv(e, ob=ob, tk=tk, pT=pT, ucs=ucs, last_mm=last_mm):
                            e.matmul(banks[ob][0:64, ucs], lhsT=V1[:, tk * 260 + h * 65:tk * 260 + h * 65 + 64], rhs=pT[:, ucs], start=False, stop=last_mm)
                            return e.matmul(banks[ob + 1][0:64, ucs], lhsT=ones[:, 0:64], rhs=pT[:, ucs], start=False, stop=last_mm)
                        OP("pe", pv, [B_V1, B_pT, B_const], [PSB[ob], PSB[ob + 1]])
                num, B_num = nums[fs]
                rr, B_rr = rrs[fs]
                zt, B_zt = zts[fs]
                P.dma("sp", zt[0:64, 0:nq], FM[FM_BZ + h * 64:FM_BZ + (h + 1) * 64, c0:c0 + nq], writes=[B_zt], key=f"b_z{fs}")
                OP("act", lambda e, zt=zt, nq=nq: e.activation(out=zt[0:64, 0:nq], in_=zt[0:64, 0:nq], func=AF.Silu), [B_zt], [B_zt])
                OP("act", lambda e, num=num, ob=ob, nq=nq: e.copy(out=num[0:64, 0:nq], in_=banks[ob][0:64, 0:nq]), [PSB[ob]], [B_num])
                OP("dve", lambda e, rr=rr, ob=ob, nq=nq: e.reciprocal(out=rr[0:64, 0:nq], in_=banks[ob + 1][0:64, 0:nq]), [PSB[ob + 1]], [B_rr])
                OP("dve", lambda e, num=num, rr=rr, nq=nq: e.tensor_tensor(out=num[0:64, 0:nq], in0=num[0:64, 0:nq], in1=rr[0:64, 0:nq], op=ALU.mult),
                   [B_num, B_rr], [B_num])
                OP("pool", lambda e, num=num, zt=zt, nq=nq: e.tensor_tensor(out=num[0:64, 0:nq], in0=num[0:64, 0:nq], in1=zt[0:64, 0:nq], op=ALU.mult),
                   [B_num, B_zt], [B_num])
                P.dma("sp", MIXB[h * 64:(h + 1) * 64, c0:c0 + nq], num[0:64, 0:nq], reads=[B_num], writes=[Buf("mixb")], key=f"b_o{fs}")
        blists = []
        for sm_ in range(2):
            P.record()
            for h in (sm_, sm_ + 2):
                for (c0, nq, a) in qtiles:
                    do_qtile(h, c0, nq, a)
            blists.append(P.stop())
        P.play(blists)

    def c_build(layer):
        (w2p, B_w2p), = mk(256, "w2p")
        (b2r, B_b2r), = mk(256, "b2r")
        P.dma("sp", w2p[0:32, :], w2p_d[layer], writes=[B_w2p], key="c_w2p")
        P.dma("sp", b2r[0:1, :], b2r_d[layer], writes=[B_b2r], key="c_b2r")
        CT = {}
        for nm, n in (("tc", 768), ("rt", 128), ("cos", 128), ("sin", 128)):
            CT[nm] = mk(n, "c" + nm, 4)
        for nm, n in (("xsw", 256), ("sp", 128), ("Ep", 128), ("Em", 128), ("gl", 8), ("qh", 128), ("kh", 128), ("qkT", 256),
                      ("khTm", 512), ("am", 512), ("tmp", 256), ("os", 256)):
            CT[nm] = mk(n, "c" + nm, 2)
        CS = [mk(256, f"cS{d_}", 2) for d_ in range(2)]
        scale_q = 32 ** -0.5

        def c_load(tile, d, ql):
            r0 = tile * 128
            tc, B_tc = CT["tc"][ql]
            rt, B_rt = CT["rt"][ql]
            P.dma("sp", tc, TM[r0:r0 + 128, 768:1536], writes=[B_tc], key=f"c_tc{ql}")
            P.dma("sp", rt[0:32, :], FM[FM_CR:FM_CR + 32, r0:r0 + 128], writes=[B_rt], key=f"c_rt{ql}")
            if tile >= 2:
                l0 = (tile - 2) * 128
                P.dma("sp", CT["cos"][ql][0], ropec_d[l0:l0 + 128, :], writes=[CT["cos"][ql][1]], key=f"c_cs{ql}")
                P.dma("sp", CT["sin"][ql][0], ropes_d[l0:l0 + 128, :], writes=[CT["sin"][ql][1]], key=f"c_sn{ql}")

        def c_chunk(tile, d, ql, S, B_S, Sn, B_Sn):
            r0 = tile * 128
            g = lambda nm: CT[nm][d]
            tc, B_tc = CT["tc"][ql]; rt, B_rt = CT["rt"][ql]; cs_, B_cs = CT["cos"][ql]; sn, B_sn = CT["sin"][ql]
            xsw, B_xsw = g("xsw"); sp_, B_sp = g("sp"); Ep, B_Ep = g("Ep"); Em, B_Em = g("Em"); gl, B_gl = g("gl")
            qh, B_qh = g("qh"); kh, B_kh = g("kh"); qkT, B_qhT = g("qkT"); khTm, B_khTm = g("khTm"); am, B_am = g("am")
            tmp, B_tmp = g("tmp"); os_, B_os = g("os")
            qhT = qkT[:, 0:128]
            TR = tril if d == 0 else triu
            bk = (lambda lb: 6 + d) if MERGE_C else (lambda lb: 4 * d + lb % 4)
            BK = lambda lb: banks[bk(lb)]
            PB = lambda lb: PSB[bk(lb)]
            if tile >= 2:
                x4 = tc[:, 0:256].rearrange("p (g two f) -> p g two f", two=2, f=8)
                xs4 = xsw.rearrange("p (g two f) -> p g two f", two=2, f=8)
                OP("pool", lambda e: e.tensor_copy(out=xs4[:, :, 0, :], in_=x4[:, :, 1, :]), [B_tc], [B_xsw])
                OP("pool", lambda e: e.tensor_copy(out=xs4[:, :, 1, :], in_=x4[:, :, 0, :]), [B_tc], [B_xsw])
                x3 = tc[:, 0:256].rearrange("p (a c) -> p a c", a=2)
                xs3 = xsw.rearrange("p (a c) -> p a c", a=2)
                OP("dve", lambda e: e.tensor_tensor(out=x3, in0=x3, in1=cs_.unsqueeze(1).to_broadcast([128, 2, 128]), op=ALU.mult), [B_tc, B_cs], [B_tc])
                OP("dve", lambda e: e.tensor_tensor(out=xs3, in0=xs3, in1=sn.unsqueeze(1).to_broadcast([128, 2, 128]), op=ALU.mult), [B_xsw, B_sn], [B_xsw])
                OP("dve", lambda e: e.tensor_tensor(out=tc[:, 0:256], in0=tc[:, 0:256], in1=xsw, op=ALU.add), [B_tc, B_xsw], [B_tc])
            def mmz(e):
                e.matmul(BK(0)[:, 0:128], lhsT=rt[0:32, :], rhs=w2p[0:32, d * 128:(d + 1) * 128], start=True, stop=False)
                return e.matmul(BK(0)[:, 0:128], lhsT=ones[0:1, 0:128], rhs=b2r[0:1, d * 128:(d + 1) * 128], start=False, stop=True)
            OP("pe", mmz, [B_rt, B_w2p, B_b2r, B_const], [PB(0)])
            OP("act", lambda e: e.activation(out=sp_, in_=BK(0)[:, 0:128], func=AF.Exp, scale=-1.0), [PB(0)], [B_sp])
            OP("act", lambda e: e.activation(out=sp_, in_=sp_, func=AF.Ln, bias=1.0, scale=1.0), [B_sp], [B_sp])
            def mmc(e):
                e.matmul(BK(1)[:, 0:128], lhsT=TR, rhs=sp_, start=True, stop=True)
                return e.matmul(BK(1)[:, 128:129], lhsT=sp_, rhs=ones[:, 0:1], start=True, stop=True)
            OP("pe", mmc, [B_sp, B_const], [PB(1)])
            OP("act", lambda e: e.activation(out=Ep, in_=BK(1)[:, 0:128], func=AF.Exp, scale=-1.0 / 16), [PB(1)], [B_Ep])
            OP("act", lambda e: e.activation(out=Em, in_=BK(1)[:, 0:128], func=AF.Exp, scale=1.0 / 16), [PB(1)], [B_Em])
            OP("act", lambda e: e.activation(out=gl[:, 0:1], in_=BK(1)[:, 128:129], func=AF.Exp, scale=-1.0 / 16), [PB(1)], [B_gl])
            OP("dve", lambda e: e.scalar_tensor_tensor(out=qh, in0=tc[:, 0:128], scalar=scale_q, in1=Ep, op0=ALU.mult, op1=ALU.mult), [B_tc, B_Ep], [B_qh])
            OP("dve", lambda e: e.tensor_tensor(out=kh, in0=tc[:, 128:256], in1=Em, op=ALU.mult), [B_tc, B_Em], [B_kh])
            def trs(e):
                e.transpose(BK(2)[:, 0:128], qh, ident)
                return e.transpose(BK(2)[:, 128:256], kh, ident)
            OP("pe", trs, [B_qh, B_kh, B_const], [PB(2)])
            OP("act", lambda e: e.copy(out=qkT, in_=BK(2)[:, 0:256]), [PB(2)], [B_qhT])
            for h in range(4):
                OP("dve", lambda e, h=h: e.tensor_scalar(out=khTm[:, h * 128:(h + 1) * 128], in0=qkT[:, 128:256], scalar1=hm[:, h:h + 1],
                                                         scalar2=None, op0=ALU.mult), [B_qhT, B_const], [B_khTm])
            def mma(e):
                ins = None
                for h in range(4):
                    ins = e.matmul(BK(3)[:, h * 128:(h + 1) * 128], lhsT=khTm[:, h * 128:(h + 1) * 128], rhs=qhT, start=True, stop=True)
                return ins
            OP("pe", mma, [B_khTm, B_qhT], [PB(3)])
            OP("dve", lambda e: e.tensor_tensor(out=am.rearrange("p (h j) -> p h j", h=4), in0=BK(3)[:, :].rearrange("p (h j) -> p h j", h=4),
                                                in1=TR.unsqueeze(1).to_broadcast([128, 4, 128]), op=ALU.mult), [PB(3), B_const], [B_am])
            def mmo(e):
                e.matmul(BK(0)[:, 0:256], lhsT=qhT, rhs=S, start=True, stop=False)
                ins = None
                for h in range(4):
                    ins = e.matmul(BK(0)[:, h * 64:(h + 1) * 64], lhsT=am[:, h * 128:(h + 1) * 128],
                                   rhs=tc[:, 256 + h * 64:256 + (h + 1) * 64], start=False, stop=(h == 3))
                return ins
            OP("pe", mmo, [B_qhT, B_S, B_am, B_tc], [PB(0)])
            OP("act", lambda e: e.copy(out=os_, in_=BK(0)[:, 0:256]), [PB(0)], [B_os])
            OP("pe", lambda e: e.matmul(BK(1)[:, 0:256], lhsT=kh, rhs=tc[:, 256:512], start=True, stop=True), [B_kh, B_tc], [PB(1)])
            OP("dve", lambda e: e.tensor_tensor(out=tmp, in0=BK(1)[:, 0:256], in1=bd, op=ALU.mult), [PB(1), B_const], [B_tmp])
            OP("dve", lambda e: e.tensor_tensor(out=tmp, in0=tmp, in1=S, op=ALU.add), [B_tmp, B_S], [B_tmp])
            OP("dve", lambda e: e.tensor_scalar(out=Sn, in0=tmp, scalar1=gl[:, 0:1], scalar2=None, op0=ALU.mult), [B_tmp, B_gl], [B_Sn])
            P.dma("pool", (OCF if d == 0 else OCB)[r0:r0 + 128, :], os_, reads=[B_os], writes=[Buf("ocd")], key=f"c_os{d}")

        lists = []
        for d in range(2):
            order = list(range(NT)) if d == 0 else [1, 0] + list(range(NT - 1, 1, -1))
            if DBG.get('c_tiles'):
                order = [t_ for t_ in order if t_ in DBG['c_tiles']]
            P.record()
            OP("pool", lambda e, d=d: e.memset(CS[d][0][0], 0.0), [], [CS[d][0][1]])
            c_load(order[0], d, d * 2)
            cur = 0
            for it, tile in enumerate(order):
                if it + 1 < len(order):
                    c_load(order[it + 1], d, d * 2 + (it + 1) % 2)
                c_chunk(tile, d, d * 2 + it % 2, CS[d][cur][0], CS[d][cur][1], CS[d][1 - cur][0], CS[d][1 - cur][1])
                cur = 1 - cur
            lists.append(P.stop())

        def c_final():
            (gn, B_gn), = mk(256, "gn")
            P.dma("sp", gn, glan_d[layer], writes=[B_gn], key="c_gn")
            fs_ = mk(256, "c2f", 2)
            bs_ = mk(256, "c2b", 2)
            zs_ = mk(256, "c2z", 2)
            sqs_ = mk(256, "c2sq", 2)
            sts_ = mk(16, "c2st", 2)
            tiles2 = list(range(NT))
            if DBG.get('c_tiles'):
                tiles2 = [t_ for t_ in tiles2 if t_ in DBG['c_tiles']]

            def ld(tile, sl):
                r0 = tile * 128
                P.dma("sp", fs_[sl][0], OCF[r0:r0 + 128, :], writes=[fs_[sl][1]], key=f"c2f{sl}")
                P.dma("sp", bs_[sl][0], OCB[r0:r0 + 128, :], writes=[bs_[sl][1]], key=f"c2b{sl}")
                P.dma("sp", zs_[sl][0], TM[r0:r0 + 128, TM_CZ:TM_CZ + 256], writes=[zs_[sl][1]], key=f"c2z{sl}")

            def comp(tile, sl):
                r0 = tile * 128
                f_, B_f = fs_[sl]
                b_, B_b = bs_[sl]
                z_, B_z = zs_[sl]
                sq, B_sq = sqs_[sl]
                st, B_st = sts_[sl]
                v4 = lambda ap: ap.rearrange("p (h v) -> p h v", h=4)
                OP("dve", lambda e: e.tensor_tensor(out=f_, in0=f_, in1=b_, op=ALU.add), [B_f, B_b], [B_f])
                OP("pool", lambda e: e.tensor_tensor(out=sq, in0=f_, in1=f_, op=ALU.mult), [B_f], [B_sq])
                OP("dve", lambda e: e.reduce_sum(out=st[:, 0:4], in_=v4(sq), axis=AX.X), [B_sq], [B_st])
                OP("act", lambda e: e.activation(out=st[:, 4:8], in_=st[:, 0:4], func=AF.Sqrt, bias=1e-6, scale=1.0 / 64), [B_st], [B_st])
                OP("dve", lambda e: e.reciprocal(out=st[:, 4:8], in_=st[:, 4:8]), [B_st], [B_st])
                OP("dve", lambda e: e.tensor_tensor(out=v4(f_), in0=v4(f_), in1=st[:, 4:8].unsqueeze(2).to_broadcast([128, 4, 64]), op=ALU.mult),
                   [B_f, B_st], [B_f])
                OP("act", lambda e: e.activation(out=z_, in_=z_, func=AF.Silu), [B_z], [B_z])
                OP("pool", lambda e: e.tensor_tensor(out=z_, in0=z_, in1=gn, op=ALU.mult), [B_z, B_gn], [B_z])
                OP("dve", lambda e: e.tensor_tensor(out=f_, in0=f_, in1=z_, op=ALU.mult), [B_f, B_z], [B_f])
                P.dma("pool", MIXC[r0:r0 + 128, :], f_, reads=[B_f], writes=[Buf("mixc")], key=f"c2o{sl}")

            ld(tiles2[0], 0)
            for it, tile in enumerate(tiles2):
                if it + 1 < len(tiles2):
                    ld(tiles2[it + 1], (it + 1) % 2)
                comp(tile, it % 2)

        return lists, c_final

    def phase_C(layer):
        P.barrier()
        A.release(persist_mark)
        lists, fin = c_build(layer)
        P.play(lists)
        P.barrier()
        A.release(persist_mark)
        fin()

    def phase_P4(layer):
        P.barrier()
        A.release(persist_mark)
        (wo, B_wo), = mk(8 * D, "wo")
        (lng, B_lng), = mk(D, "lng")
        (lnb, B_lnb), = mk(D, "lnb")
        wov = w_out[layer].rearrange("(k p) c -> p k c", p=128)
        for k in range(8):
            P.dma("pool", wo[:, k * D:(k + 1) * D], wov[:, k, :], writes=[B_wo], key="p4_wo")
        P.dma("sp", lng, lng_d[layer], writes=[B_lng], key="p4_lng")
        P.dma("sp", lnb, lnb_d[layer], writes=[B_lnb], key="p4_lnb")
        xas = mk(D, "xa", 2)
        mas = mk(512, "ma", 2)
        mcs = mk(256, "mc", 2)
        mixTs = mk(8 * 128, "mixT", 2)
        hs = mk(D, "h", 2)
        sts = mk(16, "st", 2)
        if FUSE_FIN:
            mbs = mk(768, "mb2", 2)
            zzs = mk(768, "zz", 2)
            sqs4 = mk(768, "sq4", 2)
            st8s = mk(16, "st8", 2)
            (gno, B_gno), = mk(768, "gno")
            P.dma("sp", gno[:, 0:512], gdnn_d[layer], writes=[B_gno], key="p4_gno")
            P.dma("sp", gno[:, 512:768], glan_d[layer], writes=[B_gno], key="p4_gno")
        last = layer == DEPTH - 1
        xsrc = x_all if layer == 0 else XC
        tiles = list(range(2, NT)) if last else list(range(NT))
        def p4_load(tile, sl):
            r0 = tile * 128
            xa, B_xa = xas[sl]
            ma, B_ma = mas[sl]
            mc, B_mc = mcs[sl]
            mixT, B_mixT = mixTs[sl]
            P.dma("sp", xa, xsrc[r0:r0 + 128, :], writes=[B_xa], key=f"p4_xa{sl}")
            if FUSE_FIN:
                mb_, B_mb = mbs[sl]
                zz, B_zz = zzs[sl]
                P.dma("sp", ma, OAF[r0:r0 + 128, :], writes=[B_ma], key=f"p4_ma{sl}")
                P.dma("sp", mc, OCF[r0:r0 + 128, :], writes=[B_mc], key=f"p4_mc{sl}")
                P.dma("sp", mb_[:, 0:512], OAB[r0:r0 + 128, :], writes=[B_mb], key=f"p4_mb2{sl}")
                P.dma("sp", mb_[:, 512:768], OCB[r0:r0 + 128, :], writes=[B_mb], key=f"p4_mb2{sl}")
                P.dma("sp", zz[:, 0:512], TM[r0:r0 + 128, TM_AZ:TM_AZ + 512], writes=[B_zz], key=f"p4_zz{sl}")
                P.dma("sp", zz[:, 512:768], TM[r0:r0 + 128, TM_CZ:TM_CZ + 256], writes=[B_zz], key=f"p4_zz{sl}")
            else:
                P.dma("sp", ma, MIXA[r0:r0 + 128, :], writes=[B_ma], key=f"p4_ma{sl}")
                P.dma("sp", mc, MIXC[r0:r0 + 128, :], writes=[B_mc], key=f"p4_mc{sl}")
            for j in range(2):
                P.dma("sp", mixT[:, (4 + j) * 128:(5 + j) * 128], MIXB[j * 128:(j + 1) * 128, r0:r0 + 128], writes=[B_mixT], key=f"p4_mb{sl}")

        p4_load(tiles[0], 0)
        for it, tile in enumerate(tiles):
            sl = it % 2
            if it + 1 < len(tiles):
                p4_load(tiles[it + 1], (it + 1) % 2)
            r = 1 if tile < 2 else 0
            r0 = tile * 128
            xa, B_xa = xas[sl]
            ma, B_ma = mas[sl]
            mc, B_mc = mcs[sl]
            mixT, B_mixT = mixTs[sl]
            h_, B_h = hs[sl]
            st, B_st = sts[sl]
            if FUSE_FIN:
                mb_, B_mb = mbs[sl]
                zz, B_zz = zzs[sl]
                sq4, B_sq4 = sqs4[sl]
                s8, B_s8 = st8s[sl]
                OP("dve", lambda e, ma=ma, mb_=mb_: e.tensor_tensor(out=ma, in0=ma, in1=mb_[:, 0:512], op=ALU.add), [B_ma, B_mb], [B_ma])
                OP("dve", lambda e, mc=mc, mb_=mb_: e.tensor_tensor(out=mc, in0=mc, in1=mb_[:, 512:768], op=ALU.add), [B_mc, B_mb], [B_mc])
                OP("act", lambda e, ma=ma, sq4=sq4: e.activation(out=sq4[:, 0:512], in_=ma, func=AF.Square), [B_ma], [B_sq4])
                OP("act", lambda e, mc=mc, sq4=sq4: e.activation(out=sq4[:, 512:768], in_=mc, func=AF.Square), [B_mc], [B_sq4])
                OP("dve", lambda e, sq4=sq4, s8=s8: e.reduce_sum(out=s8[:, 0:4], in_=sq4[:, 0:512].rearrange("p (h v) -> p h v", h=4), axis=AX.X), [B_sq4], [B_s8])
                OP("dve", lambda e, sq4=sq4, s8=s8: e.reduce_sum(out=s8[:, 4:8], in_=sq4[:, 512:768].rearrange("p (h v) -> p h v", h=4), axis=AX.X), [B_sq4], [B_s8])
                OP("act", lambda e, s8=s8: e.activation(out=s8[:, 0:4], in_=s8[:, 0:4], func=AF.Sqrt, bias=1e-6, scale=1.0 / 128), [B_s8], [B_s8])
                OP("act", lambda e, s8=s8: e.activation(out=s8[:, 4:8], in_=s8[:, 4:8], func=AF.Sqrt, bias=1e-6, scale=1.0 / 64), [B_s8], [B_s8])
                OP("dve", lambda e, s8=s8: e.reciprocal(out=s8[:, 0:8], in_=s8[:, 0:8]), [B_s8], [B_s8])
                OP("dve", lambda e, ma=ma, s8=s8: e.tensor_tensor(out=ma.rearrange("p (h v) -> p h v", h=4), in0=ma.rearrange("p (h v) -> p h v", h=4),
                                                               in1=s8[:, 0:4].unsqueeze(2).to_broadcast([128, 4, 128]), op=ALU.mult), [B_ma, B_s8], [B_ma])
                OP("dve", lambda e, mc=mc, s8=s8: e.tensor_tensor(out=mc.rearrange("p (h v) -> p h v", h=4), in0=mc.rearrange("p (h v) -> p h v", h=4),
                                                               in1=s8[:, 4:8].unsqueeze(2).to_broadcast([128, 4, 64]), op=ALU.mult), [B_mc, B_s8], [B_mc])
                OP("act", lambda e, zz=zz: e.activation(out=zz, in_=zz, func=AF.Silu), [B_zz], [B_zz])
                OP("pool", lambda e, zz=zz: e.tensor_tensor(out=zz, in0=zz, in1=gno, op=ALU.mult), [B_zz, B_gno], [B_zz])
                OP("dve", lambda e, ma=ma, zz=zz: e.tensor_tensor(out=ma, in0=ma, in1=zz[:, 0:512], op=ALU.mult), [B_ma, B_zz], [B_ma])
                OP("dve", lambda e, mc=mc, zz=zz: e.tensor_tensor(out=mc, in0=mc, in1=zz[:, 512:768], op=ALU.mult), [B_mc, B_zz], [B_mc])
            def trs(e, ma=ma, mc=mc):
                ins = None
                for k in range(4):
                    ins = e.transpose(banks[0][:, k * 128:(k + 1) * 128], ma[:, k * 128:(k + 1) * 128], ident)
                for k in range(2):
                    ins = e.transpose(banks[1][:, k * 128:(k + 1) * 128], mc[:, k * 128:(k + 1) * 128], ident)
                return ins
            OP("pe", trs, [B_ma, B_mc, B_const], [PSB[0], PSB[1]])
            OP("act", lambda e, mixT=mixT: e.copy(out=mixT[:, 0:512], in_=banks[0][:, :]), [PSB[0]], [B_mixT])
            OP("act", lambda e, mixT=mixT: e.copy(out=mixT[:, 768:1024], in_=banks[1][:, 0:256]), [PSB[1]], [B_mixT])
            for half in range(2):
                bank = 2 + half
                def mmy(e, mixT=mixT, half=half, bank=bank):
                    ins = None
                    for k in range(8):
                        ins = e.matmul(banks[bank][:, :], lhsT=mixT[:, k * 128:(k + 1) * 128],
                                       rhs=wo[:, k * D + half * 512:k * D + (half + 1) * 512], start=(k == 0), stop=(k == 7))
                    return ins
                OP("pe", mmy, [B_mixT, B_wo], [PSB[bank]])
                OP("dve", lambda e, h_=h_, half=half, bank=bank, r=r: e.tensor_tensor(
                    out=h_[:, half * 512:(half + 1) * 512], in0=banks[bank][:, :],
                    in1=modb[r][:, 2 * D + half * 512:2 * D + (half + 1) * 512], op=ALU.mult), [PSB[bank], B_modb], [B_h])
            OP("dve", lambda e, h_=h_, xa=xa: e.scalar_tensor_tensor(out=h_, in0=xa, scalar=DN_ALPHA, in1=h_, op0=ALU.mult, op1=ALU.add),
               [B_xa, B_h], [B_h])
            OP("dve", lambda e, h_=h_, st=st: e.bn_stats(out=st[:, 0:6], in_=h_[:, 0:512]), [B_h], [B_st])
            OP("dve", lambda e, h_=h_, st=st: e.bn_stats(out=st[:, 6:12], in_=h_[:, 512:1024]), [B_h], [B_st])
            OP("dve", lambda e, st=st: e.bn_aggr(out=st[:, 12:14], in_=st[:, 0:12]), [B_st], [B_st])
            OP("act", lambda e, st=st: e.activation(out=st[:, 14:15], in_=st[:, 13:14], func=AF.Sqrt, bias=LN_EPS, scale=1.0), [B_st], [B_st])
            OP("dve", lambda e, st=st: e.reciprocal(out=st[:, 14:15], in_=st[:, 14:15]), [B_st], [B_st])
            OP("dve", lambda e, h_=h_, st=st: e.tensor_scalar(out=h_, in0=h_, scalar1=st[:, 12:13], scalar2=st[:, 14:15],
                                                              op0=ALU.subtract, op1=ALU.mult), [B_h, B_st], [B_h])
            OP("pool", lambda e, h_=h_: e.tensor_tensor(out=h_, in0=h_, in1=lng, op=ALU.mult), [B_h, B_lng], [B_h])
            OP("pool", lambda e, h_=h_: e.tensor_tensor(out=h_, in0=h_, in1=lnb, op=ALU.add), [B_h, B_lnb], [B_h])
            if last:
                P.dma("pool", y_out[r0 - 256:r0 - 128, :], h_, reads=[B_h], writes=[Buf("yo")], key=f"p4_h{sl}")
            else:
                P.dma("pool", XC[r0:r0 + 128, :], h_, reads=[B_h], writes=[Buf("xc")], key=f"p4_h{sl}")

    def phase_P1(layer):
            P.barrier()
            A.release(persist_mark)
            cT = A.alloc(16, "cT")
            sc = A.alloc(16, "sc")
            bm = A.alloc(3 * D, "bm")
            modrow = A.alloc(3 * D, "modrow")
            wm = [A.alloc(8 * 512, f"wm{i}") for i in range(2)]
            B_c, B_sc, B_bm, B_modrow = Buf("cT"), Buf("sc"), Buf("bm"), Buf("modrow")
            B_wm = [Buf("wm0"), Buf("wm1")]
            P.dma("sp", cT, cvT, writes=[B_c], key="cT")
            P.dma("sp", bm[0:2, :], b_mod2[layer], writes=[B_bm], key="bm")
            P.op("act", lambda e: e.activation(out=sc, in_=cT, func=AF.Silu), reads=[B_c], writes=[B_sc])
            wmv = w_mod[layer].rearrange("(k p) c -> p k c", p=128)
            for ch in range(6):
                slot = ch % 2
                P.dma("sp", wm[slot].rearrange("p (k c) -> p k c", k=8), wmv[:, :, ch * 512:(ch + 1) * 512],
                      writes=[B_wm[slot]], key=f"wm{slot}")
                bank = ch % 2

                def mm(e, slot=slot, bank=bank):
                    ins = None
                    for k in range(8):
                        ins = e.matmul(banks[bank][0:2, :], lhsT=sc[:, 2 * k:2 * k + 2],
                                       rhs=wm[slot][:, k * 512:(k + 1) * 512], start=(k == 0), stop=(k == 7))
                    return ins
                P.op("pe", mm, reads=[B_sc, B_wm[slot]], writes=[PSB[bank]])
                P.op("dve", lambda e, ch=ch, bank=bank: e.tensor_tensor(
                    out=modrow[0:2, ch * 512:(ch + 1) * 512], in0=banks[bank][0:2, :],
                    in1=bm[0:2, ch * 512:(ch + 1) * 512], op=ALU.add),
                    reads=[PSB[bank], B_bm], writes=[B_modrow])
            P.op("dve", lambda e: e.tensor_scalar_add(out=modrow[0:2, D:2 * D], in0=modrow[0:2, D:2 * D], scalar1=1.0),
                 reads=[B_modrow], writes=[B_modrow])
            for r in range(2):
                for ch in range(6):
                    bank = 2 + (ch % 2)
                    P.op("pe", lambda e, r=r, ch=ch, bank=bank: e.matmul(
                        banks[bank][:, :], lhsT=sel[0:2, r * 128:(r + 1) * 128],
                        rhs=modrow[0:2, ch * 512:(ch + 1) * 512], start=True, stop=True),
                        reads=[B_modrow, B_const], writes=[PSB[bank]])
                    P.op("act", lambda e, r=r, ch=ch, bank=bank: e.copy(
                        out=modb[r][:, ch * 512:(ch + 1) * 512], in_=banks[bank][:, :]),
                        reads=[PSB[bank]], writes=[B_modb])
    def phase_P2(layer):
            P.barrier()
            A.release(persist_mark)
            wtm = A.alloc(8 * NTM, "wtm")
            wfm = A.alloc(8 * NFM, "wfm")
            B_wtm, B_wfm = Buf("wtm"), Buf("wfm")
            wtm_v = w_tm[layer].rearrange("(k p) c -> p k c", p=128)
            wfm_v = w_fm[layer].rearrange("(k p) c -> p k c", p=128)
            for k in range(8):
                P.dma("sp", wtm[:, k * NTM:(k + 1) * NTM], wtm_v[:, k, :], writes=[B_wtm], key="wtm")
                P.dma("pool", wfm[:, k * NFM:(k + 1) * NFM], wfm_v[:, k, :], writes=[B_wfm], key="wfm")
                if FAST["p2"]:
                    P.op("dve", lambda e, k=k: e.tensor_copy(out=fr(wtm[:, k * NTM:(k + 1) * NTM]), in_=wtm[:, k * NTM:(k + 1) * NTM]),
                         reads=[B_wtm], writes=[B_wtm])
                    P.op("act", lambda e, k=k: e.copy(out=fr(wfm[:, k * NFM:(k + 1) * NFM]), in_=wfm[:, k * NFM:(k + 1) * NFM]),
                         reads=[B_wfm], writes=[B_wfm])
            NXS = 2
            xt = [A.alloc(D, f"xt{i}") for i in range(NXS)]
            B_xt = [Buf(f"xt{i}") for i in range(NXS)]
            stats = A.alloc(16, "stats")
            B_stats = Buf("stats")
            mlT = [A.alloc(8 * 512, f"mlT{i}") for i in range(2)]
            B_mlT = [Buf("mlT0"), Buf("mlT1")]
            tmst = [A.alloc(NTM, f"tmst{i}") for i in range(2)]
            B_tmst = [Buf("tmst0"), Buf("tmst1")]
            fmst = [A.alloc(512, f"fmst{i}") for i in range(2)]
            B_fmst = [Buf(f"fmst{i}") for i in range(2)]
            xsrc = x_all if layer == 0 else XC
            groups = [list(range(g * 4, min(g * 4 + 4, NT))) for g in range((NT + 3) // 4)]
            fm_i = 0
            for gi, tiles in enumerate(groups):
                ms = gi % 2
                ntok = len(tiles) * 128
                for ti, tile in enumerate(tiles):
                    xs = tile % NXS
                    r = 1 if tile < 2 else 0
                    xa = xt[xs]
                    P.dma("sp", xa, xsrc[tile * 128:(tile + 1) * 128, :], writes=[B_xt[xs]], key=f"xt{xs}")
                    P.op("dve", lambda e, xa=xa: e.bn_stats(out=stats[:, 0:6], in_=xa[:, 0:512]), reads=[B_xt[xs]], writes=[B_stats])
                    P.op("dve", lambda e, xa=xa: e.bn_stats(out=stats[:, 6:12], in_=xa[:, 512:1024]), reads=[B_xt[xs]], writes=[B_stats])
                    P.op("dve", lambda e: e.bn_aggr(out=stats[:, 12:14], in_=stats[:, 0:12]), reads=[B_stats], writes=[B_stats])
                    P.op("act", lambda e: e.activation(out=stats[:, 14:15], in_=stats[:, 13:14], func=AF.Sqrt, bias=LN_EPS, scale=1.0),
                         reads=[B_stats], writes=[B_stats])
                    P.op("dve", lambda e: e.reciprocal(out=stats[:, 14:15], in_=stats[:, 14:15]), reads=[B_stats], writes=[B_stats])
                    P.op("dve", lambda e, xa=xa: e.tensor_scalar(out=xa, in0=xa, scalar1=stats[:, 12:13], scalar2=stats[:, 14:15],
                                                                 op0=ALU.subtract, op1=ALU.mult), reads=[B_xt[xs], B_stats], writes=[B_xt[xs]])
                    P.op("pool", lambda e, xa=xa, r=r: e.tensor_tensor(out=xa, in0=xa, in1=modb[r][:, D:2 * D], op=ALU.mult),
                         reads=[B_xt[xs], B_modb], writes=[B_xt[xs]])
                    P.op("pool", lambda e, xa=xa, r=r: e.tensor_tensor(out=xa, in0=xa, in1=modb[r][:, 0:D], op=ALU.add),
                         reads=[B_xt[xs], B_modb], writes=[B_xt[xs]])
                    for half in range(2):
                        bank = half

                        def tr(e, xa=xa, half=half, bank=bank):
                            ins = None
                            for kk in range(4):
                                k = half * 4 + kk
                                ins = e.transpose(banks[bank][:, kk * 128:(kk + 1) * 128], xa[:, k * 128:(k + 1) * 128], ident)
                            return ins
                        P.op("pe", tr, reads=[B_xt[xs], B_const], writes=[PSB[bank]])
                        dst = mlT[ms].rearrange("p (k t) -> p k t", k=8)[:, half * 4:half * 4 + 4, ti * 128:(ti + 1) * 128]
                        P.op("act", lambda e, dst=dst, bank=bank: e.copy(
                            out=fr(dst, FAST["p2"]), in_=banks[bank][:, :].rearrange("p (k t) -> p k t", k=4)),
                            reads=[PSB[bank]], writes=[B_mlT[ms]])
                for ti, tile in enumerate(tiles):
                    ts_ = tile % 2
                    for ci, (c0, cw) in enumerate(((0, 512), (512, 512), (1024, 512), (1536, 16))):
                        bank = 2 + ci % 2

                        def mm(e, ti=ti, c0=c0, cw=cw, bank=bank, ms=ms):
                            ins = None
                            for k in range(8):
                                ins = e.matmul(banks[bank][:, 0:cw], lhsT=fr(mlT[ms][:, k * 512 + ti * 128:k * 512 + (ti + 1) * 128], FAST["p2"] and cw >= 256),
                                               rhs=fr(wtm[:, k * NTM + c0:k * NTM + c0 + cw], FAST["p2"] and cw >= 256), start=(k == 0), stop=(k == 7))
                            return ins
                        P.op("pe", mm, reads=[B_mlT[ms], B_wtm], writes=[PSB[bank]])
                        P.op("dve", lambda e, c0=c0, cw=cw, bank=bank, ts_=ts_: e.tensor_copy(
                            out=tmst[ts_][:, c0:c0 + cw], in_=banks[bank][:, 0:cw]), reads=[PSB[bank]], writes=[B_tmst[ts_]])
                    P.dma("act", TM[tile * 128:(tile + 1) * 128, :], tmst[ts_], reads=[B_tmst[ts_]], writes=[Buf("tmd")], key=f"tmst{ts_}")
                t0 = tiles[0] * 128
                for ft in range(19):
                    f0 = ft * 128
                    fw = 128 if ft < 18 else 32
                    bank = 4 + ft % 4
                    fs = fm_i % 2
                    fm_i += 1

                    def mm(e, f0=f0, fw=fw, bank=bank, ms=ms, ntok=ntok):
                        ins = None
                        for k in range(8):
                            ins = e.matmul(banks[bank][0:fw, 0:ntok], lhsT=fr(wfm[:, k * NFM + f0:k * NFM + f0 + fw], FAST["p2"]),
                                           rhs=fr(mlT[ms][:, k * 512:k * 512 + ntok], FAST["p2"]), start=(k == 0), stop=(k == 7))
                        return ins
                    P.op("pe", mm, reads=[B_mlT[ms], B_wfm], writes=[PSB[bank]])
                    eng = "act" if ft % 2 == 0 else "dve"
                    if eng == "act":
                        P.op("act", lambda e, fw=fw, bank=bank, fs=fs, ntok=ntok: e.copy(
                            out=fmst[fs][0:fw, 0:ntok], in_=banks[bank][0:fw, 0:ntok]), reads=[PSB[bank]], writes=[B_fmst[fs]])
                    else:
                        P.op("dve", lambda e, fw=fw, bank=bank, fs=fs, ntok=ntok: e.tensor_copy(
                            out=fmst[fs][0:fw, 0:ntok], in_=banks[bank][0:fw, 0:ntok]), reads=[PSB[bank]], writes=[B_fmst[fs]])
                    P.dma("pool", FM[f0:f0 + fw, t0:t0 + ntok], fmst[fs][0:fw, 0:ntok], reads=[B_fmst[fs]],
                          writes=[Buf("fmd")], key=f"fmst{fs}")

    PH = dict(P1=phase_P1, P2=phase_P2, A=phase_A, B=phase_B, C=phase_C, P4=phase_P4)
    for layer in layers:
        for ph in phases:
            PH[ph](layer)


    P.barrier()
    P.emit()
    return nc


_CONST = {}


def _consts():
    if not _CONST:
        _CONST["ident"] = np.eye(128, dtype=np.float32)
        sel = np.zeros((2, 256), np.float32)
        sel[0, 0:128] = 1.0
        sel[1, 128:256] = 1.0
        _CONST["sel"] = sel
        _CONST["ones"] = np.ones((128, 128), np.float32)
        jj, ii = np.meshgrid(np.arange(128), np.arange(128), indexing="ij")
        _CONST["tril"] = (jj <= ii).astype(np.float32)
        _CONST["triu"] = (jj >= ii).astype(np.float32)
        hm = np.zeros((128, 8), np.float32)
        hm[np.arange(128), np.arange(128) // 32] = 1.0
        _CONST["hm"] = hm
        _CONST["bd"] = (np.arange(128)[:, None] // 32 == np.arange(256)[None, :] // 64).astype(np.float32)
        t = np.arange(SEQ)
        inv = (10000.0 ** (-np.arange(8, dtype=np.float32) / 8)).astype(np.float32)
        cosf = np.zeros((SEQ, 4, 2, 2, 8), np.float32)
        sinf = np.zeros((SEQ, 4, 2, 2, 8), np.float32)
        for half, pos in enumerate((t // 64, t % 64)):
            ang = pos.astype(np.float32)[:, None] * inv[None, :]
            c_, s_ = np.cos(ang).astype(np.float32), np.sin(ang).astype(np.float32)
            cosf[:, :, half, 0, :] = c_[:, None, :]
            cosf[:, :, half, 1, :] = c_[:, None, :]
            sinf[:, :, half, 0, :] = -s_[:, None, :]
            sinf[:, :, half, 1, :] = s_[:, None, :]
        _CONST["ropec"] = cosf.reshape(SEQ, 128)
        pj, fi = jj, ii
        offu = []
        for sz_ in (1, 2, 4, 8, 16, 32, 64):
            offu.append(((pj // (2 * sz_) == fi // (2 * sz_)) & (pj % (2 * sz_) < sz_) & (fi % (2 * sz_) >= sz_)).astype(np.float32))
        offl = [m_.T for m_ in offu]
        _CONST["offs"] = np.ascontiguousarray(np.concatenate(offu + offl, 1))
        negf = np.where(pj <= fi, 0.0, -30000.0).astype(np.float32)
        negb = np.where(pj >= fi, 0.0, -30000.0).astype(np.float32)
        _CONST["negm"] = np.ascontiguousarray(np.concatenate([np.tile(negf, (1, 4)), np.tile(negb, (1, 4))], 1))
        _CONST["offd"] = (pj != fi).astype(np.float32)
        _CONST["ropes"] = sinf.reshape(SEQ, 128)
    return _CONST


def make_in_maps(inputs):
    x, c, ctx, c_ctx = (np.asarray(inputs[k], np.float32) for k in ("x", "c", "ctx", "c_ctx"))
    w_in = np.asarray(inputs["w_in"], np.float32)
    cs = _consts()
    sl = lambda a, b: list(range(a, b))
    tm_cols = sl(1552, 2064) + sl(2576, 2832) + sl(3088, 3216) + sl(3216, 3344) + sl(3344, 3600) + sl(3632, 3888) + sl(1536, 1552)
    fm_cols = sl(0, 1536) + sl(2064, 2320) + sl(2320, 2576) + sl(2832, 3088) + sl(3600, 3632)
    assert len(tm_cols) == NTM and len(fm_cols) == NFM
    w_tm = np.ascontiguousarray(w_in[:, :, tm_cols])
    w_fm = np.ascontiguousarray(w_in[:, :, fm_cols])
    b_mod = np.asarray(inputs["b_mod"], np.float32)
    b_mod2 = np.ascontiguousarray(np.broadcast_to(b_mod[:, None, :], (DEPTH, 2, 3 * D)))
    f = lambda k: np.asarray(inputs[k], np.float32)
    shared = dict(w_mod=f("w_mod"), b_mod2=b_mod2, w_tm=w_tm, w_fm=w_fm, w_out=f("w_out"))
    for k in ("ident", "sel", "ones", "tril", "triu", "hm", "bd", "ropec", "ropes"):
        shared[k] = cs[k]
    w2 = f("gla_w2")
    w2p = np.zeros((DEPTH, 32, 256), np.float32)
    w2p[:, 0:16, 0:128] = w2[:, 0]
    w2p[:, 16:32, 128:256] = w2[:, 1]
    shared["w2p"] = w2p
    shared["b2r"] = np.ascontiguousarray(f("gla_b2").reshape(DEPTH, 1, 256))
    shared["glan"] = np.ascontiguousarray(np.broadcast_to(np.tile(f("gla_norm"), (1, 4))[:, None, :], (DEPTH, 128, 256)))
    shared["gdnn"] = np.ascontiguousarray(np.broadcast_to(np.tile(f("gdn_norm"), (1, 4))[:, None, :], (DEPTH, 128, 512)))
    shared["lng"] = np.ascontiguousarray(np.broadcast_to(f("ln_g")[:, None, :], (DEPTH, 128, D)))
    shared["lnb"] = np.ascontiguousarray(np.broadcast_to(f("ln_b")[:, None, :], (DEPTH, 128, D)))
    rpb = f("rpb")
    kc = np.arange(64)[:, None]
    qc = np.arange(64)[None, :]
    c0_ = np.clip(qc - 8, 0, 48)
    ok = (kc >= c0_) & (kc < c0_ + 16)
    dj = np.clip(kc - qc + 15, 0, 30)
    nab = np.full((DEPTH, 64, 4, 15, 64), -30000.0, np.float32)
    for e_ in range(15):
        blk = rpb[:, :, 14 - e_, :][:, :, dj]
        blk = np.where(ok[None, None], blk, np.float32(-30000.0))
        nab[:, :, :, e_, :] = blk.transpose(0, 2, 1, 3)
    nab = nab.reshape(DEPTH, 64, 3840)
    shared["nab"] = np.ascontiguousarray(np.concatenate([nab, nab], 1))
    cwv = f("conv_w")
    shared["convw"] = np.ascontiguousarray(cwv.reshape(DEPTH, 5, 12, 128).transpose(0, 3, 2, 1).reshape(DEPTH, 128, 60))
    shared["alog8"] = np.ascontiguousarray(np.broadcast_to(f("a_log").reshape(DEPTH, 1, 8), (DEPTH, 128, 8)))
    shared["dtb8"] = np.ascontiguousarray(np.broadcast_to(f("dt_bias").reshape(DEPTH, 1, 8), (DEPTH, 128, 8)))
    for k in ("offs", "negm", "offd"):
        shared[k] = cs[k]
    maps = []
    for b in range(8):
        cv = np.stack([c[b], c_ctx], 0)
        cvT = np.ascontiguousarray(cv.reshape(2, 8, 128).transpose(2, 1, 0).reshape(128, 16))
        m = dict(shared)
        m["x_all"] = np.ascontiguousarray(np.concatenate([ctx[b], x[b]], 0))
        m["cvT"] = cvT
        maps.append(m)
    return maps


_NC = {}


def kernel(**inputs):
    if "nc" not in _NC:
        _NC["nc"] = build_program()
    maps = make_in_maps(inputs)
    res = run_bass_kernel_spmd(_NC["nc"], maps, core_ids=list(range(8)))
    return np.stack([np.asarray(r["y"], np.float32) for r in res.results], 0)
```

```python
import numpy as np
from contextlib import ExitStack
import concourse.bass as bass
import concourse.mybir as mybir
from concourse.bass_utils import run_bass_kernel_spmd

F32 = mybir.dt.float32
AF = mybir.ActivationFunctionType
ALU = mybir.AluOpType
AX = mybir.AxisListType

D = 1024
SEQ = 4096
CTX = 256
T = SEQ + CTX
NT = T // 128
DEPTH = 2
NTM = 1552
NFM = 2336
DN_ALPHA = (2 * DEPTH) ** 0.25
LN_EPS = 1e-6
SEM_ROT = 30000
DBG = {}
MERGE_C = True
CONV_ON_DVE = True
SWDGE_TO = "act"
FUSE_FIN = True
F32R = mybir.dt.float32r
FAST = dict(p2=False, p4=False, b=False, a0=False)


def fr(ap, on=True):
    return ap.bitcast(F32R) if on else ap

TM_AZ, TM_BV, TM_CQ, TM_CK, TM_CV, TM_CZ, TM_AB = 0, 512, 768, 896, 1024, 1280, 1536
FM_AQKV, FM_BQ, FM_BK, FM_BZ, FM_CR = 0, 1536, 1792, 2048, 2304


class Buf:
    __slots__ = ("name", "w", "r")

    def __init__(self, name):
        self.name = name
        self.w = None
        self.r = {}


class Prog:
    ENGS = ("pe", "act", "dve", "pool", "sp")

    def __init__(self, nc):
        self.nc = nc
        self.streams = {e: [] for e in self.ENGS}
        self.sems = []
        self.cur = {}
        self.cnt = {}
        self.waited = {e: {} for e in self.ENGS}
        self.dma_sem = {}
        self.dma_cnt = {}
        self.eng_of_sem = {}
        self.rec = None
        for e in self.ENGS:
            self._new_eng_sem(e)

    def record(self):
        assert self.rec is None
        self.rec = []

    def stop(self):
        r, self.rec = self.rec, None
        return r

    def play(self, lists, lead=None):
        idx = [0] * len(lists)
        lead = lead or [0] * len(lists)
        for i, l in enumerate(lists):
            for _ in range(min(lead[i], len(l))):
                l[idx[i]]()
                idx[i] += 1
        total = sum(len(l) - idx[i] for i, l in enumerate(lists))
        for _ in range(total):
            best, bf = None, None
            for i, l in enumerate(lists):
                if idx[i] < len(l):
                    fr = (idx[i] - lead[i]) / max(1, len(l) - lead[i])
                    if bf is None or fr < bf:
                        best, bf = i, fr
            lists[best][idx[best]]()
            idx[best] += 1

    def _new_sem(self, name):
        s = self.nc.alloc_semaphore(name)
        self.sems.append(s)
        return len(self.sems) - 1

    def _new_eng_sem(self, e):
        i = self._new_sem(f"s_{e}_{len(self.sems)}")
        self.cur[e] = i
        self.cnt[e] = 0
        self.eng_of_sem[i] = e

    def _deps(self, engine, reads, writes):
        deps = {}

        def add(tok, raw):
            if tok is None:
                return
            s, v = tok
            if (not raw) and self.eng_of_sem.get(s) == engine:
                return
            if deps.get(s, 0) < v:
                deps[s] = v

        for b in reads:
            add(b.w, True)
        for b in writes:
            add(b.w, False)
            for s, v in b.r.items():
                add((s, v), False)
        w = self.waited[engine]
        out = []
        for s, v in deps.items():
            if w.get(s, 0) < v:
                w[s] = v
                out.append((s, v))
        return out

    def _commit(self, tok, reads, writes):
        s, v = tok
        for b in reads:
            if b.r.get(s, 0) < v:
                b.r[s] = v
        for b in writes:
            b.w = tok
            b.r = {}

    def op(self, engine, fn, reads=(), writes=()):
        if self.rec is not None:
            self.rec.append(lambda: self.op(engine, fn, reads, writes))
            return
        waits = self._deps(engine, reads, writes)
        if self.cnt[engine] >= SEM_ROT:
            self._new_eng_sem(engine)
        self.cnt[engine] += 1
        tok = (self.cur[engine], self.cnt[engine])
        self._commit(tok, reads, writes)
        self.streams[engine].append((waits, fn, tok, 1))

    def dma(self, queue, out_ap, in_ap, reads=(), writes=(), key=None, **kw):
        if queue == "pool":
            queue = SWDGE_TO
        if self.rec is not None:
            self.rec.append(lambda: self.dma(queue, out_ap, in_ap, reads, writes, key, **kw))
            return
        waits = self._deps(queue, reads, writes)
        if key not in self.dma_sem or self.dma_cnt[key] >= SEM_ROT * 16:
            self.dma_sem[key] = self._new_sem(f"d_{len(self.sems)}")
            self.dma_cnt[key] = 0
        self.dma_cnt[key] += 16
        tok = (self.dma_sem[key], self.dma_cnt[key])
        self._commit(tok, reads, writes)
        self.streams[queue].append((waits, lambda e: e.dma_start(out=out_ap, in_=in_ap, **kw), tok, 16))

    def dma_group(self, queue, items, key):
        if self.rec is not None:
            self.rec.append(lambda: self.dma_group(queue, items, key))
            return
        for out_ap, in_ap, writes in items:
            self.dma(queue, out_ap, in_ap, writes=writes, key=key)
        last = (self.dma_sem[key], self.dma_cnt[key])
        for _, _, writes in items:
            for b in writes:
                b.w = last

    def barrier(self):
        toks = [(self.cur[e], self.cnt[e]) for e in self.ENGS if self.cnt[e] > 0]
        toks += [(self.dma_sem[k], self.dma_cnt[k]) for k in self.dma_sem]
        for e in self.ENGS:
            w = self.waited[e]
            waits = []
            for s, v in toks:
                if self.eng_of_sem.get(s) == e:
                    continue
                if w.get(s, 0) < v:
                    w[s] = v
                    waits.append((s, v))
            if waits:
                self.streams[e].append((waits, None, None, 0))

    def final_wait(self, engine, bufs):
        waits = self._deps(engine, bufs, ())
        self.streams[engine].append((waits, None, None, 0))

    def emit(self):
        nc = self.nc
        with nc.Block() as block:
            def run(name):
                def body(eng):
                    for waits, fn, tok, inc in self.streams[name]:
                        for s, v in waits:
                            eng.wait_ge(self.sems[s], v)
                        if fn is not None:
                            ins = fn(eng)
                            ins.then_inc(self.sems[tok[0]], inc)
                return body
            block.tensor(run("pe"))
            block.scalar(run("act"))
            block.vector(run("dve"))
            block.gpsimd(run("pool"))
            block.sync(run("sp"))


class Arena:
    def __init__(self, sb, ncols):
        self.sb = sb
        self.ncols = ncols
        self.off = 0

    def alloc(self, n, name="t"):
        n = (n + 7) // 8 * 8
        assert self.off + n <= self.ncols, f"SBUF arena overflow at {name}: {self.off}+{n}>{self.ncols}"
        ap = self.sb[:, self.off:self.off + n]
        self.off += n
        return ap

    def mark(self):
        return self.off

    def release(self, m):
        self.off = m


def build_program(layers=(0, 1), phases=("P1", "P2", "A", "B", "P4"), dbg_in=(), dbg_out=()):
    nc = bass.Bass("TRN2", target_bir_lowering=False)
    P = Prog(nc)
    dbg = False

    def din(name, shape):
        return nc.dram_tensor(name, list(shape), F32, kind="ExternalInput").ap()

    def dscr(name, shape, out=False):
        kind = "ExternalOutput" if (out or name in dbg_out) else ("ExternalInput" if name in dbg_in else "Internal")
        return nc.dram_tensor(name, list(shape), F32, kind=kind).ap()

    x_all = din("x_all", (T, D))
    cvT = din("cvT", (128, 16))
    w_mod = din("w_mod", (DEPTH, D, 3 * D))
    b_mod2 = din("b_mod2", (DEPTH, 2, 3 * D))
    w_tm = din("w_tm", (DEPTH, D, NTM))
    w_fm = din("w_fm", (DEPTH, D, NFM))
    ident_d = din("ident", (128, 128))
    sel_d = din("sel", (2, 256))

    y_out = dscr("y", (SEQ, D), out=True)
    TM = dscr("TM", (T, NTM))
    FM = dscr("FM", (NFM, T))
    XC = dscr("XC", (T, D))
    MIXA = dscr("MIXA", (T, 512))
    MIXB = dscr("MIXB", (256, T))
    MIXC = dscr("MIXC", (T, 256))

    ones_d = din("ones", (128, 128))
    tril_d = din("tril", (128, 128))
    triu_d = din("triu", (128, 128))
    hm_d = din("hm", (128, 8))
    bd_d = din("bd", (128, 256))
    ropec_d = din("ropec", (SEQ, 128))
    ropes_d = din("ropes", (SEQ, 128))
    w2p_d = din("w2p", (DEPTH, 32, 256))
    b2r_d = din("b2r", (DEPTH, 1, 256))
    glan_d = din("glan", (DEPTH, 128, 256))
    gdnn_d = din("gdnn", (DEPTH, 128, 512))
    lng_d = din("lng", (DEPTH, 128, D))
    lnb_d = din("lnb", (DEPTH, 128, D))
    w_out = din("w_out", (DEPTH, D, D))
    nab_d = din("nab", (DEPTH, 128, 3840))
    convw_d = din("convw", (DEPTH, 128, 60))
    offs_d = din("offs", (128, 14 * 128))
    negm_d = din("negm", (128, 1024))
    offd_d = din("offd", (128, 128))
    alog_d = din("alog8", (DEPTH, 128, 8))
    dtb_d = din("dtb8", (DEPTH, 128, 8))
    QKV = dscr("QKV", (T, 1536))
    OAF = dscr("OAF", (T, 512))
    OAB = dscr("OAB", (T, 512))
    OCF = dscr("OCF", (T, 256))
    OCB = dscr("OCB", (T, 256))

    NCOLS = 53000
    sb_h = nc.alloc_sbuf_tensor("sb", [128, NCOLS], F32)
    A = Arena(sb_h, NCOLS)
    banks = [nc.alloc_psum_tensor(f"ps{i}", [128, 512], F32) for i in range(8)]
    PSB = [Buf(f"psum{i}") for i in range(8)]

    ident = A.alloc(128, "ident")
    sel = A.alloc(256, "sel")
    B_const = Buf("const")
    P.dma("sp", ident, ident_d, writes=[B_const], key="const")
    P.dma("sp", sel[0:2, :], sel_d, writes=[B_const], key="const")
    ones = A.alloc(128, "ones")
    tril = A.alloc(128, "tril")
    triu = A.alloc(128, "triu")
    hm = A.alloc(8, "hm")
    bd = A.alloc(256, "bd")
    P.dma("sp", ones, ones_d, writes=[B_const], key="const")
    P.dma("sp", tril, tril_d, writes=[B_const], key="const")
    P.dma("sp", triu, triu_d, writes=[B_const], key="const")
    P.dma("sp", hm, hm_d, writes=[B_const], key="const")
    P.dma("sp", bd, bd_d, writes=[B_const], key="const")
    modb = [A.alloc(3 * D, f"modb{r}") for r in range(2)]
    B_modb = Buf("modb")
    persist_mark = A.mark()

    out_bufs = []


    def mk(n, name, slots=1):
        return [(A.alloc(n, f"{name}{i}"), Buf(f"{name}{i}")) for i in range(slots)]

    def OP(eng, fn, r, w):
        P.op(eng, fn, reads=r, writes=w)

    def phase_A(layer):
        P.barrier()
        A.release(persist_mark)
        (cw, B_cw), = mk(64, "cw")
        P.dma("sp", cw[:, 0:60], convw_d[layer], writes=[B_cw], key="a_cw")
        dgs = mk(5 * 128, "dg", 2)
        xps = mk(4360, "xp", 2)
        yss = mk(512, "ys", 2)
        stgs = mk(512, "stg", 2)
        for xp, B_xp in xps:
            OP("pool", lambda e, xp=xp: e.memset(xp, 0.0), [], [B_xp])
        blocks = [(0, 256, 0)] + [(256 + 512 * b, 512, 260 + 512 * b) for b in range(8)]
        bi = 0
        pend = []
        for ct in range(12):
            sl = ct % 2
            xp, B_xp = xps[sl]
            dg, B_dg = dgs[sl]
            P.dma("sp", xp[:, 2:258], FM[ct * 128:(ct + 1) * 128, 0:256], writes=[B_xp], key=f"a_xp{sl}")
            P.dma("pool", xp[:, 262:4358], FM[ct * 128:(ct + 1) * 128, 256:T], writes=[B_xp], key=f"a_xp{sl}")
            for j in range(0 if CONV_ON_DVE else 5):
                OP("dve", lambda e, dg=dg, j=j, ct=ct: e.tensor_scalar(out=dg[:, j * 128:(j + 1) * 128], in0=ident,
                                                                        scalar1=cw[:, ct * 5 + j:ct * 5 + j + 1], scalar2=None, op0=ALU.mult),
                   [B_const, B_cw], [B_dg])
            for (t0, ntok, cb) in blocks:
                bs = bi % 2
                bi += 1
                ys, B_ys = yss[bs]
                stg, B_stg = stgs[bs]
                if CONV_ON_DVE:
                    OP("dve", lambda e, ys=ys, xp=xp, cb=cb, ntok=ntok, ct=ct: e.tensor_scalar(
                        out=ys[:, 0:ntok], in0=xp[:, cb:cb + ntok], scalar1=cw[:, ct * 5:ct * 5 + 1], scalar2=None, op0=ALU.mult), [B_xp, B_cw], [B_ys])
                    for j in range(1, 5):
                        OP("dve", lambda e, ys=ys, xp=xp, cb=cb, ntok=ntok, ct=ct, j=j: e.scalar_tensor_tensor(
                            out=ys[:, 0:ntok], in0=xp[:, cb + j:cb + j + ntok], scalar=cw[:, ct * 5 + j:ct * 5 + j + 1], in1=ys[:, 0:ntok],
                            op0=ALU.mult, op1=ALU.add), [B_xp, B_cw, B_ys], [B_ys])
                    OP("act", lambda e, ys=ys, ntok=ntok: e.activation(out=ys[:, 0:ntok], in_=ys[:, 0:ntok], func=AF.Silu), [B_ys], [B_ys])
                else:
                    def mmc(e, dg=dg, xp=xp, cb=cb, ntok=ntok, bs=bs):
                        ins = None
                        for j in range(5):
                            ins = e.matmul(banks[bs][:, 0:ntok], lhsT=dg[:, j * 128:(j + 1) * 128], rhs=xp[:, cb + j:cb + j + ntok],
                                           start=(j == 0), stop=(j == 4))
                        return ins
                    OP("pe", mmc, [B_dg, B_xp], [PSB[bs]])
                    OP("act", lambda e, ys=ys, bs=bs, ntok=ntok: e.activation(out=ys[:, 0:ntok], in_=banks[bs][:, 0:ntok], func=AF.Silu), [PSB[bs]], [B_ys])

                def tail(ys=ys, B_ys=B_ys, stg=stg, B_stg=B_stg, bs=bs, ntok=ntok, t0=t0, ct=ct):
                    def trc(e):
                        ins = None
                        for b in range(ntok // 128):
                            ins = e.transpose(banks[2 + bs][:, b * 128:(b + 1) * 128], ys[:, b * 128:(b + 1) * 128], ident)
                        return ins
                    OP("pe", trc, [B_ys, B_const], [PSB[2 + bs]])
                    OP("act", lambda e: e.copy(out=stg[:, 0:ntok], in_=banks[2 + bs][:, 0:ntok]), [PSB[2 + bs]], [B_stg])
                    P.dma("pool", QKV[t0:t0 + ntok, ct * 128:(ct + 1) * 128].rearrange("(b p) c -> p b c", p=128),
                          stg[:, 0:ntok].rearrange("p (b c) -> p b c", c=128), reads=[B_stg], writes=[Buf("qkvd")], key=f"a_stg{bs}")
                if pend:
                    pend.pop()()
                pend.append(tail)
        while pend:
            pend.pop()()

        P.barrier()
        A.release(persist_mark)
        (offs, B_offs), = mk(14 * 128, "offs")
        (negm, B_negm), = mk(2 * 512, "negm")
        (offd, B_offd), = mk(128, "offd")
        (al8, B_al8), = mk(8, "al8")
        (dt8, B_dt8), = mk(8, "dt8")
        P.dma("sp", offs, offs_d, writes=[B_offs], key="a_offs")
        P.dma("sp", negm, negm_d, writes=[B_negm], key="a_negm")
        P.dma("sp", offd, offd_d, writes=[B_offd], key="a_offd")
        P.dma("sp", al8, alog_d[layer], writes=[B_al8], key="a_al8")
        P.dma("sp", dt8, dtb_d[layer], writes=[B_dt8], key="a_dt8")
        OP("act", lambda e: e.activation(out=al8, in_=al8, func=AF.Exp), [B_al8], [B_al8])
        OP("dve", lambda e: e.tensor_scalar(out=al8, in0=al8, scalar1=-1.0, scalar2=None, op0=ALU.mult), [B_al8], [B_al8])
        TL = {}
        for nm, n in (("qkv", 1536), ("ab", 16), ("sq", 1024), ("st", 64), ("qkn", 1024), ("kqT", 1024), ("R", 512), ("DT", 512),
                      ("DTs", 512), ("N", 512), ("NT", 512), ("D", 512), ("Dt", 512), ("T1", 512), ("T2", 512),
                      ("qkT", 512), ("KG", 512), ("nW", 512), ("kdec", 512), ("vnew", 512), ("ot", 512), ("t2", 512)):
            TL[nm] = mk(n, "a" + nm, 4 if nm in ("qkv", "ab") else 2)
        TL["m1"], TL["m2"] = TL["T1"], TL["T2"]
        Ss = [mk(512, f"aS{d_}", 2) for d_ in range(2)]
        v3 = lambda ap: ap.rearrange("p (h c) -> p h c", h=4)
        hs_ = lambda ap: (lambda h: ap[:, h * 128:(h + 1) * 128])

        def a_load(tile, d, ql):
            r0 = tile * 128
            qkv, B_qkv = TL["qkv"][ql]
            ab, B_ab = TL["ab"][ql]
            P.dma("sp", qkv, QKV[r0:r0 + 128, :], writes=[B_qkv], key=f"a_qkv{ql}")
            P.dma("sp", ab, TM[r0:r0 + 128, TM_AB:TM_AB + 16], writes=[B_ab], key=f"a_ab{ql}")

        def a_chunk(tile, d, ql, S, B_S, Sn, B_Sn):
            r0 = tile * 128
            sl = d
            g = lambda nm: TL[nm][sl]
            qkv, B_qkv = TL["qkv"][ql]; ab, B_ab = TL["ab"][ql]; sq, B_sq = g("sq"); st, B_st = g("st")
            qkn, B_qkn = g("qkn"); kqT, B_kqT = g("kqT"); R_, B_R = g("R"); DT, B_DT = g("DT"); DTs, B_DTs = g("DTs")
            N_, B_N = g("N"); NT_, B_NT = g("NT"); Dm, B_Dm = g("D"); Dt, B_Dt = g("Dt"); T1, B_T1 = g("T1"); T2, B_T2 = g("T2")
            m1, B_m1 = g("m1"); m2, B_m2 = g("m2"); qkT, B_qkT = g("qkT"); KG, B_KG = g("KG"); nW, B_nW = g("nW")
            kdec, B_kdec = g("kdec"); vnew, B_vnew = g("vnew"); ot, B_ot = g("ot"); t2_, B_t2 = g("t2")
            bk = (lambda lb: 3 * sl + (0, 1, 2, 0, 1, 2, 0, 1, 2)[lb]) if MERGE_C else (lambda lb: 4 * sl + (0, 1, 2, 3, 0, 1, 2, 3, 0)[lb])
            BK = lambda lb: banks[bk(lb)]
            PB = lambda lb: PSB[bk(lb)]

            def mm4(lb, lf, rf, reads, start=True, stop=True):
                def f(e):
                    ins = None
                    for h in range(4):
                        ins = e.matmul(BK(lb)[:, h * 128:(h + 1) * 128], lhsT=lf(h), rhs=rf(h), start=start, stop=stop)
                    return ins
                OP("pe", f, reads, [PB(lb)])

            TR = tril if d == 0 else triu
            offN = (lambda li: offs[:, li * 128:(li + 1) * 128]) if d == 0 else (lambda li: offs[:, (7 + li) * 128:(8 + li) * 128])
            offT = (lambda li: offs[:, (7 + li) * 128:(8 + li) * 128]) if d == 0 else (lambda li: offs[:, li * 128:(li + 1) * 128])
            ngm = negm[:, d * 512:(d + 1) * 512]
            b4 = lambda ap: ap.unsqueeze(1).to_broadcast([128, 4, 128])
            c4 = lambda ap: ap.unsqueeze(2).to_broadcast([128, 4, 128])
            OP("act", lambda e: e.activation(out=sq, in_=qkv[:, 0:1024], func=AF.Square), [B_qkv], [B_sq])
            OP("dve", lambda e: e.reduce_sum(out=st[:, 0:8], in_=sq.rearrange("p (g c) -> p g c", g=8), axis=AX.X), [B_sq], [B_st])
            OP("act", lambda e: e.activation(out=st[:, 0:8], in_=st[:, 0:8], func=AF.Sqrt, bias=1e-6, scale=1.0), [B_st], [B_st])
            OP("dve", lambda e: e.reciprocal(out=st[:, 0:8], in_=st[:, 0:8]), [B_st], [B_st])
            OP("dve", lambda e: e.tensor_scalar(out=st[:, 0:4], in0=st[:, 0:4], scalar1=128 ** -0.5, scalar2=None, op0=ALU.mult), [B_st], [B_st])
            OP("dve", lambda e: e.tensor_tensor(out=qkn.rearrange("p (g c) -> p g c", g=8), in0=qkv[:, 0:1024].rearrange("p (g c) -> p g c", g=8),
                                                in1=st[:, 0:8].unsqueeze(2).to_broadcast([128, 8, 128]), op=ALU.mult), [B_qkv, B_st], [B_qkn])
            qn, kn, v_ = qkn[:, 0:512], qkn[:, 512:1024], qkv[:, 1024:1536]
            OP("dve", lambda e: e.tensor_tensor(out=st[:, 8:12], in0=ab[:, d * 4:d * 4 + 4], in1=dt8[:, d * 4:d * 4 + 4], op=ALU.add), [B_ab, B_dt8], [B_st])
            OP("act", lambda e: e.activation(out=st[:, 8:12], in_=st[:, 8:12], func=AF.Exp), [B_st], [B_st])
            OP("act", lambda e: e.activation(out=st[:, 8:12], in_=st[:, 8:12], func=AF.Ln, bias=1.0, scale=1.0), [B_st], [B_st])
            OP("dve", lambda e: e.tensor_tensor(out=st[:, 8:12], in0=st[:, 8:12], in1=al8[:, d * 4:d * 4 + 4], op=ALU.mult), [B_st, B_al8], [B_st])
            OP("act", lambda e: e.activation(out=st[:, 12:16], in_=ab[:, 8 + d * 4:12 + d * 4], func=AF.Exp, scale=-1.0), [B_ab], [B_st])
            OP("dve", lambda e: e.tensor_scalar_add(out=st[:, 12:16], in0=st[:, 12:16], scalar1=1.0), [B_st], [B_st])
            OP("dve", lambda e: e.reciprocal(out=st[:, 12:16], in_=st[:, 12:16]), [B_st], [B_st])
            g_, beta = st[:, 8:12], st[:, 12:16]
            def mmg(e):
                e.matmul(BK(0)[:, 0:4], lhsT=TR, rhs=g_, start=True, stop=True)
                return e.matmul(BK(0)[:, 4:8], lhsT=ones, rhs=g_, start=True, stop=True)
            OP("pe", mmg, [B_st, B_const], [PB(0)])
            OP("act", lambda e: e.copy(out=st[:, 16:24], in_=BK(0)[:, 0:8]), [PB(0)], [B_st])
            OP("dve", lambda e: e.tensor_scalar(out=st[:, 24:28], in0=st[:, 16:20], scalar1=-1.0, scalar2=None, op0=ALU.mult), [B_st], [B_st])
            OP("act", lambda e: e.activation(out=st[:, 28:32], in_=st[:, 16:20], func=AF.Exp), [B_st], [B_st])
            OP("dve", lambda e: e.tensor_tensor(out=st[:, 32:36], in0=st[:, 20:24], in1=st[:, 16:20], op=ALU.subtract), [B_st], [B_st])
            OP("act", lambda e: e.activation(out=st[:, 32:36], in_=st[:, 32:36], func=AF.Exp), [B_st], [B_st])
            OP("act", lambda e: e.activation(out=st[:, 36:40], in_=st[:, 20:24], func=AF.Exp), [B_st], [B_st])
            negGc, expG, kds, glast = st[:, 24:28], st[:, 28:32], st[:, 32:36], st[:, 36:40]
            OP("dve", lambda e: e.tensor_tensor(out=v3(R_), in0=b4(TR), in1=c4(g_), op=ALU.mult), [B_const, B_st], [B_R])
            def mmb(e):
                e.matmul(BK(1)[:, :], lhsT=ones, rhs=R_, start=True, stop=False)
                return e.matmul(BK(1)[:, :], lhsT=ident, rhs=ngm, start=False, stop=True)
            OP("pe", mmb, [B_R, B_const, B_negm], [PB(1)])
            for h in range(4):
                OP("act", lambda e, h=h: e.activation(out=DT[:, h * 128:(h + 1) * 128], in_=BK(1)[:, h * 128:(h + 1) * 128], func=AF.Exp,
                                                      bias=negGc[:, h:h + 1], scale=1.0), [PB(1), B_st], [B_DT])
            def trk(e):
                ins = None
                for h in range(4):
                    ins = e.transpose(BK(2)[:, h * 128:(h + 1) * 128], kn[:, h * 128:(h + 1) * 128], ident)
                for h in range(4):
                    ins = e.transpose(BK(3)[:, h * 128:(h + 1) * 128], qn[:, h * 128:(h + 1) * 128], ident)
                return ins
            OP("pe", trk, [B_qkn, B_const], [PB(2), PB(3)])
            OP("act", lambda e: e.copy(out=kqT[:, 0:512], in_=BK(2)[:, :]), [PB(2)], [B_kqT])
            OP("act", lambda e: e.copy(out=kqT[:, 512:1024], in_=BK(3)[:, :]), [PB(3)], [B_kqT])
            kT, qT = kqT[:, 0:512], kqT[:, 512:1024]
            mm4(4, hs_(kT), hs_(kT), [B_kqT])
            mm4(5, hs_(kT), hs_(qT), [B_kqT])
            for h in range(4):
                OP("dve", lambda e, h=h: e.scalar_tensor_tensor(out=N_[:, h * 128:(h + 1) * 128], in0=BK(4)[:, h * 128:(h + 1) * 128],
                                                                scalar=beta[:, h:h + 1], in1=DT[:, h * 128:(h + 1) * 128], op0=ALU.mult, op1=ALU.mult),
                   [PB(4), B_st, B_DT], [B_N])
            OP("dve", lambda e: e.tensor_tensor(out=qkT, in0=BK(5)[:, :], in1=DT, op=ALU.mult), [PB(5), B_DT], [B_qkT])
            def trn(e):
                ins = None
                for h in range(4):
                    ins = e.transpose(BK(6)[:, h * 128:(h + 1) * 128], N_[:, h * 128:(h + 1) * 128], ident)
                return ins
            OP("pe", trn, [B_N, B_const], [PB(6)])
            OP("act", lambda e: e.copy(out=NT_, in_=BK(6)[:, :]), [PB(6)], [B_NT])
            OP("dve", lambda e: e.tensor_tensor(out=v3(m1), in0=v3(N_), in1=b4(offN(0)), op=ALU.mult), [B_N, B_offs], [B_m1])
            OP("dve", lambda e: e.tensor_tensor(out=v3(m2), in0=v3(NT_), in1=b4(offT(0)), op=ALU.mult), [B_NT, B_offs], [B_m2])
            OP("dve", lambda e: e.tensor_tensor(out=v3(Dm), in0=b4(ident), in1=v3(m1), op=ALU.subtract), [B_const, B_m1], [B_Dm])
            OP("dve", lambda e: e.tensor_tensor(out=v3(Dt), in0=b4(ident), in1=v3(m2), op=ALU.subtract), [B_const, B_m2], [B_Dt])
            for li in range(1, 7):
                lastl = li == 6
                mm4(2, hs_(NT_), hs_(Dm), [B_NT, B_Dm])
                OP("dve", lambda e, li=li: e.tensor_tensor(out=v3(T1), in0=v3(BK(2)[:, :]), in1=b4(offN(li)), op=ALU.mult), [PB(2), B_offs], [B_T1])
                mm4(4, hs_(Dt), hs_(T1), [B_Dt, B_T1])
                OP("dve", lambda e: e.tensor_tensor(out=Dm, in0=Dm, in1=BK(4)[:, :], op=ALU.subtract), [B_Dm, PB(4)], [B_Dm])
                if not lastl:
                    def trd(e):
                        ins = None
                        for h in range(4):
                            ins = e.transpose(BK(3)[:, h * 128:(h + 1) * 128], Dm[:, h * 128:(h + 1) * 128], ident)
                        return ins
                    OP("pe", trd, [B_Dm, B_const], [PB(3)])
                    OP("act", lambda e: e.copy(out=Dt, in_=BK(3)[:, :]), [PB(3)], [B_Dt])
            X = Dm
            OP("dve", lambda e: e.tensor_tensor(out=v3(KG), in0=v3(kn), in1=c4(expG), op=ALU.mult), [B_qkn, B_st], [B_KG])
            OP("dve", lambda e: e.tensor_tensor(out=v3(kdec), in0=v3(kn), in1=c4(kds), op=ALU.mult), [B_qkn, B_st], [B_kdec])
            mm4(6, hs_(KG), hs_(X), [B_KG, B_Dm])
            OP("act", lambda e: e.mul(out=nW, in_=BK(6)[:, :], mul=-1.0), [PB(6)], [B_nW])
            def mmv(e):
                ins = None
                for h in range(4):
                    e.matmul(BK(7)[:, h * 128:(h + 1) * 128], lhsT=X[:, h * 128:(h + 1) * 128], rhs=v_[:, h * 128:(h + 1) * 128], start=True, stop=False)
                    ins = e.matmul(BK(7)[:, h * 128:(h + 1) * 128], lhsT=nW[:, h * 128:(h + 1) * 128], rhs=S[:, h * 128:(h + 1) * 128], start=False, stop=True)
                return ins
            OP("pe", mmv, [B_Dm, B_qkv, B_nW, B_S], [PB(7)])
            OP("dve", lambda e: e.tensor_tensor(out=v3(vnew), in0=v3(BK(7)[:, :]), in1=c4(beta), op=ALU.mult), [PB(7), B_st], [B_vnew])
            mm4(1, hs_(qT), hs_(S), [B_kqT, B_S])
            mm4(6, hs_(qkT), hs_(vnew), [B_qkT, B_vnew])
            mm4(8, hs_(kdec), hs_(vnew), [B_kdec, B_vnew])
            OP("dve", lambda e: e.tensor_tensor(out=v3(t2_), in0=v3(S), in1=c4(glast), op=ALU.mult), [B_S, B_st], [B_t2])
            OP("dve", lambda e: e.tensor_tensor(out=Sn, in0=t2_, in1=BK(8)[:, :], op=ALU.add), [B_t2, PB(8)], [B_Sn])
            OP("dve", lambda e: e.tensor_tensor(out=v3(ot), in0=v3(BK(1)[:, :]), in1=c4(expG), op=ALU.mult), [PB(1), B_st], [B_ot])
            OP("dve", lambda e: e.tensor_tensor(out=ot, in0=ot, in1=BK(6)[:, :], op=ALU.add), [B_ot, PB(6)], [B_ot])
            P.dma("pool", (OAF if d == 0 else OAB)[r0:r0 + 128, :], ot, reads=[B_ot], writes=[Buf("oad")], key=f"a_ot{sl}")

        lists = []
        for d in range(2):
            order = list(range(NT)) if d == 0 else [1, 0] + list(range(NT - 1, 1, -1))
            if DBG.get('a_tiles'):
                order = [t_ for t_ in order if t_ in DBG['a_tiles']]
            P.record()
            OP("pool", lambda e, d=d: e.memset(Ss[d][0][0], 0.0), [], [Ss[d][0][1]])
            a_load(order[0], d, d * 2)
            cur = 0
            for it, tile in enumerate(order):
                if it + 1 < len(order):
                    a_load(order[it + 1], d, d * 2 + (it + 1) % 2)
                a_chunk(tile, d, d * 2 + it % 2, Ss[d][cur][0], Ss[d][cur][1], Ss[d][1 - cur][0], Ss[d][1 - cur][1])
                cur = 1 - cur
            lists.append(P.stop())
        cfin = None
        if MERGE_C:
            clists, cfin = c_build(layer)
            if DBG.get("seq_c"):
                P.play(lists)
                lists = clists
            else:
                lists = lists + clists
        per_chunk = len(lists[0]) // max(1, NT)
        P.play(lists, lead=[0, DBG.get("a_lead", per_chunk // 2)] + [0] * (len(lists) - 2))

        if FUSE_FIN:
            return
        P.barrier()
        A.release(persist_mark)
        (gnn2, B_gnn2), = mk(512, "gnn2")
        P.dma("sp", gnn2, gdnn_d[layer], writes=[B_gnn2], key="a2_gnn")
        fs_ = mk(512, "a2f", 2)
        bs_ = mk(512, "a2b", 2)
        zs_ = mk(512, "a2z", 2)
        sqs_ = mk(512, "a2sq", 2)
        sts_ = mk(16, "a2st", 2)
        tiles2 = list(range(NT))
        if DBG.get('a_tiles'):
            tiles2 = [t_ for t_ in tiles2 if t_ in DBG['a_tiles']]

        def a2_load(tile, sl):
            r0 = tile * 128
            P.dma("sp", fs_[sl][0], OAF[r0:r0 + 128, :], writes=[fs_[sl][1]], key=f"a2f{sl}")
            P.dma("sp", bs_[sl][0], OAB[r0:r0 + 128, :], writes=[bs_[sl][1]], key=f"a2b{sl}")
            P.dma("sp", zs_[sl][0], TM[r0:r0 + 128, TM_AZ:TM_AZ + 512], writes=[zs_[sl][1]], key=f"a2z{sl}")

        def a2_comp(tile, sl):
            r0 = tile * 128
            f_, B_f = fs_[sl]
            b_, B_b = bs_[sl]
            z_, B_z = zs_[sl]
            sq, B_sq = sqs_[sl]
            st, B_st = sts_[sl]
            OP("dve", lambda e: e.tensor_tensor(out=f_, in0=f_, in1=b_, op=ALU.add), [B_f, B_b], [B_f])
            OP("pool", lambda e: e.tensor_tensor(out=sq, in0=f_, in1=f_, op=ALU.mult), [B_f], [B_sq])
            OP("dve", lambda e: e.reduce_sum(out=st[:, 0:4], in_=v3(sq), axis=AX.X), [B_sq], [B_st])
            OP("act", lambda e: e.activation(out=st[:, 0:4], in_=st[:, 0:4], func=AF.Sqrt, bias=1e-6, scale=1.0 / 128), [B_st], [B_st])
            OP("dve", lambda e: e.reciprocal(out=st[:, 0:4], in_=st[:, 0:4]), [B_st], [B_st])
            OP("dve", lambda e: e.tensor_tensor(out=v3(f_), in0=v3(f_), in1=st[:, 0:4].unsqueeze(2).to_broadcast([128, 4, 128]), op=ALU.mult),
               [B_f, B_st], [B_f])
            OP("act", lambda e: e.activation(out=z_, in_=z_, func=AF.Silu), [B_z], [B_z])
            OP("pool", lambda e: e.tensor_tensor(out=z_, in0=z_, in1=gnn2, op=ALU.mult), [B_z, B_gnn2], [B_z])
            OP("dve", lambda e: e.tensor_tensor(out=f_, in0=f_, in1=z_, op=ALU.mult), [B_f, B_z], [B_f])
            P.dma("pool", MIXA[r0:r0 + 128, :], f_, reads=[B_f], writes=[Buf("mixa")], key=f"a2o{sl}")

        a2_load(tiles2[0], 0)
        for it, tile in enumerate(tiles2):
            if it + 1 < len(tiles2):
                a2_load(tiles2[it + 1], (it + 1) % 2)
            a2_comp(tile, it % 2)
        if cfin is not None:
            cfin()

    def phase_B(layer):
        P.barrier()
        A.release(persist_mark)
        (kT, B_kT), = mk(2 * T, "kT")
        (qT, B_qT), = mk(2 * T, "qT")
        (V1, B_V1), = mk(NT * 260, "V1")
        (G, B_G), = mk(4 * 15 * 64, "G")
        for hp in range(2):
            P.dma("sp", kT[:, hp * T:(hp + 1) * T], FM[FM_BK + hp * 128:FM_BK + (hp + 1) * 128, :], writes=[B_kT], key="b_kT")
            P.dma("pool", qT[:, hp * T:(hp + 1) * T], FM[FM_BQ + hp * 128:FM_BQ + (hp + 1) * 128, :], writes=[B_qT], key="b_qT")
        P.dma("sp", G, nab_d[layer], writes=[B_G], key="b_G")
        OP("pool", lambda e: e.memset(V1, 1.0), [], [B_V1])
        for t_ in range(NT):
            P.dma("sp", V1[:, t_ * 260:(t_ + 1) * 260].rearrange("p (h c) -> p h c", h=4)[:, :, 0:64],
                  TM[t_ * 128:(t_ + 1) * 128, TM_BV:TM_BV + 256].rearrange("p (h c) -> p h c", h=4), writes=[B_V1], key="b_V1")
        G4 = G.rearrange("p (h e c) -> p h e c", h=4, e=15)
        tmps = mk(512, "btmp", 4)
        pTs = mk(512, "bpT", 4)
        nums = mk(512, "bnum", 4)
        rrs = mk(512, "brr", 4)
        zts = mk(512, "bz", 4)
        qtiles = [(256 + 512 * a, 512, a) for a in range(8)]
        if layer < DEPTH - 1:
            qtiles = [(0, 256, None)] + qtiles
        cnts = [dict(st=0, w=0, o=0) for _ in range(2)]
        def do_qtile(h, c0, nq, a):
            if True:
                hp, base = h // 2, (h % 2) * 64
                sm = h % 2
                cnt = cnts[sm]
                ob = 4 * sm + 2
                fs = 2 * sm + cnt["o"] % 2
                cnt["o"] += 1
                qop = qT[base:base + 64, hp * T + c0:hp * T + c0 + nq]
                first = True
                for kt in range(2):
                    sb_ = 4 * sm + cnt["st"] % 2
                    cnt["st"] += 1
                    ws = 2 * sm + cnt["w"] % 2
                    cnt["w"] += 1
                    pT, B_pT = pTs[ws]
                    OP("pe", lambda e, sb_=sb_, kt=kt, qop=qop, nq=nq: e.matmul(
                        banks[sb_][:, 0:nq], lhsT=kT[base:base + 64, hp * T + kt * 128:hp * T + (kt + 1) * 128], rhs=qop, start=True, stop=True),
                        [B_kT, B_qT], [PSB[sb_]])
                    OP("act", lambda e, sb_=sb_, pT=pT, nq=nq: e.activation(out=pT[:, 0:nq], in_=banks[sb_][:, 0:nq], func=AF.Exp, scale=0.125),
                       [PSB[sb_]], [B_pT])
                    last_mm = (a is None and kt == 1)
                    def pv(e, ob=ob, kt=kt, pT=pT, nq=nq, first=first, last_mm=last_mm):
                        e.matmul(banks[ob][0:64, 0:nq], lhsT=V1[:, kt * 260 + h * 65:kt * 260 + h * 65 + 64], rhs=pT[:, 0:nq], start=first, stop=last_mm)
                        return e.matmul(banks[ob + 1][0:64, 0:nq], lhsT=ones[:, 0:64], rhs=pT[:, 0:nq], start=first, stop=last_mm)
                    OP("pe", pv, [B_V1, B_pT, B_const], [PSB[ob], PSB[ob + 1]])
                    first = False
                if a is not None:
                    rows = {}
                    for kr in range(64):
                        js = [j for j in range(8) if min(max(8 * a + j - 4, 0), 56) <= kr <= min(max(8 * a + j - 4, 0), 56) + 7]
                        if js:
                            rows[kr] = (js[0], js[-1])
                    kts = sorted(set(kr // 2 for kr in rows))
                    for mi, m in enumerate(kts):
                        halves = [(hf, rows[2 * m + hf]) for hf in range(2) if (2 * m + hf) in rows]
                        jl = min(v[0] for _, v in halves)
                        jh = max(v[1] for _, v in halves)
                        sb_ = 4 * sm + cnt["st"] % 2
                        cnt["st"] += 1
                        ws = 2 * sm + cnt["w"] % 2
                        cnt["w"] += 1
                        tmp, B_tmp = tmps[ws]
                        pT, B_pT = pTs[ws]
                        tk = 2 + m
                        OP("pe", lambda e, sb_=sb_, tk=tk, jl=jl, jh=jh: e.matmul(
                            banks[sb_][:, jl * 64:(jh + 1) * 64], lhsT=kT[base:base + 64, hp * T + tk * 128:hp * T + (tk + 1) * 128],
                            rhs=qT[base:base + 64, hp * T + c0 + jl * 64:hp * T + c0 + (jh + 1) * 64], start=True, stop=True),
                            [B_kT, B_qT], [PSB[sb_]])
                        ucs = slice(jl * 64, (jh + 1) * 64)
                        uneven = any((j0, j1) != (jl, jh) for _, (j0, j1) in halves) or len(halves) < 2
                        if uneven:
                            OP("pool", lambda e, pT=pT, ucs=ucs: e.memset(pT[:, ucs], 0.0), [], [B_pT])
                        for hi, (hf, (j0, j1)) in enumerate(halves):
                            kr = 2 * m + hf
                            e_lo = 7 - kr + 8 * a + j0
                            nj = j1 - j0 + 1
                            assert 0 <= e_lo and e_lo + nj <= 15
                            pr = slice(hf * 64, hf * 64 + 64)
                            cs = slice(j0 * 64, (j1 + 1) * 64)
                            OP("dve", lambda e, tmp=tmp, sb_=sb_, pr=pr, cs=cs, e_lo=e_lo, nj=nj: e.scalar_tensor_tensor(
                                out=tmp[pr, cs].rearrange("p (j c) -> p j c", c=64), in0=banks[sb_][pr, cs].rearrange("p (j c) -> p j c", c=64),
                                scalar=0.125, in1=G4[pr, h, e_lo:e_lo + nj, :], op0=ALU.mult, op1=ALU.add), [PSB[sb_], B_G], [B_tmp])
                            OP("act", lambda e, tmp=tmp, pT=pT, pr=pr, cs=cs: e.activation(out=pT[pr, cs], in_=tmp[pr, cs], func=AF.Exp),
                               [B_tmp], [B_pT])
                        last_mm = (mi == len(kts) - 1)
                        def pv(e, ob=ob, tk=tk, pT=pT, ucs=ucs, last_mm=last_mm):
                            e.matmul(banks[ob][0:64, ucs], lhsT=V1[:, tk * 260 + h * 65:tk * 260 + h * 65 + 64], rhs=pT[:, ucs], start=False, stop=last_mm)
                            return e.matmul(banks[ob + 1][0:64, ucs], lhsT=ones[:, 0:64], rhs=pT[:, ucs], start=False, stop=last_mm)
                        OP("pe", pv, [B_V1, B_pT, B_const], [PSB[ob], PSB[ob + 1]])
                num, B_num = nums[fs]
                rr, B_rr = rrs[fs]
                zt, B_zt = zts[fs]
                P.dma("sp", zt[0:64, 0:nq], FM[FM_BZ + h * 64:FM_BZ + (h + 1) * 64, c0:c0 + nq], writes=[B_zt], key=f"b_z{fs}")
                OP("act", lambda e, zt=zt, nq=nq: e.activation(out=zt[0:64, 0:nq], in_=zt[0:64, 0:nq], func=AF.Silu), [B_zt], [B_zt])
                OP("act", lambda e, num=num, ob=ob, nq=nq: e.copy(out=num[0:64, 0:nq], in_=banks[ob][0:64, 0:nq]), [PSB[ob]], [B_num])
                OP("dve", lambda e, rr=rr, ob=ob, nq=nq: e.reciprocal(out=rr[0:64, 0:nq], in_=banks[ob + 1][0:64, 0:nq]), [PSB[ob + 1]], [B_rr])
                OP("dve", lambda e, num=num, rr=rr, nq=nq: e.tensor_tensor(out=num[0:64, 0:nq], in0=num[0:64, 0:nq], in1=rr[0:64, 0:nq], op=ALU.mult),
                   [B_num, B_rr], [B_num])
                OP("pool", lambda e, num=num, zt=zt, nq=nq: e.tensor_tensor(out=num[0:64, 0:nq], in0=num[0:64, 0:nq], in1=zt[0:64, 0:nq], op=ALU.mult),
                   [B_num, B_zt], [B_num])
                P.dma("sp", MIXB[h * 64:(h + 1) * 64, c0:c0 + nq], num[0:64, 0:nq], reads=[B_num], writes=[Buf("mixb")], key=f"b_o{fs}")
        blists = []
        for sm_ in range(2):
            P.record()
            for h in (sm_, sm_ + 2):
                for (c0, nq, a) in qtiles:
                    do_qtile(h, c0, nq, a)
            blists.append(P.stop())
        P.play(blists)

    def c_build(layer):
        (w2p, B_w2p), = mk(256, "w2p")
        (b2r, B_b2r), = mk(256, "b2r")
        P.dma("sp", w2p[0:32, :], w2p_d[layer], writes=[B_w2p], key="c_w2p")
        P.dma("sp", b2r[0:1, :], b2r_d[layer], writes=[B_b2r], key="c_b2r")
        CT = {}
        for nm, n in (("tc", 768), ("rt", 128), ("cos", 128), ("sin", 128)):
            CT[nm] = mk(n, "c" + nm, 4)
        for nm, n in (("xsw", 256), ("sp", 128), ("Ep", 128), ("Em", 128), ("gl", 8), ("qh", 128), ("kh", 128), ("qkT", 256),
                      ("khTm", 512), ("am", 512), ("tmp", 256), ("os", 256)):
            CT[nm] = mk(n, "c" + nm, 2)
        CS = [mk(256, f"cS{d_}", 2) for d_ in range(2)]
        scale_q = 32 ** -0.5

        def c_load(tile, d, ql):
            r0 = tile * 128
            tc, B_tc = CT["tc"][ql]
            rt, B_rt = CT["rt"][ql]
            P.dma("sp", tc, TM[r0:r0 + 128, 768:1536], writes=[B_tc], key=f"c_tc{ql}")
            P.dma("sp", rt[0:32, :], FM[FM_CR:FM_CR + 32, r0:r0 + 128], writes=[B_rt], key=f"c_rt{ql}")
            if tile >= 2:
                l0 = (tile - 2) * 128
                P.dma("sp", CT["cos"][ql][0], ropec_d[l0:l0 + 128, :], writes=[CT["cos"][ql][1]], key=f"c_cs{ql}")
                P.dma("sp", CT["sin"][ql][0], ropes_d[l0:l0 + 128, :], writes=[CT["sin"][ql][1]], key=f"c_sn{ql}")

        def c_chunk(tile, d, ql, S, B_S, Sn, B_Sn):
            r0 = tile * 128
            g = lambda nm: CT[nm][d]
            tc, B_tc = CT["tc"][ql]; rt, B_rt = CT["rt"][ql]; cs_, B_cs = CT["cos"][ql]; sn, B_sn = CT["sin"][ql]
            xsw, B_xsw = g("xsw"); sp_, B_sp = g("sp"); Ep, B_Ep = g("Ep"); Em, B_Em = g("Em"); gl, B_gl = g("gl")
            qh, B_qh = g("qh"); kh, B_kh = g("kh"); qkT, B_qhT = g("qkT"); khTm, B_khTm = g("khTm"); am, B_am = g("am")
            tmp, B_tmp = g("tmp"); os_, B_os = g("os")
            qhT = qkT[:, 0:128]
            TR = tril if d == 0 else triu
            bk = (lambda lb: 6 + d) if MERGE_C else (lambda lb: 4 * d + lb % 4)
            BK = lambda lb: banks[bk(lb)]
            PB = lambda lb: PSB[bk(lb)]
            if tile >= 2:
                x4 = tc[:, 0:256].rearrange("p (g two f) -> p g two f", two=2, f=8)
                xs4 = xsw.rearrange("p (g two f) -> p g two f", two=2, f=8)
                OP("pool", lambda e: e.tensor_copy(out=xs4[:, :, 0, :], in_=x4[:, :, 1, :]), [B_tc], [B_xsw])
                OP("pool", lambda e: e.tensor_copy(out=xs4[:, :, 1, :], in_=x4[:, :, 0, :]), [B_tc], [B_xsw])
                x3 = tc[:, 0:256].rearrange("p (a c) -> p a c", a=2)
                xs3 = xsw.rearrange("p (a c) -> p a c", a=2)
                OP("dve", lambda e: e.tensor_tensor(out=x3, in0=x3, in1=cs_.unsqueeze(1).to_broadcast([128, 2, 128]), op=ALU.mult), [B_tc, B_cs], [B_tc])
                OP("dve", lambda e: e.tensor_tensor(out=xs3, in0=xs3, in1=sn.unsqueeze(1).to_broadcast([128, 2, 128]), op=ALU.mult), [B_xsw, B_sn], [B_xsw])
                OP("dve", lambda e: e.tensor_tensor(out=tc[:, 0:256], in0=tc[:, 0:256], in1=xsw, op=ALU.add), [B_tc, B_xsw], [B_tc])
            def mmz(e):
                e.matmul(BK(0)[:, 0:128], lhsT=rt[0:32, :], rhs=w2p[0:32, d * 128:(d + 1) * 128], start=True, stop=False)
                return e.matmul(BK(0)[:, 0:128], lhsT=ones[0:1, 0:128], rhs=b2r[0:1, d * 128:(d + 1) * 128], start=False, stop=True)
            OP("pe", mmz, [B_rt, B_w2p, B_b2r, B_const], [PB(0)])
            OP("act", lambda e: e.activation(out=sp_, in_=BK(0)[:, 0:128], func=AF.Exp, scale=-1.0), [PB(0)], [B_sp])
            OP("act", lambda e: e.activation(out=sp_, in_=sp_, func=AF.Ln, bias=1.0, scale=1.0), [B_sp], [B_sp])
            def mmc(e):
                e.matmul(BK(1)[:, 0:128], lhsT=TR, rhs=sp_, start=True, stop=True)
                return e.matmul(BK(1)[:, 128:129], lhsT=sp_, rhs=ones[:, 0:1], start=True, stop=True)
            OP("pe", mmc, [B_sp, B_const], [PB(1)])
            OP("act", lambda e: e.activation(out=Ep, in_=BK(1)[:, 0:128], func=AF.Exp, scale=-1.0 / 16), [PB(1)], [B_Ep])
            OP("act", lambda e: e.activation(out=Em, in_=BK(1)[:, 0:128], func=AF.Exp, scale=1.0 / 16), [PB(1)], [B_Em])
            OP("act", lambda e: e.activation(out=gl[:, 0:1], in_=BK(1)[:, 128:129], func=AF.Exp, scale=-1.0 / 16), [PB(1)], [B_gl])
            OP("dve", lambda e: e.scalar_tensor_tensor(out=qh, in0=tc[:, 0:128], scalar=scale_q, in1=Ep, op0=ALU.mult, op1=ALU.mult), [B_tc, B_Ep], [B_qh])
            OP("dve", lambda e: e.tensor_tensor(out=kh, in0=tc[:, 128:256], in1=Em, op=ALU.mult), [B_tc, B_Em], [B_kh])
            def trs(e):
                e.transpose(BK(2)[:, 0:128], qh, ident)
                return e.transpose(BK(2)[:, 128:256], kh, ident)
            OP("pe", trs, [B_qh, B_kh, B_const], [PB(2)])
            OP("act", lambda e: e.copy(out=qkT, in_=BK(2)[:, 0:256]), [PB(2)], [B_qhT])
            for h in range(4):
                OP("dve", lambda e, h=h: e.tensor_scalar(out=khTm[:, h * 128:(h + 1) * 128], in0=qkT[:, 128:256], scalar1=hm[:, h:h + 1],
                                                         scalar2=None, op0=ALU.mult), [B_qhT, B_const], [B_khTm])
            def mma(e):
                ins = None
                for h in range(4):
                    ins = e.matmul(BK(3)[:, h * 128:(h + 1) * 128], lhsT=khTm[:, h * 128:(h + 1) * 128], rhs=qhT, start=True, stop=True)
                return ins
            OP("pe", mma, [B_khTm, B_qhT], [PB(3)])
            OP("dve", lambda e: e.tensor_tensor(out=am.rearrange("p (h j) -> p h j", h=4), in0=BK(3)[:, :].rearrange("p (h j) -> p h j", h=4),
                                                in1=TR.unsqueeze(1).to_broadcast([128, 4, 128]), op=ALU.mult), [PB(3), B_const], [B_am])
            def mmo(e):
                e.matmul(BK(0)[:, 0:256], lhsT=qhT, rhs=S, start=True, stop=False)
                ins = None
                for h in range(4):
                    ins = e.matmul(BK(0)[:, h * 64:(h + 1) * 64], lhsT=am[:, h * 128:(h + 1) * 128],
                                   rhs=tc[:, 256 + h * 64:256 + (h + 1) * 64], start=False, stop=(h == 3))
                return ins
            OP("pe", mmo, [B_qhT, B_S, B_am, B_tc], [PB(0)])
            OP("act", lambda e: e.copy(out=os_, in_=BK(0)[:, 0:256]), [PB(0)], [B_os])
            OP("pe", lambda e: e.matmul(BK(1)[:, 0:256], lhsT=kh, rhs=tc[:, 256:512], start=True, stop=True), [B_kh, B_tc], [PB(1)])
            OP("dve", lambda e: e.tensor_tensor(out=tmp, in0=BK(1)[:, 0:256], in1=bd, op=ALU.mult), [PB(1), B_const], [B_tmp])
            OP("dve", lambda e: e.tensor_tensor(out=tmp, in0=tmp, in1=S, op=ALU.add), [B_tmp, B_S], [B_tmp])
            OP("dve", lambda e: e.tensor_scalar(out=Sn, in0=tmp, scalar1=gl[:, 0:1], scalar2=None, op0=ALU.mult), [B_tmp, B_gl], [B_Sn])
            P.dma("pool", (OCF if d == 0 else OCB)[r0:r0 + 128, :], os_, reads=[B_os], writes=[Buf("ocd")], key=f"c_os{d}")

        lists = []
        for d in range(2):
            order = list(range(NT)) if d == 0 else [1, 0] + list(range(NT - 1, 1, -1))
            if DBG.get('c_tiles'):
                order = [t_ for t_ in order if t_ in DBG['c_tiles']]
            P.record()
            OP("pool", lambda e, d=d: e.memset(CS[d][0][0], 0.0), [], [CS[d][0][1]])
            c_load(order[0], d, d * 2)
            cur = 0
            for it, tile in enumerate(order):
                if it + 1 < len(order):
                    c_load(order[it + 1], d, d * 2 + (it + 1) % 2)
                c_chunk(tile, d, d * 2 + it % 2, CS[d][cur][0], CS[d][cur][1], CS[d][1 - cur][0], CS[d][1 - cur][1])
                cur = 1 - cur
            lists.append(P.stop())

        def c_final():
            (gn, B_gn), = mk(256, "gn")
            P.dma("sp", gn, glan_d[layer], writes=[B_gn], key="c_gn")
            fs_ = mk(256, "c2f", 2)
            bs_ = mk(256, "c2b", 2)
            zs_ = mk(256, "c2z", 2)
            sqs_ = mk(256, "c2sq", 2)
            sts_ = mk(16, "c2st", 2)
            tiles2 = list(range(NT))
            if DBG.get('c_tiles'):
                tiles2 = [t_ for t_ in tiles2 if t_ in DBG['c_tiles']]

            def ld(tile, sl):
                r0 = tile * 128
                P.dma("sp", fs_[sl][0], OCF[r0:r0 + 128, :], writes=[fs_[sl][1]], key=f"c2f{sl}")
                P.dma("sp", bs_[sl][0], OCB[r0:r0 + 128, :], writes=[bs_[sl][1]], key=f"c2b{sl}")
                P.dma("sp", zs_[sl][0], TM[r0:r0 + 128, TM_CZ:TM_CZ + 256], writes=[zs_[sl][1]], key=f"c2z{sl}")

            def comp(tile, sl):
                r0 = tile * 128
                f_, B_f = fs_[sl]
                b_, B_b = bs_[sl]
                z_, B_z = zs_[sl]
                sq, B_sq = sqs_[sl]
                st, B_st = sts_[sl]
                v4 = lambda ap: ap.rearrange("p (h v) -> p h v", h=4)
                OP("dve", lambda e: e.tensor_tensor(out=f_, in0=f_, in1=b_, op=ALU.add), [B_f, B_b], [B_f])
                OP("pool", lambda e: e.tensor_tensor(out=sq, in0=f_, in1=f_, op=ALU.mult), [B_f], [B_sq])
                OP("dve", lambda e: e.reduce_sum(out=st[:, 0:4], in_=v4(sq), axis=AX.X), [B_sq], [B_st])
                OP("act", lambda e: e.activation(out=st[:, 4:8], in_=st[:, 0:4], func=AF.Sqrt, bias=1e-6, scale=1.0 / 64), [B_st], [B_st])
                OP("dve", lambda e: e.reciprocal(out=st[:, 4:8], in_=st[:, 4:8]), [B_st], [B_st])
                OP("dve", lambda e: e.tensor_tensor(out=v4(f_), in0=v4(f_), in1=st[:, 4:8].unsqueeze(2).to_broadcast([128, 4, 64]), op=ALU.mult),
                   [B_f, B_st], [B_f])
                OP("act", lambda e: e.activation(out=z_, in_=z_, func=AF.Silu), [B_z], [B_z])
                OP("pool", lambda e: e.tensor_tensor(out=z_, in0=z_, in1=gn, op=ALU.mult), [B_z, B_gn], [B_z])
                OP("dve", lambda e: e.tensor_tensor(out=f_, in0=f_, in1=z_, op=ALU.mult), [B_f, B_z], [B_f])
                P.dma("pool", MIXC[r0:r0 + 128, :], f_, reads=[B_f], writes=[Buf("mixc")], key=f"c2o{sl}")

            ld(tiles2[0], 0)
            for it, tile in enumerate(tiles2):
                if it + 1 < len(tiles2):
                    ld(tiles2[it + 1], (it + 1) % 2)
                comp(tile, it % 2)

        return lists, c_final

    def phase_C(layer):
        P.barrier()
        A.release(persist_mark)
        lists, fin = c_build(layer)
        P.play(lists)
        P.barrier()
        A.release(persist_mark)
        fin()

    def phase_P4(layer):
        P.barrier()
        A.release(persist_mark)
        (wo, B_wo), = mk(8 * D, "wo")
        (lng, B_lng), = mk(D, "lng")
        (lnb, B_lnb), = mk(D, "lnb")
        wov = w_out[layer].rearrange("(k p) c -> p k c", p=128)
        for k in range(8):
            P.dma("pool", wo[:, k * D:(k + 1) * D], wov[:, k, :], writes=[B_wo], key="p4_wo")
        P.dma("sp", lng, lng_d[layer], writes=[B_lng], key="p4_lng")
        P.dma("sp", lnb, lnb_d[layer], writes=[B_lnb], key="p4_lnb")
        xas = mk(D, "xa", 4)
        mas = mk(512, "ma", 4)
        mcs = mk(256, "mc", 4)
        mixTs = mk(8 * 128, "mixT", 4)
        hs = mk(D, "h", 4)
        sts = mk(16, "st", 4)
        if FUSE_FIN:
            mbs = mk(768, "mb2", 4)
            zzs = mk(768, "zz", 4)
            sqs4 = mk(768, "sq4", 4)
            st8s = mk(16, "st8", 4)
            (gno, B_gno), = mk(768, "gno")
            P.dma("sp", gno[:, 0:512], gdnn_d[layer], writes=[B_gno], key="p4_gno")
            P.dma("sp", gno[:, 512:768], glan_d[layer], writes=[B_gno], key="p4_gno")
        last = layer == DEPTH - 1
        xsrc = x_all if layer == 0 else XC
        tiles = list(range(2, NT)) if last else list(range(NT))
        def p4_load(tile, sl):
            r0 = tile * 128
            xa, B_xa = xas[sl]
            ma, B_ma = mas[sl]
            mc, B_mc = mcs[sl]
            mixT, B_mixT = mixTs[sl]
            items = [(xa, xsrc[r0:r0 + 128, :], [B_xa])]
            if FUSE_FIN:
                mb_, B_mb = mbs[sl]
                zz, B_zz = zzs[sl]
                items += [(ma, OAF[r0:r0 + 128, :], [B_ma]), (mc, OCF[r0:r0 + 128, :], [B_mc]),
                          (mb_[:, 0:512], OAB[r0:r0 + 128, :], [B_mb]), (mb_[:, 512:768], OCB[r0:r0 + 128, :], [B_mb]),
                          (zz[:, 0:512], TM[r0:r0 + 128, TM_AZ:TM_AZ + 512], [B_zz]), (zz[:, 512:768], TM[r0:r0 + 128, TM_CZ:TM_CZ + 256], [B_zz])]
            else:
                items += [(ma, MIXA[r0:r0 + 128, :], [B_ma]), (mc, MIXC[r0:r0 + 128, :], [B_mc])]
            for j in range(2):
                items.append((mixT[:, (4 + j) * 128:(5 + j) * 128], MIXB[j * 128:(j + 1) * 128, r0:r0 + 128], [B_mixT]))
            P.dma_group("sp", items, key=f"p4_ld{sl}")

        def p4_body(tile, sl, bo):
            r = 1 if tile < 2 else 0
            r0 = tile * 128
            xa, B_xa = xas[sl]
            ma, B_ma = mas[sl]
            mc, B_mc = mcs[sl]
            mixT, B_mixT = mixTs[sl]
            h_, B_h = hs[sl]
            st, B_st = sts[sl]
            if FUSE_FIN:
                mb_, B_mb = mbs[sl]
                zz, B_zz = zzs[sl]
                sq4, B_sq4 = sqs4[sl]
                s8, B_s8 = st8s[sl]
                OP("dve", lambda e, ma=ma, mb_=mb_: e.tensor_tensor(out=ma, in0=ma, in1=mb_[:, 0:512], op=ALU.add), [B_ma, B_mb], [B_ma])
                OP("dve", lambda e, mc=mc, mb_=mb_: e.tensor_tensor(out=mc, in0=mc, in1=mb_[:, 512:768], op=ALU.add), [B_mc, B_mb], [B_mc])
                OP("act", lambda e, ma=ma, sq4=sq4: e.activation(out=sq4[:, 0:512], in_=ma, func=AF.Square), [B_ma], [B_sq4])
                OP("act", lambda e, mc=mc, sq4=sq4: e.activation(out=sq4[:, 512:768], in_=mc, func=AF.Square), [B_mc], [B_sq4])
                OP("dve", lambda e, sq4=sq4, s8=s8: e.reduce_sum(out=s8[:, 0:4], in_=sq4[:, 0:512].rearrange("p (h v) -> p h v", h=4), axis=AX.X), [B_sq4], [B_s8])
                OP("dve", lambda e, sq4=sq4, s8=s8: e.reduce_sum(out=s8[:, 4:8], in_=sq4[:, 512:768].rearrange("p (h v) -> p h v", h=4), axis=AX.X), [B_sq4], [B_s8])
                OP("act", lambda e, s8=s8: e.activation(out=s8[:, 0:4], in_=s8[:, 0:4], func=AF.Sqrt, bias=1e-6, scale=1.0 / 128), [B_s8], [B_s8])
                OP("act", lambda e, s8=s8: e.activation(out=s8[:, 4:8], in_=s8[:, 4:8], func=AF.Sqrt, bias=1e-6, scale=1.0 / 64), [B_s8], [B_s8])
                OP("dve", lambda e, s8=s8: e.reciprocal(out=s8[:, 0:8], in_=s8[:, 0:8]), [B_s8], [B_s8])
                OP("dve", lambda e, ma=ma, s8=s8: e.tensor_tensor(out=ma.rearrange("p (h v) -> p h v", h=4), in0=ma.rearrange("p (h v) -> p h v", h=4),
                                                               in1=s8[:, 0:4].unsqueeze(2).to_broadcast([128, 4, 128]), op=ALU.mult), [B_ma, B_s8], [B_ma])
                OP("dve", lambda e, mc=mc, s8=s8: e.tensor_tensor(out=mc.rearrange("p (h v) -> p h v", h=4), in0=mc.rearrange("p (h v) -> p h v", h=4),
                                                               in1=s8[:, 4:8].unsqueeze(2).to_broadcast([128, 4, 64]), op=ALU.mult), [B_mc, B_s8], [B_mc])
                OP("act", lambda e, zz=zz: e.activation(out=zz, in_=zz, func=AF.Silu), [B_zz], [B_zz])
                OP("pool", lambda e, zz=zz: e.tensor_tensor(out=zz, in0=zz, in1=gno, op=ALU.mult), [B_zz, B_gno], [B_zz])
                OP("dve", lambda e, ma=ma, zz=zz: e.tensor_tensor(out=ma, in0=ma, in1=zz[:, 0:512], op=ALU.mult), [B_ma, B_zz], [B_ma])
                OP("dve", lambda e, mc=mc, zz=zz: e.tensor_tensor(out=mc, in0=mc, in1=zz[:, 512:768], op=ALU.mult), [B_mc, B_zz], [B_mc])
            def trs(e, ma=ma, mc=mc):
                ins = None
                for k in range(4):
                    ins = e.transpose(banks[bo][:, k * 128:(k + 1) * 128], ma[:, k * 128:(k + 1) * 128], ident)
                for k in range(2):
                    ins = e.transpose(banks[bo + 1][:, k * 128:(k + 1) * 128], mc[:, k * 128:(k + 1) * 128], ident)
                return ins
            OP("pe", trs, [B_ma, B_mc, B_const], [PSB[bo], PSB[bo + 1]])
            OP("act", lambda e, mixT=mixT: e.copy(out=mixT[:, 0:512], in_=banks[bo][:, :]), [PSB[bo]], [B_mixT])
            OP("act", lambda e, mixT=mixT: e.copy(out=mixT[:, 768:1024], in_=banks[bo + 1][:, 0:256]), [PSB[bo + 1]], [B_mixT])
            for half in range(2):
                bank = bo + 2 + half
                def mmy(e, mixT=mixT, half=half, bank=bank):
                    ins = None
                    for k in range(8):
                        ins = e.matmul(banks[bank][:, :], lhsT=mixT[:, k * 128:(k + 1) * 128],
                                       rhs=wo[:, k * D + half * 512:k * D + (half + 1) * 512], start=(k == 0), stop=(k == 7))
                    return ins
                OP("pe", mmy, [B_mixT, B_wo], [PSB[bank]])
                OP("dve", lambda e, h_=h_, half=half, bank=bank, r=r: e.tensor_tensor(
                    out=h_[:, half * 512:(half + 1) * 512], in0=banks[bank][:, :],
                    in1=modb[r][:, 2 * D + half * 512:2 * D + (half + 1) * 512], op=ALU.mult), [PSB[bank], B_modb], [B_h])
            OP("dve", lambda e, h_=h_, xa=xa: e.scalar_tensor_tensor(out=h_, in0=xa, scalar=DN_ALPHA, in1=h_, op0=ALU.mult, op1=ALU.add),
               [B_xa, B_h], [B_h])
            OP("dve", lambda e, h_=h_, st=st: e.bn_stats(out=st[:, 0:6], in_=h_[:, 0:512]), [B_h], [B_st])
            OP("dve", lambda e, h_=h_, st=st: e.bn_stats(out=st[:, 6:12], in_=h_[:, 512:1024]), [B_h], [B_st])
            OP("dve", lambda e, st=st: e.bn_aggr(out=st[:, 12:14], in_=st[:, 0:12]), [B_st], [B_st])
            OP("act", lambda e, st=st: e.activation(out=st[:, 14:15], in_=st[:, 13:14], func=AF.Sqrt, bias=LN_EPS, scale=1.0), [B_st], [B_st])
            OP("dve", lambda e, st=st: e.reciprocal(out=st[:, 14:15], in_=st[:, 14:15]), [B_st], [B_st])
            OP("dve", lambda e, h_=h_, st=st: e.tensor_scalar(out=h_, in0=h_, scalar1=st[:, 12:13], scalar2=st[:, 14:15],
                                                              op0=ALU.subtract, op1=ALU.mult), [B_h, B_st], [B_h])
            OP("pool", lambda e, h_=h_: e.tensor_tensor(out=h_, in0=h_, in1=lng, op=ALU.mult), [B_h, B_lng], [B_h])
            OP("pool", lambda e, h_=h_: e.tensor_tensor(out=h_, in0=h_, in1=lnb, op=ALU.add), [B_h, B_lnb], [B_h])
            if last:
                P.dma("pool", y_out[r0 - 256:r0 - 128, :], h_, reads=[B_h], writes=[Buf("yo")], key=f"p4_h{sl}")
            else:
                P.dma("pool", XC[r0:r0 + 128, :], h_, reads=[B_h], writes=[Buf("xc")], key=f"p4_h{sl}")

        plists = []
        for s_ in range(2):
            tl_ = tiles[s_::2]
            P.record()
            p4_load(tl_[0], 2 * s_)
            for it, tile in enumerate(tl_):
                if it + 1 < len(tl_):
                    p4_load(tl_[it + 1], 2 * s_ + (it + 1) % 2)
                p4_body(tile, 2 * s_ + it % 2, 4 * s_)
            plists.append(P.stop())
        P.play(plists)

    def phase_P1(layer):
            P.barrier()
            A.release(persist_mark)
            cT = A.alloc(16, "cT")
            sc = A.alloc(16, "sc")
            bm = A.alloc(3 * D, "bm")
            modrow = A.alloc(3 * D, "modrow")
            wm = [A.alloc(8 * 512, f"wm{i}") for i in range(2)]
            B_c, B_sc, B_bm, B_modrow = Buf("cT"), Buf("sc"), Buf("bm"), Buf("modrow")
            B_wm = [Buf("wm0"), Buf("wm1")]
            P.dma("sp", cT, cvT, writes=[B_c], key="cT")
            P.dma("sp", bm[0:2, :], b_mod2[layer], writes=[B_bm], key="bm")
            P.op("act", lambda e: e.activation(out=sc, in_=cT, func=AF.Silu), reads=[B_c], writes=[B_sc])
            wmv = w_mod[layer].rearrange("(k p) c -> p k c", p=128)
            for ch in range(6):
                slot = ch % 2
                P.dma("sp", wm[slot].rearrange("p (k c) -> p k c", k=8), wmv[:, :, ch * 512:(ch + 1) * 512],
                      writes=[B_wm[slot]], key=f"wm{slot}")
                bank = ch % 2

                def mm(e, slot=slot, bank=bank):
                    ins = None
                    for k in range(8):
                        ins = e.matmul(banks[bank][0:2, :], lhsT=sc[:, 2 * k:2 * k + 2],
                                       rhs=wm[slot][:, k * 512:(k + 1) * 512], start=(k == 0), stop=(k == 7))
                    return ins
                P.op("pe", mm, reads=[B_sc, B_wm[slot]], writes=[PSB[bank]])
                P.op("dve", lambda e, ch=ch, bank=bank: e.tensor_tensor(
                    out=modrow[0:2, ch * 512:(ch + 1) * 512], in0=banks[bank][0:2, :],
                    in1=bm[0:2, ch * 512:(ch + 1) * 512], op=ALU.add),
                    reads=[PSB[bank], B_bm], writes=[B_modrow])
            P.op("dve", lambda e: e.tensor_scalar_add(out=modrow[0:2, D:2 * D], in0=modrow[0:2, D:2 * D], scalar1=1.0),
                 reads=[B_modrow], writes=[B_modrow])
            for r in range(2):
                for ch in range(6):
                    bank = 2 + (ch % 2)
                    P.op("pe", lambda e, r=r, ch=ch, bank=bank: e.matmul(
                        banks[bank][:, :], lhsT=sel[0:2, r * 128:(r + 1) * 128],
                        rhs=modrow[0:2, ch * 512:(ch + 1) * 512], start=True, stop=True),
                        reads=[B_modrow, B_const], writes=[PSB[bank]])
                    P.op("act", lambda e, r=r, ch=ch, bank=bank: e.copy(
                        out=modb[r][:, ch * 512:(ch + 1) * 512], in_=banks[bank][:, :]),
                        reads=[PSB[bank]], writes=[B_modb])
    def phase_P2(layer):
            P.barrier()
            A.release(persist_mark)
            wtm = A.alloc(8 * NTM, "wtm")
            wfm = A.alloc(8 * NFM, "wfm")
            B_wtm, B_wfm = Buf("wtm"), Buf("wfm")
            wtm_v = w_tm[layer].rearrange("(k p) c -> p k c", p=128)
            wfm_v = w_fm[layer].rearrange("(k p) c -> p k c", p=128)
            for k in range(8):
                P.dma("sp", wtm[:, k * NTM:(k + 1) * NTM], wtm_v[:, k, :], writes=[B_wtm], key="wtm")
                P.dma("pool", wfm[:, k * NFM:(k + 1) * NFM], wfm_v[:, k, :], writes=[B_wfm], key="wfm")
                if FAST["p2"]:
                    P.op("dve", lambda e, k=k: e.tensor_copy(out=fr(wtm[:, k * NTM:(k + 1) * NTM]), in_=wtm[:, k * NTM:(k + 1) * NTM]),
                         reads=[B_wtm], writes=[B_wtm])
                    P.op("act", lambda e, k=k: e.copy(out=fr(wfm[:, k * NFM:(k + 1) * NFM]), in_=wfm[:, k * NFM:(k + 1) * NFM]),
                         reads=[B_wfm], writes=[B_wfm])
            NXS = 2
            xt = [A.alloc(D, f"xt{i}") for i in range(NXS)]
            B_xt = [Buf(f"xt{i}") for i in range(NXS)]
            stats = A.alloc(16, "stats")
            B_stats = Buf("stats")
            mlT = [A.alloc(8 * 512, f"mlT{i}") for i in range(2)]
            B_mlT = [Buf("mlT0"), Buf("mlT1")]
            tmst = [A.alloc(NTM, f"tmst{i}") for i in range(2)]
            B_tmst = [Buf("tmst0"), Buf("tmst1")]
            fmst = [A.alloc(512, f"fmst{i}") for i in range(2)]
            B_fmst = [Buf(f"fmst{i}") for i in range(2)]
            xsrc = x_all if layer == 0 else XC
            groups = [list(range(g * 4, min(g * 4 + 4, NT))) for g in range((NT + 3) // 4)]
            fm_i = 0
            for gi, tiles in enumerate(groups):
                ms = gi % 2
                ntok = len(tiles) * 128
                for ti, tile in enumerate(tiles):
                    xs = tile % NXS
                    r = 1 if tile < 2 else 0
                    xa = xt[xs]
                    P.dma("sp", xa, xsrc[tile * 128:(tile + 1) * 128, :], writes=[B_xt[xs]], key=f"xt{xs}")
                    P.op("dve", lambda e, xa=xa: e.bn_stats(out=stats[:, 0:6], in_=xa[:, 0:512]), reads=[B_xt[xs]], writes=[B_stats])
                    P.op("dve", lambda e, xa=xa: e.bn_stats(out=stats[:, 6:12], in_=xa[:, 512:1024]), reads=[B_xt[xs]], writes=[B_stats])
                    P.op("dve", lambda e: e.bn_aggr(out=stats[:, 12:14], in_=stats[:, 0:12]), reads=[B_stats], writes=[B_stats])
                    P.op("act", lambda e: e.activation(out=stats[:, 14:15], in_=stats[:, 13:14], func=AF.Sqrt, bias=LN_EPS, scale=1.0),
                         reads=[B_stats], writes=[B_stats])
                    P.op("dve", lambda e: e.reciprocal(out=stats[:, 14:15], in_=stats[:, 14:15]), reads=[B_stats], writes=[B_stats])
                    P.op("dve", lambda e, xa=xa: e.tensor_scalar(out=xa, in0=xa, scalar1=stats[:, 12:13], scalar2=stats[:, 14:15],
                                                                 op0=ALU.subtract, op1=ALU.mult), reads=[B_xt[xs], B_stats], writes=[B_xt[xs]])
                    P.op("pool", lambda e, xa=xa, r=r: e.tensor_tensor(out=xa, in0=xa, in1=modb[r][:, D:2 * D], op=ALU.mult),
                         reads=[B_xt[xs], B_modb], writes=[B_xt[xs]])
                    P.op("pool", lambda e, xa=xa, r=r: e.tensor_tensor(out=xa, in0=xa, in1=modb[r][:, 0:D], op=ALU.add),
                         reads=[B_xt[xs], B_modb], writes=[B_xt[xs]])
                    for half in range(2):
                        bank = half

                        def tr(e, xa=xa, half=half, bank=bank):
                            ins = None
                            for kk in range(4):
                                k = half * 4 + kk
                                ins = e.transpose(banks[bank][:, kk * 128:(kk + 1) * 128], xa[:, k * 128:(k + 1) * 128], ident)
                            return ins
                        P.op("pe", tr, reads=[B_xt[xs], B_const], writes=[PSB[bank]])
                        dst = mlT[ms].rearrange("p (k t) -> p k t", k=8)[:, half * 4:half * 4 + 4, ti * 128:(ti + 1) * 128]
                        P.op("act", lambda e, dst=dst, bank=bank: e.copy(
                            out=fr(dst, FAST["p2"]), in_=banks[bank][:, :].rearrange("p (k t) -> p k t", k=4)),
                            reads=[PSB[bank]], writes=[B_mlT[ms]])
                for ti, tile in enumerate(tiles):
                    ts_ = tile % 2
                    for ci, (c0, cw) in enumerate(((0, 512), (512, 512), (1024, 512), (1536, 16))):
                        bank = 2 + ci % 2

                        def mm(e, ti=ti, c0=c0, cw=cw, bank=bank, ms=ms):
                            ins = None
                            for k in range(8):
                                ins = e.matmul(banks[bank][:, 0:cw], lhsT=fr(mlT[ms][:, k * 512 + ti * 128:k * 512 + (ti + 1) * 128], FAST["p2"] and cw >= 256),
                                               rhs=fr(wtm[:, k * NTM + c0:k * NTM + c0 + cw], FAST["p2"] and cw >= 256), start=(k == 0), stop=(k == 7))
                            return ins
                        P.op("pe", mm, reads=[B_mlT[ms], B_wtm], writes=[PSB[bank]])
                        P.op("dve", lambda e, c0=c0, cw=cw, bank=bank, ts_=ts_: e.tensor_copy(
                            out=tmst[ts_][:, c0:c0 + cw], in_=banks[bank][:, 0:cw]), reads=[PSB[bank]], writes=[B_tmst[ts_]])
                    P.dma("act", TM[tile * 128:(tile + 1) * 128, :], tmst[ts_], reads=[B_tmst[ts_]], writes=[Buf("tmd")], key=f"tmst{ts_}")
                t0 = tiles[0] * 128
                for ft in range(19):
                    f0 = ft * 128
                    fw = 128 if ft < 18 else 32
                    bank = 4 + ft % 4
                    fs = fm_i % 2
                    fm_i += 1

                    def mm(e, f0=f0, fw=fw, bank=bank, ms=ms, ntok=ntok):
                        ins = None
                        for k in range(8):
                            ins = e.matmul(banks[bank][0:fw, 0:ntok], lhsT=fr(wfm[:, k * NFM + f0:k * NFM + f0 + fw], FAST["p2"]),
                                           rhs=fr(mlT[ms][:, k * 512:k * 512 + ntok], FAST["p2"]), start=(k == 0), stop=(k == 7))
                        return ins
                    P.op("pe", mm, reads=[B_mlT[ms], B_wfm], writes=[PSB[bank]])
                    eng = "act" if ft % 2 == 0 else "dve"
                    if eng == "act":
                        P.op("act", lambda e, fw=fw, bank=bank, fs=fs, ntok=ntok: e.copy(
                            out=fmst[fs][0:fw, 0:ntok], in_=banks[bank][0:fw, 0:ntok]), reads=[PSB[bank]], writes=[B_fmst[fs]])
                    else:
                        P.op("dve", lambda e, fw=fw, bank=bank, fs=fs, ntok=ntok: e.tensor_copy(
                            out=fmst[fs][0:fw, 0:ntok], in_=banks[bank][0:fw, 0:ntok]), reads=[PSB[bank]], writes=[B_fmst[fs]])
                    P.dma("pool", FM[f0:f0 + fw, t0:t0 + ntok], fmst[fs][0:fw, 0:ntok], reads=[B_fmst[fs]],
                          writes=[Buf("fmd")], key=f"fmst{fs}")

    PH = dict(P1=phase_P1, P2=phase_P2, A=phase_A, B=phase_B, C=phase_C, P4=phase_P4)
    for layer in layers:
        for ph in phases:
            PH[ph](layer)


    P.barrier()
    P.emit()
    return nc


_CONST = {}


def _consts():
    if not _CONST:
        _CONST["ident"] = np.eye(128, dtype=np.float32)
        sel = np.zeros((2, 256), np.float32)
        sel[0, 0:128] = 1.0
        sel[1, 128:256] = 1.0
        _CONST["sel"] = sel
        _CONST["ones"] = np.ones((128, 128), np.float32)
        jj, ii = np.meshgrid(np.arange(128), np.arange(128), indexing="ij")
        _CONST["tril"] = (jj <= ii).astype(np.float32)
        _CONST["triu"] = (jj >= ii).astype(np.float32)
        hm = np.zeros((128, 8), np.float32)
        hm[np.arange(128), np.arange(128) // 32] = 1.0
        _CONST["hm"] = hm
        _CONST["bd"] = (np.arange(128)[:, None] // 32 == np.arange(256)[None, :] // 64).astype(np.float32)
        t = np.arange(SEQ)
        inv = (10000.0 ** (-np.arange(8, dtype=np.float32) / 8)).astype(np.float32)
        cosf = np.zeros((SEQ, 4, 2, 2, 8), np.float32)
        sinf = np.zeros((SEQ, 4, 2, 2, 8), np.float32)
        for half, pos in enumerate((t // 64, t % 64)):
            ang = pos.astype(np.float32)[:, None] * inv[None, :]
            c_, s_ = np.cos(ang).astype(np.float32), np.sin(ang).astype(np.float32)
            cosf[:, :, half, 0, :] = c_[:, None, :]
            cosf[:, :, half, 1, :] = c_[:, None, :]
            sinf[:, :, half, 0, :] = -s_[:, None, :]
            sinf[:, :, half, 1, :] = s_[:, None, :]
        _CONST["ropec"] = cosf.reshape(SEQ, 128)
        pj, fi = jj, ii
        offu = []
        for sz_ in (1, 2, 4, 8, 16, 32, 64):
            offu.append(((pj // (2 * sz_) == fi // (2 * sz_)) & (pj % (2 * sz_) < sz_) & (fi % (2 * sz_) >= sz_)).astype(np.float32))
        offl = [m_.T for m_ in offu]
        _CONST["offs"] = np.ascontiguousarray(np.concatenate(offu + offl, 1))
        negf = np.where(pj <= fi, 0.0, -30000.0).astype(np.float32)
        negb = np.where(pj >= fi, 0.0, -30000.0).astype(np.float32)
        _CONST["negm"] = np.ascontiguousarray(np.concatenate([np.tile(negf, (1, 4)), np.tile(negb, (1, 4))], 1))
        _CONST["offd"] = (pj != fi).astype(np.float32)
        _CONST["ropes"] = sinf.reshape(SEQ, 128)
    return _CONST


def make_in_maps(inputs):
    x, c, ctx, c_ctx = (np.asarray(inputs[k], np.float32) for k in ("x", "c", "ctx", "c_ctx"))
    w_in = np.asarray(inputs["w_in"], np.float32)
    cs = _consts()
    sl = lambda a, b: list(range(a, b))
    tm_cols = sl(1552, 2064) + sl(2576, 2832) + sl(3088, 3216) + sl(3216, 3344) + sl(3344, 3600) + sl(3632, 3888) + sl(1536, 1552)
    fm_cols = sl(0, 1536) + sl(2064, 2320) + sl(2320, 2576) + sl(2832, 3088) + sl(3600, 3632)
    assert len(tm_cols) == NTM and len(fm_cols) == NFM
    w_tm = np.ascontiguousarray(w_in[:, :, tm_cols])
    w_fm = np.ascontiguousarray(w_in[:, :, fm_cols])
    b_mod = np.asarray(inputs["b_mod"], np.float32)
    b_mod2 = np.ascontiguousarray(np.broadcast_to(b_mod[:, None, :], (DEPTH, 2, 3 * D)))
    f = lambda k: np.asarray(inputs[k], np.float32)
    shared = dict(w_mod=f("w_mod"), b_mod2=b_mod2, w_tm=w_tm, w_fm=w_fm, w_out=f("w_out"))
    for k in ("ident", "sel", "ones", "tril", "triu", "hm", "bd", "ropec", "ropes"):
        shared[k] = cs[k]
    w2 = f("gla_w2")
    w2p = np.zeros((DEPTH, 32, 256), np.float32)
    w2p[:, 0:16, 0:128] = w2[:, 0]
    w2p[:, 16:32, 128:256] = w2[:, 1]
    shared["w2p"] = w2p
    shared["b2r"] = np.ascontiguousarray(f("gla_b2").reshape(DEPTH, 1, 256))
    shared["glan"] = np.ascontiguousarray(np.broadcast_to(np.tile(f("gla_norm"), (1, 4))[:, None, :], (DEPTH, 128, 256)))
    shared["gdnn"] = np.ascontiguousarray(np.broadcast_to(np.tile(f("gdn_norm"), (1, 4))[:, None, :], (DEPTH, 128, 512)))
    shared["lng"] = np.ascontiguousarray(np.broadcast_to(f("ln_g")[:, None, :], (DEPTH, 128, D)))
    shared["lnb"] = np.ascontiguousarray(np.broadcast_to(f("ln_b")[:, None, :], (DEPTH, 128, D)))
    rpb = f("rpb")
    kc = np.arange(64)[:, None]
    qc = np.arange(64)[None, :]
    c0_ = np.clip(qc - 8, 0, 48)
    ok = (kc >= c0_) & (kc < c0_ + 16)
    dj = np.clip(kc - qc + 15, 0, 30)
    nab = np.full((DEPTH, 64, 4, 15, 64), -30000.0, np.float32)
    for e_ in range(15):
        blk = rpb[:, :, 14 - e_, :][:, :, dj]
        blk = np.where(ok[None, None], blk, np.float32(-30000.0))
        nab[:, :, :, e_, :] = blk.transpose(0, 2, 1, 3)
    nab = nab.reshape(DEPTH, 64, 3840)
    shared["nab"] = np.ascontiguousarray(np.concatenate([nab, nab], 1))
    cwv = f("conv_w")
    shared["convw"] = np.ascontiguousarray(cwv.reshape(DEPTH, 5, 12, 128).transpose(0, 3, 2, 1).reshape(DEPTH, 128, 60))
    shared["alog8"] = np.ascontiguousarray(np.broadcast_to(f("a_log").reshape(DEPTH, 1, 8), (DEPTH, 128, 8)))
    shared["dtb8"] = np.ascontiguousarray(np.broadcast_to(f("dt_bias").reshape(DEPTH, 1, 8), (DEPTH, 128, 8)))
    for k in ("offs", "negm", "offd"):
        shared[k] = cs[k]
    maps = []
    for b in range(8):
        cv = np.stack([c[b], c_ctx], 0)
        cvT = np.ascontiguousarray(cv.reshape(2, 8, 128).transpose(2, 1, 0).reshape(128, 16))
        m = dict(shared)
        m["x_all"] = np.ascontiguousarray(np.concatenate([ctx[b], x[b]], 0))
        m["cvT"] = cvT
        maps.append(m)
    return maps


_NC = {}


def kernel(**inputs):
    if "nc" not in _NC:
        _NC["nc"] = build_program()
    maps = make_in_maps(inputs)
    res = run_bass_kernel_spmd(_NC["nc"], maps, core_ids=list(range(8)))
    return np.stack([np.asarray(r["y"], np.float32) for r in res.results], 0)
```
